# Optimizing a Trainium2 kernel written in Bass

```python
import jax
import jax.numpy as jnp
from jax import lax
import numpy as np

D_MODEL = 2048
BATCH = 4
SEQ = 4096
DEPTH = 2

PLE_DIM = 256
NORM_EPS = 1e-6
ROPE_THETA = 10000.0

GDN_HEADS = 8
GDN_DK = 128
GDN_DV = 128
GDN_CONV = 4
GDN_CHUNK = 64
MLA_HEADS = 8
MLA_Q_RANK = 512
MLA_KV_RANK = 512
MLA_NOPE = 128
MLA_ROPE = 64
MLA_V = 128
MLA_BLOCK = 128
RET_HEADS = 8
RET_DK = D_MODEL // RET_HEADS
RET_DV = 2 * RET_DK
RET_CHUNK = 64
D_FF = 5632
FFN_CONV = 3

L0_SPLITS = (GDN_HEADS * GDN_DK, GDN_HEADS * GDN_DK, GDN_HEADS * GDN_DV, GDN_HEADS * GDN_DV,
             GDN_HEADS, GDN_HEADS, MLA_Q_RANK, MLA_KV_RANK, MLA_ROPE)
L0_IN = sum(L0_SPLITS)
L0_MIX = GDN_HEADS * GDN_DV + MLA_HEADS * MLA_V
L1_SPLITS = (RET_HEADS * RET_DK, RET_HEADS * RET_DK, RET_HEADS * RET_DV, RET_HEADS * RET_DV)
L1_IN = sum(L1_SPLITS)
L1_MIX = RET_HEADS * RET_DV

kernel_name = 'hybrid_gdn_mla_retnet_block'


def rmsnorm(x, g):
    xf = x.astype(jnp.float32)
    y = xf * lax.rsqrt(jnp.mean(xf * xf, axis=-1, keepdims=True) + NORM_EPS)
    return (y * g.astype(jnp.float32)).astype(x.dtype)


def split_cols(x, sizes):
    idx = [int(i) for i in np.cumsum(sizes)[:-1]]
    return jnp.split(x, idx, axis=-1)


def to_heads(x, n_heads):
    b, s, _ = x.shape
    return x.reshape(b, s, n_heads, -1).transpose(0, 2, 1, 3)


def from_heads(x):
    b, h, s, d = x.shape
    return x.transpose(0, 2, 1, 3).reshape(b, s, h * d)


def causal_dwconv(x, w):
    width, s = w.shape[0], x.shape[1]
    xp = jnp.pad(x, ((0, 0), (width - 1, 0), (0, 0)))
    y = xp[:, 0:s] * w[0]
    for j in range(1, width):
        y = y + xp[:, j:j + s] * w[j]
    return y


def rope_tables(positions, dim):
    inv_freq = ROPE_THETA ** (-jnp.arange(0, dim, 2, dtype=jnp.float32) / dim)
    ang = positions.astype(jnp.float32)[..., None] * inv_freq
    return jnp.cos(ang), jnp.sin(ang)


def apply_rope(x, cos, sin):
    x1, x2 = jnp.split(x, 2, axis=-1)
    return jnp.concatenate([x1 * cos - x2 * sin, x2 * cos + x1 * sin], axis=-1).astype(x.dtype)


def l2norm(x):
    return x * lax.rsqrt(jnp.sum(x * x, axis=-1, keepdims=True) + NORM_EPS)


def chunk(t, c):
    b, h, s = t.shape[:3]
    return t.reshape(b, h, s // c, c, *t.shape[3:])


def gated_deltanet(q, k, v, z, a, b, conv_w, A_log, dt_bias, norm_w):
    f32 = jnp.float32
    dtype = q.dtype
    bsz, s, _ = q.shape
    c = GDN_CHUNK
    qkv = jax.nn.silu(causal_dwconv(jnp.concatenate([q, k, v], axis=-1), conv_w)).astype(f32)
    q, k, v = split_cols(qkv, L0_SPLITS[:3])
    q = l2norm(to_heads(q, GDN_HEADS)) * GDN_DK ** -0.5
    k = l2norm(to_heads(k, GDN_HEADS))
    v = to_heads(v, GDN_HEADS)
    beta = jax.nn.sigmoid(b.astype(f32)).transpose(0, 2, 1)
    g = (-jnp.exp(A_log.astype(f32)) * jax.nn.softplus(a.astype(f32) + dt_bias.astype(f32))
         ).transpose(0, 2, 1)
    qc, kc, vc = chunk(q, c), chunk(k, c), chunk(v, c)
    bc = chunk(beta, c)[..., None]
    G = jnp.cumsum(chunk(g, c), axis=-1)
    incl = jnp.tril(jnp.ones((c, c), dtype=bool))
    strict = jnp.tril(jnp.ones((c, c), dtype=bool), -1)
    gamma = jnp.exp(jnp.where(incl, G[..., :, None] - G[..., None, :], -jnp.inf))
    kb = kc * bc
    A = jnp.where(strict, jnp.einsum('bhncd,bhnsd->bhncs', kb, kc) * gamma, 0.0)
    M = A + jnp.eye(c, dtype=f32)
    rhs = jnp.concatenate([vc * bc, kb * jnp.exp(G)[..., None]], axis=-1)
    sol = lax.linalg.triangular_solve(M, rhs, left_side=True, lower=True, unit_diagonal=True)
    u, w = sol[..., :GDN_DV], sol[..., GDN_DV:]
    attn = jnp.einsum('bhncd,bhnsd->bhncs', qc, kc) * gamma
    q_dec = qc * jnp.exp(G)[..., None]
    g_last = G[..., -1]
    k_dec = kc * jnp.exp(g_last[..., None] - G)[..., None]

    def step(state, xs):
        u_n, w_n, qd_n, kd_n, attn_n, gl_n = xs
        v_new = u_n - jnp.einsum('bhcd,bhde->bhce', w_n, state)
        o_n = jnp.einsum('bhcd,bhde->bhce', qd_n, state) + jnp.einsum('bhcs,bhse->bhce', attn_n, v_new)
        state = state * jnp.exp(gl_n)[..., None, None] + jnp.einsum('bhcd,bhce->bhde', kd_n, v_new)
        return state, o_n

    xs = tuple(jnp.moveaxis(t, 2, 0) for t in (u, w, q_dec, k_dec, attn, g_last))
    state0 = jnp.zeros((bsz, GDN_HEADS, GDN_DK, GDN_DV), f32)
    _, o = lax.scan(step, state0, xs)
    o = jnp.moveaxis(o, 0, 2).reshape(bsz, GDN_HEADS, s, GDN_DV)
    o = rmsnorm(o, norm_w) * jax.nn.silu(to_heads(z.astype(f32), GDN_HEADS))
    return from_heads(o).astype(dtype)


def mla(c_q, c_kv, k_rope, cos, sin, q_norm, w_uq, kv_norm, w_ukv):
    bsz, s, _ = c_q.shape
    q = to_heads(rmsnorm(c_q, q_norm) @ w_uq, MLA_HEADS)
    q_nope = q[..., :MLA_NOPE]
    q_pe = apply_rope(q[..., MLA_NOPE:], cos[:, None], sin[:, None])
    kv = to_heads(rmsnorm(c_kv, kv_norm) @ w_ukv, MLA_HEADS)
    k_nope, v = kv[..., :MLA_NOPE], kv[..., MLA_NOPE:]
    k_pe = apply_rope(k_rope, cos, sin)
    scale = (MLA_NOPE + MLA_ROPE) ** -0.5
    nb = s // MLA_BLOCK
    qn_b = q_nope.reshape(bsz, MLA_HEADS, nb, MLA_BLOCK, MLA_NOPE).transpose(2, 0, 1, 3, 4)
    qp_b = q_pe.reshape(bsz, MLA_HEADS, nb, MLA_BLOCK, MLA_ROPE).transpose(2, 0, 1, 3, 4)
    key_idx = jnp.arange(s)

    def block(args):
        i, qn, qp = args
        sc = (jnp.einsum('bhqd,bhkd->bhqk', qn, k_nope)
              + jnp.einsum('bhqr,bkr->bhqk', qp, k_pe)).astype(jnp.float32) * scale
        q_idx = i * MLA_BLOCK + jnp.arange(MLA_BLOCK)
        sc = jnp.where(key_idx[None, :] <= q_idx[:, None], sc, -jnp.inf)
        pr = jax.nn.softmax(sc, axis=-1).astype(v.dtype)
        return jnp.einsum('bhqk,bhkd->bhqd', pr, v)

    o = lax.map(block, (jnp.arange(nb), qn_b, qp_b))
    o = o.transpose(1, 2, 0, 3, 4).reshape(bsz, MLA_HEADS, s, MLA_V)
    return from_heads(o)


def retention(q, k, v, cos, sin, norm_w):
    f32 = jnp.float32
    dtype = q.dtype
    bsz, s, _ = q.shape
    c = RET_CHUNK
    q = apply_rope(to_heads(q.astype(f32), RET_HEADS), cos[:, None], sin[:, None])
    k = apply_rope(to_heads(k.astype(f32), RET_HEADS), cos[:, None], sin[:, None]) * RET_DK ** -0.5
    v = to_heads(v.astype(f32), RET_HEADS)
    log_gamma = jnp.log1p(-jnp.power(2.0, -5.0 - jnp.arange(RET_HEADS, dtype=f32)))
    pos = jnp.arange(c, dtype=f32)
    incl = jnp.tril(jnp.ones((c, c), dtype=bool))
    decay = jnp.exp(jnp.where(incl, (pos[:, None] - pos[None, :]) * log_gamma[:, None, None], -jnp.inf))
    qc, kc, vc = chunk(q, c), chunk(k, c), chunk(v, c)
    inner = jnp.einsum('bhncs,bhnse->bhnce', jnp.einsum('bhncd,bhnsd->bhncs', qc, kc) * decay[:, None], vc)
    xi = jnp.exp((pos + 1.0) * log_gamma[:, None])
    zeta = jnp.exp((c - 1.0 - pos) * log_gamma[:, None])
    gamma_c = jnp.exp(c * log_gamma)
    q_dec = qc * xi[:, None, :, None]
    k_dec = kc * zeta[:, None, :, None]

    def step(state, xs):
        qd_n, kd_n, v_n = xs
        cross = jnp.einsum('bhcd,bhde->bhce', qd_n, state)
        state = state * gamma_c[:, None, None] + jnp.einsum('bhcd,bhce->bhde', kd_n, v_n)
        return state, cross

    xs = tuple(jnp.moveaxis(t, 2, 0) for t in (q_dec, k_dec, vc))
    state0 = jnp.zeros((bsz, RET_HEADS, RET_DK, RET_DV), f32)
    _, cross = lax.scan(step, state0, xs)
    o = (inner + jnp.moveaxis(cross, 0, 2)).reshape(bsz, RET_HEADS, s, RET_DV)
    mu = jnp.mean(o, axis=-1, keepdims=True)
    var = jnp.mean(jnp.square(o - mu), axis=-1, keepdims=True)
    o = (o - mu) * lax.rsqrt(var + NORM_EPS)
    return (from_heads(o) * norm_w.astype(f32)).astype(dtype)


def even_mixer(hn, cos, sin, w_in, gdn_conv, gdn_A_log, gdn_dt_bias, gdn_norm,
               q_norm, w_uq, kv_norm, w_ukv, w_out):
    q, k, v, z, a, b, c_q, c_kv, k_rope = split_cols(hn @ w_in, L0_SPLITS)
    y_a = gated_deltanet(q, k, v, z, a, b, gdn_conv, gdn_A_log, gdn_dt_bias, gdn_norm)
    y_b = mla(c_q, c_kv, k_rope, cos, sin, q_norm, w_uq, kv_norm, w_ukv)
    return jnp.concatenate([y_a, y_b], axis=-1) @ w_out


def odd_mixer(hn, cos, sin, w_in, ret_norm, w_out):
    q, k, v, g = split_cols(hn @ w_in, L1_SPLITS)
    y = retention(q, k, v, cos, sin, ret_norm)
    return (jax.nn.silu(g) * y) @ w_out


def conv_ffn(hn, w_up, conv_w, conv_b, w_down):
    u = causal_dwconv(hn @ w_up, conv_w) + conv_b
    gate, up = jnp.split(u, 2, axis=-1)
    return (jax.nn.silu(gate) * up) @ w_down


def ple_term(h, p_i, w_proj, gate_norm, w_gate):
    return (p_i @ w_proj) * jax.nn.sigmoid(rmsnorm(h, gate_norm) @ w_gate)


def setup_inputs(seed: int = 0) -> dict:
    key = jax.random.key(seed)
    keys = jax.random.split(key, 64)
    counter = [0]

    def nk():
        kk = keys[counter[0]]
        counter[0] += 1
        return kk

    def dense(fan_in, fan_out):
        return jax.random.normal(nk(), (fan_in, fan_out), jnp.float32) * fan_in ** -0.5

    def gain(n):
        return 1.0 + 0.02 * jax.random.normal(nk(), (n,), jnp.float32)

    x = jax.random.normal(nk(), (BATCH, SEQ, D_MODEL), jnp.float32)
    p = jax.random.normal(nk(), (DEPTH, BATCH, SEQ, PLE_DIM), jnp.float32)
    offset = jax.random.randint(nk(), (BATCH, 1), 0, 1024, jnp.int32)
    positions = offset + jnp.arange(SEQ, dtype=jnp.int32)[None, :]

    gdn_ch = 2 * GDN_HEADS * GDN_DK + GDN_HEADS * GDN_DV
    dt = jnp.exp(jax.random.uniform(nk(), (GDN_HEADS,), jnp.float32, np.log(1e-3), np.log(1e-1)))

    def ffn_params():
        return (gain(D_MODEL), dense(D_MODEL, 2 * D_FF),
                jax.random.normal(nk(), (FFN_CONV, 2 * D_FF), jnp.float32) * FFN_CONV ** -0.5,
                0.01 * jax.random.normal(nk(), (2 * D_FF,), jnp.float32),
                dense(D_FF, D_MODEL))

    def ple_params():
        return (dense(PLE_DIM, D_MODEL), gain(D_MODEL), dense(D_MODEL, D_MODEL))

    out = {'x': x, 'p': p, 'positions': positions}
    out['l0_attn_norm'] = gain(D_MODEL)
    out['l0_w_in'] = dense(D_MODEL, L0_IN)
    out['l0_gdn_conv'] = jax.random.normal(nk(), (GDN_CONV, gdn_ch), jnp.float32) * GDN_CONV ** -0.5
    out['l0_gdn_A_log'] = jnp.log(jax.random.uniform(nk(), (GDN_HEADS,), jnp.float32, 1.0, 16.0))
    out['l0_gdn_dt_bias'] = dt + jnp.log(-jnp.expm1(-dt))
    out['l0_gdn_norm'] = gain(GDN_DV)
    out['l0_mla_q_norm'] = gain(MLA_Q_RANK)
    out['l0_mla_w_uq'] = dense(MLA_Q_RANK, MLA_HEADS * (MLA_NOPE + MLA_ROPE))
    out['l0_mla_kv_norm'] = gain(MLA_KV_RANK)
    out['l0_mla_w_ukv'] = dense(MLA_KV_RANK, MLA_HEADS * (MLA_NOPE + MLA_V))
    out['l0_w_out'] = dense(L0_MIX, D_MODEL)
    (out['l0_ffn_norm'], out['l0_ffn_w_up'], out['l0_ffn_conv_w'], out['l0_ffn_conv_b'],
     out['l0_ffn_w_down']) = ffn_params()
    out['l0_ple_proj'], out['l0_ple_gate_norm'], out['l0_ple_gate'] = ple_params()
    out['l1_attn_norm'] = gain(D_MODEL)
    out['l1_w_in'] = dense(D_MODEL, L1_IN)
    out['l1_ret_norm'] = gain(L1_MIX)
    out['l1_w_out'] = dense(L1_MIX, D_MODEL)
    (out['l1_ffn_norm'], out['l1_ffn_w_up'], out['l1_ffn_conv_w'], out['l1_ffn_conv_b'],
     out['l1_ffn_w_down']) = ffn_params()
    out['l1_ple_proj'], out['l1_ple_gate_norm'], out['l1_ple_gate'] = ple_params()
    out['final_norm'] = gain(D_MODEL)
    return out


def reference(x, p, positions,
              l0_attn_norm, l0_w_in, l0_gdn_conv, l0_gdn_A_log, l0_gdn_dt_bias, l0_gdn_norm,
              l0_mla_q_norm, l0_mla_w_uq, l0_mla_kv_norm, l0_mla_w_ukv, l0_w_out,
              l0_ffn_norm, l0_ffn_w_up, l0_ffn_conv_w, l0_ffn_conv_b, l0_ffn_w_down,
              l0_ple_proj, l0_ple_gate_norm, l0_ple_gate,
              l1_attn_norm, l1_w_in, l1_ret_norm, l1_w_out,
              l1_ffn_norm, l1_ffn_w_up, l1_ffn_conv_w, l1_ffn_conv_b, l1_ffn_w_down,
              l1_ple_proj, l1_ple_gate_norm, l1_ple_gate,
              final_norm):
    cos_mla, sin_mla = rope_tables(positions, MLA_ROPE)
    cos_ret, sin_ret = rope_tables(positions, RET_DK)
    mixers = (
        lambda hn: even_mixer(hn, cos_mla, sin_mla, l0_w_in, l0_gdn_conv, l0_gdn_A_log, l0_gdn_dt_bias,
                              l0_gdn_norm, l0_mla_q_norm, l0_mla_w_uq, l0_mla_kv_norm, l0_mla_w_ukv, l0_w_out),
        lambda hn: odd_mixer(hn, cos_ret, sin_ret, l1_w_in, l1_ret_norm, l1_w_out),
    )
    attn_norms = (l0_attn_norm, l1_attn_norm)
    ffn_params = ((l0_ffn_norm, l0_ffn_w_up, l0_ffn_conv_w, l0_ffn_conv_b, l0_ffn_w_down),
                  (l1_ffn_norm, l1_ffn_w_up, l1_ffn_conv_w, l1_ffn_conv_b, l1_ffn_w_down))
    ple_params = ((l0_ple_proj, l0_ple_gate_norm, l0_ple_gate),
                  (l1_ple_proj, l1_ple_gate_norm, l1_ple_gate))
    h = x
    for i in range(DEPTH):
        h = h + mixers[i](rmsnorm(h, attn_norms[i]))
        ffn_norm, w_up, conv_w, conv_b, w_down = ffn_params[i]
        h = h + conv_ffn(rmsnorm(h, ffn_norm), w_up, conv_w, conv_b, w_down)
        w_proj, gate_norm, w_gate = ple_params[i]
        h = h + ple_term(h, p[i], w_proj, gate_norm, w_gate)
    return rmsnorm(h, final_norm)
```

```python
import contextlib
import numpy as np
import concourse.bass as bass
import concourse.mybir as mybir
from concourse.bass_utils import run_bass_kernel_spmd

F32 = mybir.dt.float32
BF16 = mybir.dt.bfloat16
I32 = mybir.dt.int32
ALU = mybir.AluOpType
AF = mybir.ActivationFunctionType
AX = mybir.AxisListType

ENGS = ('pe', 'act', 'dve', 'pool', 'sp')
GEN = 8192
DGEN = 512


class Prog:
    def __init__(self, nc):
        self.nc = nc
        self.ops = []
        self.last_w = {}
        self.readers = {}
        self.stack = contextlib.ExitStack()
        self.dma_groups = {}
        self.slots = {}
        self.last_dma = {}
        self.lane_of = {}
        self.lane_cnt = {}

    def sb(self, name, shape, dt=F32):
        return self.stack.enter_context(self.nc.sbuf_tensor(name, list(shape), dt))

    def ps(self, name, shape, dt=F32):
        return self.stack.enter_context(self.nc.psum_tensor(name, list(shape), dt))

    def barrier(self):
        lasts = {}
        for oid, op in enumerate(self.ops):
            if op['dma'] is None and op['fn'] is not None:
                lasts[op['eng']] = oid
        dmas = list(self.last_dma.values())
        for e in ENGS:
            self.add(e, None, xdeps=[o for en, o in lasts.items() if en != e] + dmas)
        self.lane_of = {}

    def add(self, eng, fn, reads=(), writes=(), dma=None, xdeps=()):
        oid = len(self.ops)
        is_dma = dma is not None
        deps = set((d, 0) for d in xdeps)
        for k in reads:
            w = self.last_w.get(k)
            if w is not None:
                deps.add((w, 0))
            if isinstance(k, tuple) and k[0] == 'ps':
                for r in self.readers.get(k, ()):
                    if self.ops[r]['eng'] != eng:
                        deps.add((r, 3))
        for k in writes:
            w = self.last_w.get(k)
            if w is not None:
                deps.add((w, 1))
            for r in self.readers.get(k, ()):
                deps.add((r, 2))
        fdeps = set()
        for d, kind in deps:
            o = self.ops[d]
            if o['dma'] is None and not is_dma and o['eng'] == eng:
                if eng == 'pe' or fn is None:
                    continue
            fdeps.add(d)
        for d in fdeps:
            self.ops[d]['sig'] = True
        op = dict(eng=eng, fn=fn, deps=sorted(fdeps), dma=dma, sig=False)
        if is_dma:
            g = self.dma_groups.setdefault(dma, 0)
            self.dma_groups[dma] = g + 1
            ns = self.slots.get(dma, 1)
            slot = g % ns
            lk = (dma, slot, eng == 'pool')
            lane = self.lane_of.get(lk)
            if lane is None:
                lane = ('sw' if eng == 'pool' else 'hw', sum(1 for q in self.lane_of if q[2] == (eng == 'pool')))
                self.lane_of[lk] = lane
            op['dma'] = lane
            op['dma_idx'] = self.lane_cnt.get(lane, 0)
            self.lane_cnt[lane] = op['dma_idx'] + 1
            op['sig'] = True
            prev = self.last_dma.get(lane)
            if prev is not None and prev not in fdeps:
                op['deps'] = sorted(fdeps | {prev})
            self.last_dma[lane] = oid
        self.ops.append(op)
        for k in reads:
            self.readers.setdefault(k, []).append(oid)
        for k in writes:
            self.last_w[k] = oid
            self.readers[k] = []
        return oid

    def dma(self, eng, out, in_, reads=(), writes=(), group=None):
        g = group if group is not None else writes[0]
        return self.add(eng, lambda e: e.dma_start(out=out, in_=in_), reads, writes, dma=g)

    def fence(self, eng, reads):
        return self.add(eng, None, reads=reads)

    def emit(self):
        nc = self.nc
        cnt = {e: 0 for e in ENGS}
        for op in self.ops:
            if op['dma'] is None and op['sig']:
                op['cidx'] = cnt[op['eng']]
                cnt[op['eng']] += 1
        sems = {}

        def sem(key):
            if key not in sems:
                sems[key] = self.stack.enter_context(nc.semaphore('s%d' % len(sems)))
            return sems[key]

        def dep_target(d):
            o = self.ops[d]
            if o['dma'] is not None:
                gi = o['dma_idx']
                return ('d', o['dma'], gi // DGEN), 16 * (gi % DGEN + 1), ('d', o['dma']), gi
            ci = o['cidx']
            return ('e', o['eng'], ci // GEN), (ci % GEN) + 1, ('e', o['eng']), ci

        waited = {e: {} for e in ENGS}
        for op in self.ops:
            e = op['eng']
            ws = {}
            for d in op['deps']:
                skey, val, stream, pos = dep_target(d)
                if waited[e].get(stream, -1) >= pos:
                    continue
                waited[e][stream] = pos
                if ws.get(skey, 0) < val:
                    ws[skey] = val
            op['waits'] = [(sem(k), v) for k, v in ws.items()]
            if op['sig']:
                if op['dma'] is not None:
                    gi = op['dma_idx']
                    op['inc'] = (sem(('d', op['dma'], gi // DGEN)), 16)
                else:
                    op['inc'] = (sem(('e', e, op['cidx'] // GEN)), 1)
        self.nsems = len(sems)
        engobj = dict(pe='tensor', act='scalar', dve='vector', pool='gpsimd', sp='sync')
        with nc.Block() as block:
            for e in ENGS:
                mine = [op for op in self.ops if op['eng'] == e]
                if not mine:
                    continue

                def body(eng, mine=mine):
                    for op in mine:
                        for s, v in op['waits']:
                            eng.wait_ge(s, v)
                        if op['fn'] is None:
                            continue
                        ins = op['fn'](eng)
                        if op['sig']:
                            ins.then_inc(op['inc'][0], op['inc'][1])
                getattr(block, engobj[e])(body)
        self.stack.close()


class Arena:
    def __init__(self, P, name, cols, dt):
        self.t = P.sb(name, [128, cols], dt)
        self.cols = cols
        self.off = 0

    def get(self, shape):
        n = 1
        for d in shape[1:]:
            n *= d
        assert self.off + n <= self.cols, (self.off, n, self.cols)
        ap = self.t[:, self.off:self.off + n]
        self.off += n
        if len(shape) == 3:
            ap = ap.rearrange("p (a b) -> p a b", b=shape[2])
        return ap

    def reset(self, off=0):
        self.off = off


class Env:
    def __init__(self, nc, cols=52000):
        self.nc = nc
        self.P = Prog(nc)
        self.arena = Arena(self.P, "arena", cols, F32)
        self.ps = [self.P.ps("ps%d" % i, [128, 512]) for i in range(8)]

    @staticmethod
    def _n(shape):
        n = 1
        for d in shape[1:]:
            n *= d
        return n

    def f32(self, shape):
        return self.arena.get(list(shape))

    def _cast(self, shape, dt, per):
        n = self._n(shape)
        ap = self.arena.get([128, (n + per - 1) // per]).bitcast(dt)[:, 0:n]
        if len(shape) == 3:
            ap = ap.rearrange("p (a b) -> p a b", b=shape[2])
        return ap

    def bf16(self, shape):
        return self._cast(list(shape), BF16, 2)

    def i32(self, shape):
        return self._cast(list(shape), I32, 1)

    def mark(self):
        return self.arena.off

    def reset(self, to=0):
        self.P.barrier()
        self.arena.reset(to)


EPS = 1e-6


def arr_cols(W, cb):
    K, N = W.shape
    return np.ascontiguousarray(W.reshape(K // 128, 128, N // cb, cb).transpose(2, 1, 0, 3))


def arr_vec(v):
    return np.ascontiguousarray(v.reshape(-1, 128).T)


def build_ffn(D, FF, KM, PD, NT, final, T=512, CB=256, E=None, io=None, halo=True, pfx="", cscale=None):
    own = E is None
    if own:
        E = Env(bass.Bass("TRN2", target_bir_lowering=False))
    nc = E.nc
    io = io or {}
    KC, KMC, FC, PC = D // 128, KM // 128, FF // 128, PD // 128
    NCB = D // CB
    H0 = 128 if halo else 0
    NTOT = NT + H0
    dr = lambda n, s, dt=F32, kind="ExternalInput": nc.dram_tensor(pfx + n, list(s), dt, kind=kind).ap()
    hin = io['hin'] if 'hin' in io else dr("hin", [NTOT, D])
    ysrc = io.get('ysrc')
    yT = None if ysrc is not None else dr("yT", [128, KMC, NTOT])
    pT = dr("pT", [128, PC, NT])
    w_out = dr("w_out", [NCB, 128, KMC, CB])
    w_up = dr("w_up", [2 * FC, 128, KC, 128])
    w_down = dr("w_down", [NCB, 128, FC, CB])
    w_proj = dr("w_proj", [NCB, 128, PC, CB])
    w_gate = dr("w_gate", [NCB, 128, KC, CB])
    NCONST = 2 * KC + 8 * FC
    consts = dr("consts", [128, NCONST])
    idn = dr("idn", [128, 128])
    fng = dr("fng", [D])
    out = io['out'] if 'out' in io else dr("out", [NT, D], kind="ExternalOutput")

    P = E.P
    P.slots['out'] = 4
    NTL = T // 128
    h = [E.f32([128, D]) for i in range(NTL)]
    big = E.bf16([128, max(FC, KMC), T])
    wcol = [E.bf16([128, max(FC, KMC, KC), CB]) for i in range(2)]
    wp = [E.bf16([128, PC, CB]) for i in range(2)]
    hnT = E.bf16([128, KC, T])
    pTs = E.bf16([128, PC, T])
    NWU = 3
    wup = [[E.bf16([128, KC, 128]) for g in range(2)] for i in range(NWU)]
    ub = [E.f32([128, T + 2]) for g in range(2)]
    tmp = [E.f32([128, T]) for g in range(2)]
    gs = E.f32([128, T])
    carry = E.f32([128, 2 * FC, 2])
    cst = E.f32([128, NCONST])
    ident = E.f32([128, 128])
    junk = E.bf16([128, D])
    hs = E.f32([128, D])
    st = E.f32([128, 4])
    sg = E.f32([128, CB])
    fg = E.f32([128, D]) if final else None
    ps = E.ps
    epsc = E.f32([128, 1])
    P.add('dve', lambda e: e.memset(epsc[:], EPS), writes=['epsc'])
    C = dict(D=D, junk=junk, st=st, hs=hs, ident=ident, ps=ps, epsc=epsc)
    g_ffn = cst[:, 0:KC]
    g_gate = cst[:, KC:2 * KC]
    cw = lambda j, c: cst[:, 2 * KC + j * 2 * FC + c: 2 * KC + j * 2 * FC + c + 1]
    cb_ = lambda c: cst[:, 2 * KC + 6 * FC + c: 2 * KC + 6 * FC + c + 1]

    P.dma('sp', cst[:], consts, writes=['consts'])
    P.dma('sp', ident[:], idn, writes=['ident'])
    if final:
        P.dma('sp', fg[:], fng.partition_broadcast(128), writes=['fg'])
    wcnt = [0]

    def lin_tok(xT, xkey, KCx, Wd, ntl, tile_off, consume, extra=None):
        for c in range(NCB):
            b = wcnt[0] % 2
            wcnt[0] += 1
            P.dma('pool', wcol[b][:, 0:KCx, :], Wd[c], writes=[('wcol', b)])
            if extra is not None:
                P.dma('pool', wp[b][:], w_proj[c], writes=[('wp', b)])
            for i in range(ntl):
                bank = 4 + (i % 2)
                t0 = tile_off + i * 128
                for k in range(KCx):
                    P.add('pe', lambda e, k=k, b=b, bank=bank, t0=t0: e.matmul(
                        out=ps[bank][:, 0:CB], lhsT=xT[:, k, t0:t0 + 128], rhs=wcol[b][:, k, :],
                        start=(k == 0), stop=(k == KCx - 1)),
                        reads=[xkey(k), ('wcol', b)], writes=[('ps', bank)])
                if extra is not None:
                    for k in range(PC):
                        P.add('pe', lambda e, k=k, b=b, bank=bank, t0=t0: e.matmul(
                            out=ps[bank][:, CB:2 * CB], lhsT=pTs[:, k, t0:t0 + 128], rhs=wp[b][:, k, :],
                            start=(k == 0), stop=(k == PC - 1)),
                            reads=['pTs', ('wp', b)], writes=[('ps', bank)])
                consume(i, c, bank)

    def add_to_h(i, c, bank):
        P.add('dve', lambda e: e.tensor_tensor(out=h[i][:, c * CB:(c + 1) * CB], in0=h[i][:, c * CB:(c + 1) * CB],
                                               in1=ps[bank][:, 0:CB], op=ALU.add),
              reads=[('ps', bank), ('h', i, c)], writes=[('h', i, c)])

    def ple_consume(i, c, bank):
        P.add('act', lambda e: e.activation(out=sg[:], in_=ps[bank][:, 0:CB], func=AF.Sigmoid),
              reads=[('ps', bank)], writes=['sg'])
        P.add('dve', lambda e: e.tensor_tensor(out=sg[:], in0=sg[:], in1=ps[bank][:, CB:2 * CB], op=ALU.mult),
              reads=['sg', ('ps', bank)], writes=['sg'])
        P.add('dve', lambda e: e.tensor_tensor(out=h[i][:, c * CB:(c + 1) * CB], in0=h[i][:, c * CB:(c + 1) * CB],
                                               in1=sg[:], op=ALU.add),
              reads=['sg', ('h', i, c)], writes=[('h', i, c)])

    hkeys = lambda i: [('h', i, c) for c in range(NCB)]
    P.add('pool', lambda e: e.memset(carry[:], 0.0), writes=['carry'])

    blocks = ([(0, 1, True)] if halo else []) + [(H0 + b * T, NTL, False) for b in range(NT // T)]
    upc = [0]
    for (tok0, ntl, halo) in blocks:
        TT = ntl * 128
        for i in range(ntl):
            P.dma('sp', h[i][:], hin[tok0 + i * 128: tok0 + (i + 1) * 128, :], writes=hkeys(i))
        if ysrc is None:
            P.dma('pool', big[:, 0:KMC, 0:TT], yT[:, :, tok0:tok0 + TT], writes=[('big', k) for k in range(KMC)],
                  group='bigld')
        else:
            W = min(KM, D)
            for i in range(ntl):
                for q in range(KM // W):
                    P.dma('sp', hs[:, 0:W], ysrc[tok0 + i * 128: tok0 + (i + 1) * 128, q * W:(q + 1) * W],
                          writes=['hs'])
                    for kk in range(W // 128):
                        bank = 6 + (kk // 4) % 2
                        off = (kk % 4) * 128
                        kc = q * (W // 128) + kk
                        P.add('pe', lambda e, kk=kk, bank=bank, off=off: e.transpose(
                            out=ps[bank][:, off:off + 128], in_=hs[:, kk * 128:(kk + 1) * 128], identity=ident[:]),
                            reads=['hs', 'ident'], writes=[('ps', bank)])
                        if bank == 6:
                            P.add('act', lambda e, kc=kc, bank=bank, off=off, i=i: e.copy(
                                out=big[:, kc, i * 128:(i + 1) * 128], in_=ps[bank][:, off:off + 128]),
                                reads=[('ps', bank)], writes=[('big', kc)])
                        else:
                            P.add('dve', lambda e, kc=kc, bank=bank, off=off, i=i: e.tensor_copy(
                                out=big[:, kc, i * 128:(i + 1) * 128], in_=ps[bank][:, off:off + 128]),
                                reads=[('ps', bank)], writes=[('big', kc)])
        lin_tok(big, lambda k: ('big', k), KMC, w_out, ntl, 0, add_to_h)
        for i in range(ntl):
            _rms(P, h, i, hkeys, g_ffn, hnT, 'hnT', C)
        for j in range(FC):
            b = upc[0] % NWU
            pb_ = upc[0] % 2
            upc[0] += 1
            for g in range(2):
                P.dma('pool', wup[b][g][:], w_up[j + g * FC], writes=[('wup', b, g)])
            for g in range(2):
                ch = j + g * FC
                bank = (pb_ * 2 + g)
                if halo:
                    c0, n = 126, 2
                else:
                    c0, n = 0, TT
                for k in range(KC):
                    P.add('pe', lambda e, k=k, b=b, g=g, bank=bank, c0=c0, n=n: e.matmul(
                        out=ps[bank][:, 0:n], lhsT=wup[b][g][:, k, :], rhs=hnT[:, k, c0:c0 + n],
                        start=(k == 0), stop=(k == KC - 1)),
                        reads=[('hnT', k), ('wup', b, g)], writes=[('ps', bank)])
                if halo:
                    P.add('act', lambda e, ch=ch, bank=bank: e.copy(out=carry[:, ch, :], in_=ps[bank][:, 0:2]),
                          reads=[('ps', bank), 'carry'], writes=[('carry', ch)])
                    continue
                P.add('dve', lambda e, ch=ch, g=g: e.tensor_copy(out=ub[g][:, 0:2], in_=carry[:, ch, :]),
                      reads=[('carry', ch), 'carry'], writes=[('ub', g, 'c')])
                P.add('act', lambda e, g=g, bank=bank: e.copy(out=ub[g][:, 2:2 + TT], in_=ps[bank][:, 0:TT]),
                      reads=[('ps', bank)], writes=[('ub', g)])
                P.add('dve', lambda e, ch=ch, g=g: e.tensor_copy(out=carry[:, ch, :], in_=ub[g][:, TT:TT + 2]),
                      reads=[('ub', g), 'carry'], writes=[('carry', ch)])
                P.add('dve', lambda e, ch=ch, g=g: e.tensor_scalar(
                    out=tmp[g][:, 0:TT], in0=ub[g][:, 0:TT], scalar1=cw(0, ch), scalar2=None, op0=ALU.mult),
                    reads=[('ub', g), ('ub', g, 'c'), 'consts'], writes=[('tmp', g)])
                for tap in (1, 2):
                    P.add('dve', lambda e, ch=ch, g=g, tap=tap: e.scalar_tensor_tensor(
                        out=tmp[g][:, 0:TT], in0=ub[g][:, tap:tap + TT], scalar=cw(tap, ch), in1=tmp[g][:, 0:TT],
                        op0=ALU.mult, op1=ALU.add),
                        reads=[('ub', g), ('ub', g, 'c'), ('tmp', g), 'consts'], writes=[('tmp', g)])
            if halo:
                continue
            P.add('act', lambda e, j=j: e.activation(out=gs[:, 0:TT], in_=tmp[0][:, 0:TT], func=AF.Silu,
                                                     bias=cb_(j)), reads=[('tmp', 0), 'consts'], writes=['gs'])
            P.add('dve', lambda e, j=j: e.scalar_tensor_tensor(
                out=big[:, j, 0:TT], in0=tmp[1][:, 0:TT], scalar=cb_(j + FC), in1=gs[:, 0:TT],
                op0=ALU.add, op1=ALU.mult), reads=[('tmp', 1), 'gs', 'consts'], writes=[('big', j)])
        if halo:
            if cscale is not None:
                ck = ['carry'] + [('carry', ch) for ch in range(2 * FC)]
                P.add('dve', lambda e: e.tensor_scalar(out=carry[:], in0=carry[:], scalar1=cscale[:, 0:1], scalar2=None,
                                                       op0=ALU.mult), reads=ck + ['cscale'], writes=ck)
            continue
        lin_tok(big, lambda k: ('big', k), FC, w_down, ntl, 0, add_to_h)
        P.dma('pool', pTs[:, :, 0:TT], pT[:, :, tok0 - H0: tok0 - H0 + TT], writes=['pTs'])
        for i in range(ntl):
            _rms(P, h, i, hkeys, g_gate, hnT, 'hnT', C)
        lin_tok(hnT, lambda k: ('hnT', k), KC, w_gate, ntl, 0, ple_consume, extra=True)
        for i in range(ntl):
            orow = tok0 - H0 + i * 128
            if final:
                P.add('act', lambda e, i=i: e.activation(out=junk[:, 0:D], in_=h[i][:], func=AF.Square,
                                                         accum_out=st[:, 0:1]),
                      reads=hkeys(i), writes=['st', 'junk'])
                P.add('act', lambda e: e.activation(out=st[:, 1:2], in_=st[:, 0:1], func=AF.Sqrt, scale=1.0 / D,
                                                    bias=epsc[:, 0:1]), reads=['st', 'epsc'], writes=['st1'])
                P.add('dve', lambda e: e.reciprocal(out=st[:, 2:3], in_=st[:, 1:2]), reads=['st1'], writes=['st2'])
                P.add('dve', lambda e, i=i: e.scalar_tensor_tensor(
                    out=h[i][:], in0=h[i][:], scalar=st[:, 2:3], in1=fg[:], op0=ALU.mult, op1=ALU.mult),
                    reads=hkeys(i) + ['st2', 'fg'], writes=hkeys(i))
            P.dma('sp', out[orow:orow + 128, :], h[i][:], reads=hkeys(i), writes=[(pfx + 'out', orow)], group='out')
    okeys = [(pfx + 'out', r) for r in range(0, NT, 128)]
    if not own:
        return okeys
    P.fence('sp', okeys)
    P.emit()
    return nc


def _rms(P, h, i, hkeys, gcol, dstT, dkey, C):
    D = C['D']
    KC = D // 128
    junk, st, hs, ident, pst = C['junk'], C['st'], C['hs'], C['ident'], C['ps']
    P.add('act', lambda e: e.activation(out=junk[:, 0:D], in_=h[i][:], func=AF.Square, accum_out=st[:, 0:1]),
          reads=hkeys(i), writes=['st', 'junk'])
    P.add('act', lambda e: e.activation(out=st[:, 1:2], in_=st[:, 0:1], func=AF.Sqrt, scale=1.0 / D,
                                        bias=C['epsc'][:, 0:1]), reads=['st', 'epsc'], writes=['st1'])
    P.add('dve', lambda e: e.reciprocal(out=st[:, 2:3], in_=st[:, 1:2]), reads=['st1'], writes=['st2'])
    P.add('dve', lambda e: e.tensor_scalar(out=hs[:, 0:D], in0=h[i][:], scalar1=st[:, 2:3], scalar2=None,
                                           op0=ALU.mult), reads=hkeys(i) + ['st2'], writes=['hs'])
    for k in range(KC):
        bank = 6 + (k // 4) % 2
        off = (k % 4) * 128
        P.add('pe', lambda e, k=k, bank=bank, off=off: e.transpose(
            out=pst[bank][:, off:off + 128], in_=hs[:, k * 128:(k + 1) * 128], identity=ident[:]),
            reads=['hs', 'ident'], writes=[('ps', bank)])
        if bank == 6:
            P.add('act', lambda e, k=k, bank=bank, off=off: e.activation(
                out=dstT[:, k, i * 128:(i + 1) * 128], in_=pst[bank][:, off:off + 128], func=AF.Copy,
                scale=gcol[:, k:k + 1]), reads=[('ps', bank), 'consts'], writes=[(dkey, k)])
        else:
            P.add('dve', lambda e, k=k, bank=bank, off=off: e.tensor_scalar(
                out=dstT[:, k, i * 128:(i + 1) * 128], in0=pst[bank][:, off:off + 128], scalar1=gcol[:, k:k + 1],
                scalar2=None, op0=ALU.mult), reads=[('ps', bank), 'consts'], writes=[(dkey, k)])


def ffn_inputs(hin, y, p, w_out, ffn_norm, w_up, conv_w, conv_b, w_down, w_proj, gate_norm, w_gate, final_norm, CB=256):
    D = hin.shape[1]
    FF = w_down.shape[0]
    KM = y.shape[1]
    KC, FC = D // 128, FF // 128
    consts = np.concatenate([arr_vec(ffn_norm), arr_vec(gate_norm)] + [arr_vec(conv_w[j]) for j in range(3)]
                            + [arr_vec(conv_b)], axis=1).astype(np.float32)
    d = dict(
        hin=np.ascontiguousarray(hin),
        yT=np.ascontiguousarray(y.T.reshape(KM // 128, 128, -1).transpose(1, 0, 2)),
        pT=np.ascontiguousarray(p.T.reshape(p.shape[1] // 128, 128, -1).transpose(1, 0, 2)),
        consts=np.ascontiguousarray(consts), idn=np.eye(128, dtype=np.float32),
        fng=np.ascontiguousarray(final_norm.astype(np.float32)),
    )
    return d


def ffn_weights(w_out, w_up, w_down, w_proj, w_gate, CB=256):
    D = w_out.shape[1]
    KC = D // 128
    return dict(
        w_out=arr_cols(w_out, CB), w_down=arr_cols(w_down, CB), w_proj=arr_cols(w_proj, CB),
        w_gate=arr_cols(w_gate, CB),
        w_up=np.ascontiguousarray(w_up.reshape(KC, 128, -1, 128).transpose(2, 1, 0, 3)),
    )


PI = float(np.pi)
RET_HEADS_ALL = 8


def sincos_block(P, posf, c0, T, invf, pf, tmpf, tmpi, negpi, cosT, sinT, tag):
    for (dst, shift, nm) in ((sinT, PI, 'sin'), (cosT, 1.5 * PI, 'cos')):
        P.add('dve', lambda e, shift=shift: e.tensor_scalar(out=pf[:, 0:T], in0=posf[:, c0:c0 + T], scalar1=invf,
                                                            scalar2=shift, op0=ALU.mult, op1=ALU.add),
              reads=['posf', 'cst'], writes=[tag + 'pf'])
        P.add('dve', lambda e: e.tensor_scalar(out=tmpf[:, 0:T], in0=pf[:, 0:T], scalar1=1.0 / (2 * PI), scalar2=None,
                                               op0=ALU.mult), reads=[tag + 'pf'], writes=[tag + 'tmpf'])
        P.add('dve', lambda e: e.tensor_copy(out=tmpi[:, 0:T], in_=tmpf[:, 0:T]), reads=[tag + 'tmpf'],
              writes=[tag + 'tmpi'])
        P.add('dve', lambda e: e.tensor_copy(out=tmpf[:, 0:T], in_=tmpi[:, 0:T]), reads=[tag + 'tmpi'],
              writes=[tag + 'tmpf'])
        P.add('dve', lambda e: e.scalar_tensor_tensor(out=pf[:, 0:T], in0=tmpf[:, 0:T], scalar=-2 * PI, in1=pf[:, 0:T],
                                                      op0=ALU.mult, op1=ALU.add),
              reads=[tag + 'pf', tag + 'tmpf'], writes=[tag + 'pf'])
        P.add('dve', lambda e: e.tensor_scalar(out=tmpf[:, 0:T], in0=pf[:, 0:T], scalar1=0.0, scalar2=2 * PI,
                                               op0=ALU.is_lt, op1=ALU.mult), reads=[tag + 'pf'], writes=[tag + 'tmpf'])
        P.add('dve', lambda e: e.tensor_tensor(out=pf[:, 0:T], in0=pf[:, 0:T], in1=tmpf[:, 0:T], op=ALU.add),
              reads=[tag + 'pf', tag + 'tmpf'], writes=[tag + 'pf'])
        P.add('act', lambda e, dst=dst: e.activation(out=dst[:, 0:T], in_=pf[:, 0:T], func=AF.Sin, bias=negpi),
              reads=[tag + 'pf', 'negpi'], writes=[tag + nm])


def build_ret(D, S, T=512, E=None, io=None, pfx="", ycol0=0):
    own = E is None
    if own:
        E = Env(bass.Bass("TRN2", target_bir_lowering=False))
    nc = E.nc
    io = io or {}
    KC = D // 128
    H = 4
    DK, DV = 256, 512
    NB = S // T
    NTL = T // 128
    dr = lambda n, s, dt=F32, kind="ExternalInput": nc.dram_tensor(pfx + n, list(s), dt, kind=kind).ap()
    x = io['x'] if 'x' in io else dr("x", [S, D])
    pos = dr("pos", [S], I32)
    consts = dr("consts", [128, KC + 1 + H])
    wfm = dr("wfm", [4 * H, 128, KC, 128])
    wtm = dr("wtm", [2 * H, 128, KC, DV])
    decayT = dr("decayT", [H, 128, 128])
    xi2 = dr("xi2", [H, 128, 2, 128])
    normw = dr("normw", [H * DV])
    idn = dr("idn", [128, 128])
    gam = dr("gam", [128, H])
    y = io['y'] if 'y' in io else dr("y", [S, H * DV], kind="ExternalOutput")
    qTs = dr("qTs", [2, H, 2, 128, S], kind="Internal")
    vs = dr("vs", [2, H, S, DV], kind="Internal")

    P = E.P
    P.slots['scr'] = 8
    P.slots['y'] = 4
    A = lambda eng, fn, r, w: P.add(eng, fn, reads=r, writes=w)
    cst = E.f32([128, KC + 1 + H])
    gm = E.f32([128, H])
    ident = E.f32([128, 128])
    st = E.f32([128, 4])
    epsc = E.f32([128, 1])
    negpi = E.f32([128, 1])
    mark = E.mark()
    h2 = [E.f32([128, D]) for i in range(2)]
    h = [h2[i % 2] for i in range(NTL)]
    hnT = E.bf16([128, KC, T])
    junk = E.bf16([128, max(D, 512)])
    hs = E.f32([128, D])
    posi = E.i32([128, T])
    posf = E.f32([128, T])
    pf = E.f32([128, T]); tmpf = E.f32([128, T]); tmpi = E.i32([128, T])
    cosT = E.f32([128, T]); sinT = E.f32([128, T])
    wfb = [E.bf16([128, KC, 128]) for i in range(4)]
    wtb = [E.bf16([128, KC, DV]) for i in range(3)]
    raws = [[E.f32([128, T]) for i in range(2)] for _ in range(2)]
    rts = [[E.f32([128, T]) for i in range(4)] for _ in range(2)]
    o12s = [[E.f32([128, T]) for i in range(2)] for _ in range(2)]
    stg = [E.f32([128, DV]) for i in range(2)]
    ps = E.ps
    C = dict(D=D, junk=junk, st=st, hs=hs, ident=ident, ps=ps, epsc=epsc)
    g_attn = cst[:, 0:KC]
    invf = cst[:, KC:KC + 1]

    P.dma('sp', cst[:], consts, writes=['cst'])
    P.dma('sp', gm[:], gam, writes=['gm'])
    P.dma('sp', ident[:], idn, writes=['ident'])
    A('dve', lambda e: e.memset(epsc[:], EPS), [], ['epsc'])
    A('dve', lambda e: e.memset(negpi[:], -PI), [], ['negpi'])
    hkeys = lambda i: [('h', i % 2)]
    P.last_w['consts'] = P.last_w['cst']
    P.readers['consts'] = []

    fcnt = [0]
    tcnt = [0]
    scnt = [0]
    for blk in range(NB):
        c0 = blk * T
        if blk == 0:
            for i in range(2):
                P.dma('sp', h[i][:], x[i * 128:(i + 1) * 128, :], writes=hkeys(i))
        for i in range(NTL):
            _rms(P, h, i, hkeys, g_attn, hnT, 'hnT', C)
            gi_ = blk * NTL + i + 2
            if gi_ < NB * NTL:
                P.dma('sp', h[i][:], x[gi_ * 128:(gi_ + 1) * 128, :], writes=hkeys(i))
        P.dma('sp', posi[:], pos[c0:c0 + T].partition_broadcast(128), writes=['posi'])
        A('dve', lambda e: e.tensor_copy(out=posf[:], in_=posi[:]), ['posi'], ['posf'])
        sincos_block(P, posf, 0, T, invf, pf, tmpf, tmpi, negpi[:, 0:1], cosT, sinT, 'r')
        def qk_chunk(qk, hh, blk, c0, pp):
            raw, rt, o12 = raws[pp], rts[pp], o12s[pp]
            for c in range(2):
                ci = qk * 2 * H + hh * 2 + c
                b = fcnt[0] % 4
                bank = fcnt[0] % 2
                fcnt[0] += 1
                P.dma('pool', wfb[b][:], wfm[ci], writes=[('wfb', b)])
                for k in range(KC):
                    A('pe', lambda e, k=k, b=b, bank=bank: e.matmul(
                        out=ps[bank][:, 0:T], lhsT=wfb[b][:, k, :], rhs=hnT[:, k, 0:T],
                        start=(k == 0), stop=(k == KC - 1)), [('hnT', k), ('wfb', b)], [('ps', bank)])
                A('act', lambda e, c=c, bank=bank: e.copy(out=raw[c][:, 0:T], in_=ps[bank][:, 0:T]),
                  [('ps', bank)], [('raw', c, pp)])
            A('dve', lambda e: e.tensor_tensor(out=rt[0][:], in0=raw[0][:], in1=cosT[:], op=ALU.mult),
              [('raw', 0, pp), 'rcos'], [('rt', 0, pp)])
            A('dve', lambda e: e.tensor_tensor(out=rt[1][:], in0=raw[1][:], in1=sinT[:], op=ALU.mult),
              [('raw', 1, pp), 'rsin'], [('rt', 1, pp)])
            A('dve', lambda e: e.tensor_tensor(out=rt[2][:], in0=raw[1][:], in1=cosT[:], op=ALU.mult),
              [('raw', 1, pp), 'rcos'], [('rt', 2, pp)])
            A('dve', lambda e: e.tensor_tensor(out=rt[3][:], in0=raw[0][:], in1=sinT[:], op=ALU.mult),
              [('raw', 0, pp), 'rsin'], [('rt', 3, pp)])
            A('dve', lambda e: e.tensor_tensor(out=o12[0][:], in0=rt[0][:], in1=rt[1][:], op=ALU.subtract),
              [('rt', 0, pp), ('rt', 1, pp)], [('o12', 0, pp)])
            A('dve', lambda e: e.tensor_tensor(out=o12[1][:], in0=rt[2][:], in1=rt[3][:], op=ALU.add),
              [('rt', 2, pp), ('rt', 3, pp)], [('o12', 1, pp)])
            for c in range(2):
                P.dma('sp', qTs[qk, hh, c, :, c0:c0 + T], o12[c][:], reads=[('o12', c, pp)],
                      writes=[('qTs', qk, hh, c, blk)], group='scr')
        for qk in range(2):
            for hh in range(H):
                qk_chunk(qk, hh, blk, c0, (qk * H + hh) % 2)
        for vg in range(2):
            for hh in range(H):
                b = tcnt[0] % 3
                tcnt[0] += 1
                P.dma('pool', wtb[b][:], wtm[vg * H + hh], writes=[('wtb', b)])
                for i in range(NTL):
                    bank = 2 + (i % 2)
                    for k in range(KC):
                        A('pe', lambda e, k=k, b=b, bank=bank, i=i: e.matmul(
                            out=ps[bank][:, 0:DV], lhsT=hnT[:, k, i * 128:(i + 1) * 128], rhs=wtb[b][:, k, :],
                            start=(k == 0), stop=(k == KC - 1)), [('hnT', k), ('wtb', b)], [('ps', bank)])
                    sb_ = scnt[0] % 2
                    scnt[0] += 1
                    A('act', lambda e, bank=bank, sb_=sb_, vg=vg: e.activation(
                        out=stg[sb_][:], in_=ps[bank][:, 0:DV], func=(AF.Silu if vg else AF.Copy)),
                      [('ps', bank)], [('stg', sb_)])
                    P.dma('sp', vs[vg, hh, c0 + i * 128: c0 + (i + 1) * 128, :], stg[sb_][:], reads=[('stg', sb_)],
                          writes=[('vs', vg, hh, blk * NTL + i)], group='scr')

    E.reset(mark)
    dcy = [E.f32([128, 128]) for i in range(H)]
    xit = [E.f32([128, 2, 128]) for i in range(H)]
    nw = [E.f32([128, DV]) for i in range(H)]
    state = [[E.f32([128, DV]) for j in range(2)] for i in range(H)]
    qt = [E.f32([128, 2, 128]) for i in range(H)]
    kt = [E.f32([128, 2, 128]) for i in range(H)]
    vt = [E.f32([128, DV]) for i in range(H)]
    gt = [E.f32([128, DV]) for i in range(H)]
    qd = [E.f32([128, 2, 128]) for i in range(H)]
    kz = [E.f32([128, 256]) for i in range(H)]
    AT = [E.f32([128, 128]) for i in range(H)]
    yt = [E.f32([128, DV]) for i in range(H)]
    sm = [E.f32([128, 8]) for i in range(H)]
    jk = [E.bf16([128, DV]) for i in range(H)]
    jk2 = [E.bf16([128, DV]) for i in range(H)]
    for hh in range(H):
        P.dma('sp', dcy[hh][:], decayT[hh], writes=[('dcy', hh)])
        P.dma('sp', xit[hh][:], xi2[hh], writes=[('xit', hh)])
        P.dma('sp', nw[hh][:], normw[hh * DV:(hh + 1) * DV].partition_broadcast(128), writes=[('nw', hh)])
        for j in range(2):
            A('pool', lambda e, hh=hh, j=j: e.memset(state[hh][j][:], 0.0), [], [('state', hh, j)])
    scr_q = lambda qk, hh: [('qTs', qk, hh, b) for b in range(NB)]
    scr_v = lambda vg, hh: [('vs', vg, hh, b) for b in range(NB)]
    def ret_body(n, hh):
        t0 = n * 128
        blk = t0 // T
        bA, bB = 2 * hh, 2 * hh + 1
        P.dma('sp', qt[hh][:], qTs[0, hh, :, :, t0:t0 + 128].rearrange("c p t -> p c t"),
              reads=[('qTs', 0, hh, 0, blk), ('qTs', 0, hh, 1, blk)], writes=[('qt', hh)])
        P.dma('sp', kt[hh][:], qTs[1, hh, :, :, t0:t0 + 128].rearrange("c p t -> p c t"),
              reads=[('qTs', 1, hh, 0, blk), ('qTs', 1, hh, 1, blk)], writes=[('kt', hh)])
        P.dma('sp', vt[hh][:], vs[0, hh, t0:t0 + 128, :], reads=[('vs', 0, hh, n)], writes=[('vt', hh)])
        P.dma('sp', gt[hh][:], vs[1, hh, t0:t0 + 128, :], reads=[('vs', 1, hh, n)], writes=[('gt', hh)])
        for j in range(2):
            A('pe', lambda e, hh=hh, j=j, bA=bA: e.matmul(
                out=ps[bA][:, 0:128], lhsT=kt[hh][:, j, :], rhs=qt[hh][:, j, :], start=(j == 0), stop=(j == 1)),
              [('kt', hh), ('qt', hh)], [('ps', bA)])
        for j in range(2):
            A('pe', lambda e, hh=hh, j=j, bA=bA: e.transpose(
                out=ps[bA][:, 128 + j * 128: 256 + j * 128], in_=kt[hh][:, j, :], identity=ident[:]),
              [('kt', hh), 'ident'], [('ps', bA)])
        A('dve', lambda e, hh=hh, bA=bA: e.tensor_tensor(out=AT[hh][:], in0=ps[bA][:, 0:128], in1=dcy[hh][:],
                                                         op=ALU.mult), [('ps', bA), ('dcy', hh)], [('AT', hh)])
        A('act', lambda e, hh=hh, bA=bA: e.activation(out=kz[hh][:], in_=ps[bA][:, 128:384], func=AF.Copy,
                                                      scale=cst[:, KC + 1 + hh: KC + 2 + hh]),
          [('ps', bA), 'cst'], [('kz', hh)])
        A('dve', lambda e, hh=hh: e.tensor_tensor(out=qd[hh][:], in0=qt[hh][:], in1=xit[hh][:], op=ALU.mult),
          [('qt', hh), ('xit', hh)], [('qd', hh)])
        yield
        A('pe', lambda e, hh=hh, bB=bB: e.matmul(out=ps[bB][:, 0:DV], lhsT=AT[hh][:], rhs=vt[hh][:],
                                                 start=True, stop=False),
          [('AT', hh), ('vt', hh)], [('ps', bB)])
        for j in range(2):
            A('pe', lambda e, hh=hh, j=j, bB=bB: e.matmul(out=ps[bB][:, 0:DV], lhsT=qd[hh][:, j, :],
                                                          rhs=state[hh][j][:], start=False, stop=(j == 1)),
              [('qd', hh), ('state', hh, j)], [('ps', bB)])
        yield
        s_ = sm[hh]
        A('act', lambda e, hh=hh, bB=bB: e.activation(out=jk[hh][:], in_=ps[bB][:, 0:DV], func=AF.Copy,
                                                      accum_out=sm[hh][:, 0:1]), [('ps', bB)], [('sm0', hh), ('jk', hh)])
        A('act', lambda e, hh=hh, bB=bB: e.activation(out=jk2[hh][:], in_=ps[bB][:, 0:DV], func=AF.Square,
                                                      accum_out=sm[hh][:, 1:2]), [('ps', bB)], [('sm1', hh), ('jk2', hh)])
        A('dve', lambda e, hh=hh: e.tensor_scalar(out=sm[hh][:, 2:3], in0=sm[hh][:, 0:1], scalar1=1.0 / DV,
                                                  scalar2=None, op0=ALU.mult), [('sm0', hh)], [('sm2', hh)])
        A('dve', lambda e, hh=hh: e.tensor_tensor(out=sm[hh][:, 3:4], in0=sm[hh][:, 2:3], in1=sm[hh][:, 2:3],
                                                  op=ALU.mult), [('sm2', hh)], [('sm3', hh)])
        A('dve', lambda e, hh=hh: e.scalar_tensor_tensor(out=sm[hh][:, 4:5], in0=sm[hh][:, 1:2], scalar=1.0 / DV,
                                                         in1=sm[hh][:, 3:4], op0=ALU.mult, op1=ALU.subtract),
          [('sm1', hh), ('sm3', hh)], [('sm4', hh)])
        A('act', lambda e, hh=hh: e.activation(out=sm[hh][:, 5:6], in_=sm[hh][:, 4:5], func=AF.Sqrt,
                                               bias=epsc[:, 0:1]), [('sm4', hh), 'epsc'], [('sm5', hh)])
        A('dve', lambda e, hh=hh: e.reciprocal(out=sm[hh][:, 6:7], in_=sm[hh][:, 5:6]), [('sm5', hh)],
          [('sm6', hh)])
        A('dve', lambda e, hh=hh, bB=bB: e.tensor_scalar(out=yt[hh][:], in0=ps[bB][:, 0:DV],
                                                         scalar1=sm[hh][:, 2:3], scalar2=sm[hh][:, 6:7],
                                                         op0=ALU.subtract, op1=ALU.mult),
          [('ps', bB), ('sm2', hh), ('sm6', hh)], [('yt', hh)])
        A('dve', lambda e, hh=hh: e.tensor_tensor(out=yt[hh][:], in0=yt[hh][:], in1=nw[hh][:], op=ALU.mult),
          [('yt', hh), ('nw', hh)], [('yt', hh)])
        A('dve', lambda e, hh=hh: e.tensor_tensor(out=yt[hh][:], in0=yt[hh][:], in1=gt[hh][:], op=ALU.mult),
          [('yt', hh), ('gt', hh)], [('yt', hh)])
        P.dma('pool', y[t0:t0 + 128, ycol0 + hh * DV: ycol0 + (hh + 1) * DV], yt[hh][:], reads=[('yt', hh)],
              writes=[(pfx + 'y', n, hh)], group='y')
        yield
        for j, bk in ((0, bA), (1, bB)):
            A('pe', lambda e, hh=hh, j=j, bk=bk: e.matmul(out=ps[bk][:, 0:DV], lhsT=kz[hh][:, j * 128:(j + 1) * 128],
                                                          rhs=vt[hh][:], start=True, stop=True),
              [('kz', hh), ('vt', hh)], [('ps', bk)])
            A('dve', lambda e, hh=hh, j=j, bk=bk: e.scalar_tensor_tensor(
                out=state[hh][j][:], in0=state[hh][j][:], scalar=gm[:, hh:hh + 1], in1=ps[bk][:, 0:DV],
                op0=ALU.mult, op1=ALU.add), [('state', hh, j), ('ps', bk), 'gm'], [('state', hh, j)])
    for n in range(S // 128):
        alive = [ret_body(n, hh) for hh in range(H)]
        while alive:
            for g_ in list(alive):
                try:
                    next(g_)
                except StopIteration:
                    alive.remove(g_)
    okeys = [(pfx + 'y', n, hh) for n in range(S // 128) for hh in range(H)]
    if not own:
        return okeys
    P.fence('sp', okeys)
    P.emit()
    return nc


def ret_consts(hg):
    c = 128
    hs_ = np.arange(4) + 4 * hg
    lg = np.log1p(-np.power(2.0, -5.0 - hs_.astype(np.float64)))
    pos = np.arange(c, dtype=np.float64)
    sc = 256 ** -0.5
    dec = np.where(pos[None, :, None] >= pos[None, None, :],
                   np.exp((pos[None, :, None] - pos[None, None, :]) * lg[:, None, None]), 0.0)
    decayT = np.ascontiguousarray(dec.transpose(0, 2, 1) * sc).astype(np.float32)
    xi = np.exp((pos + 1.0)[None, :] * lg[:, None]) * sc
    xi2 = np.ascontiguousarray(np.broadcast_to(xi[:, None, None, :], (4, 128, 2, c))).astype(np.float32)
    zeta = np.exp((c - 1.0 - pos)[None, :] * lg[:, None])
    gam = np.broadcast_to(np.exp(c * lg)[None, :], (128, 4)).astype(np.float32)
    return decayT, xi2, np.ascontiguousarray(zeta.T).astype(np.float32), np.ascontiguousarray(gam)


def ret_inputs(xb, posb, attn_norm, w_in, ret_norm, hg):
    D = xb.shape[0 + 1]
    KC = D // 128
    H, DK, DV = 4, 256, 512
    HA = RET_HEADS_ALL
    decayT, xi2, zetaT, gam = ret_consts(hg)
    invf = (10000.0 ** (-np.arange(0, DK, 2, dtype=np.float32) / DK)).astype(np.float32)
    consts = np.concatenate([arr_vec(attn_norm), invf[:, None], zetaT], axis=1).astype(np.float32)
    qo, ko, vo, go = 0, HA * DK, 2 * HA * DK, 2 * HA * DK + HA * DV
    chunks = []
    for base in (qo, ko):
        for hh in range(H):
            for c in range(2):
                col = base + (4 * hg + hh) * DK + c * 128
                chunks.append(w_in[:, col:col + 128].reshape(KC, 128, 128).transpose(1, 0, 2))
    wfm = np.ascontiguousarray(np.stack(chunks))
    tch = []
    for base in (vo, go):
        for hh in range(H):
            col = base + (4 * hg + hh) * DV
            tch.append(w_in[:, col:col + DV].reshape(KC, 128, DV).transpose(1, 0, 2))
    wtm = np.ascontiguousarray(np.stack(tch))
    normw = np.ascontiguousarray(ret_norm[4 * hg * DV:(4 * hg + 4) * DV])
    return dict(x=np.ascontiguousarray(xb), pos=np.ascontiguousarray(posb.astype(np.int32)), consts=np.ascontiguousarray(consts),
                wfm=wfm, wtm=wtm, decayT=decayT, xi2=xi2, normw=normw, idn=np.eye(128, dtype=np.float32), gam=gam)


NEG = -1.0e30


def build_l0(D, S, T=512, phases=(1, 2, 3), E=None, io=None, pfx="", ycols=(0, 512)):
    own = E is None
    if own:
        E = Env(bass.Bass("TRN2", target_bir_lowering=False))
    nc = E.nc
    io = io or {}
    KC = D // 128
    H = 4
    NB = S // T
    NTL = T // 128
    NT = S // 128
    dr = lambda n, s, dt=F32, kind="ExternalInput": nc.dram_tensor(pfx + n, list(s), dt, kind=kind).ap()
    x = io['x'] if 'x' in io else dr("x", [S, D])
    pos = dr("pos", [S], I32)
    NCST = KC + 4 + 4 + 48 + 1
    consts = dr("consts", [128, NCST])
    tabs = dr("tabs", [7, 128, 128])
    wfm = dr("wfm", [13, 128, KC, 128])
    wtm = dr("wtm", [3, 128, KC, 512])
    wab = dr("wab", [128, KC, 8])
    w2fm = dr("w2fm", [12, 128, 4, 128])
    w2tm = dr("w2tm", [128, 4, 512])
    hv = dr("hv", [3, 128])
    y = io['y'] if 'y' in io else dr("y", [S, 1024], kind="ExternalOutput")
    gT = dr("gT", [3, H, 128, S], kind="Internal")
    kpes = dr("kpes", [64, S], kind="Internal")
    zs = dr("zs", [S, 512], kind="Internal")
    gbs = dr("gbs", [S, 8], kind="Internal")
    qnTs = dr("qnTs", [H, 128, S], kind="Internal")
    qpTs = dr("qpTs", [H, 64, S], kind="Internal")
    knTs = dr("knTs", [H, 128, S], kind="Internal")
    mvs = dr("mvs", [S, 512], kind="Internal")

    P = E.P
    P.slots['scr'] = 8
    P.slots['y'] = 4
    A = lambda eng, fn, r, w: P.add(eng, fn, reads=r, writes=w)
    cst = E.f32([128, NCST])
    tab = E.f32([128, 7, 128])
    hvb = E.f32([128, 3, 128])
    st = E.f32([128, 4])
    epsc = E.f32([128, 1])
    negpi = E.f32([128, 1])
    onec = E.f32([128, 1])
    negA = E.f32([128, 4])
    mark = E.mark()

    class _AL:
        def __init__(self, fn):
            self.get = fn

        def reset(self):
            pass
    AFa = _AL(E.f32)
    ABa = _AL(E.bf16)
    ps = E.ps
    ident, ones, Umat, maskI, maskIT, strict01, Rm = [tab[:, i, :] for i in range(7)]
    g_attn = cst[:, 0:KC]
    g_q = cst[:, KC:KC + 4]
    g_kv = cst[:, KC + 4:KC + 8]
    ctap = lambda ci, j: cst[:, KC + 8 + ci * 4 + j: KC + 8 + ci * 4 + j + 1]
    invf = cst[:, KC + 56:KC + 57]
    dtb = hvb[:, 0, 0:4]
    gnw = hvb[:, 2, :]

    P.dma('sp', cst[:], consts, writes=['cst'])
    P.dma('sp', tab[:], tabs.rearrange("n p c -> p n c"), writes=['tab'])
    P.dma('sp', hvb[:], hv.partition_broadcast(128), writes=['hvb'])
    A('dve', lambda e: e.memset(epsc[:], EPS), [], ['epsc'])
    A('dve', lambda e: e.memset(negpi[:], -PI), [], ['negpi'])
    A('dve', lambda e: e.memset(onec[:], 1.0), [], ['onec'])
    A('act', lambda e: e.activation(out=negA[:], in_=hvb[:, 1, 0:4], func=AF.Exp), ['hvb'], ['negA0'])
    A('dve', lambda e: e.tensor_scalar(out=negA[:], in0=negA[:], scalar1=-1.0, scalar2=None, op0=ALU.mult),
      ['negA0'], ['negA'])
    P.last_w['consts'] = P.last_w['cst']
    P.readers['consts'] = []
    P.last_w['ident'] = P.last_w['tab']
    P.readers['ident'] = []

    if 1 in phases:
        h2 = [AFa.get([128, D]) for _ in range(2)]
        hbuf = [h2[i % 2] for i in range(NTL)]
        hs = AFa.get([128, max(D, 512)])
        junk = ABa.get([128, max(D, 512)])
        hnT = ABa.get([128, KC, T])
        cqnT = ABa.get([128, 4, T])
        cknT = ABa.get([128, 4, T])
        wfb = [ABa.get([128, KC, 128]) for _ in range(4)]
        wtbs = [ABa.get([128, KC, 512]) for _ in range(3)]
        wabb = ABa.get([128, KC, 8])
        w2b = [ABa.get([128, 4, 128]) for _ in range(4)]
        w2t = ABa.get([128, 4, 512])
        posi = E.i32([128, T])
        tmpi = E.i32([128, T])
        posf = AFa.get([128, T]); pf = AFa.get([128, T]); tmpf = AFa.get([128, T])
        cosF = AFa.get([128, T]); sinF = AFa.get([128, T])
        ubs = [AFa.get([128, T + 3]) for _ in range(2)]; tmps = [AFa.get([128, T]) for _ in range(2)]
        css = [AFa.get([128, T]) for _ in range(2)]; sqs = [AFa.get([128, T]) for _ in range(2)]
        rns = [AFa.get([128, T]) for _ in range(2)]
        raw = AFa.get([128, T]); r1 = AFa.get([128, T]); r2 = AFa.get([128, T])
        stgT = [AFa.get([128, T]) for _ in range(2)]
        stg = [AFa.get([128, 512]) for _ in range(2)]
        cqb = [AFa.get([128, 512]) for _ in range(2)]
        hs4 = [AFa.get([128, 512]) for _ in range(2)]
        st4 = [AFa.get([128, 4]) for _ in range(2)]
        abt = AFa.get([128, 8]); abw = AFa.get([128, 8]); gbt = AFa.get([128, 8])
        carry = AFa.get([128, 12, 3])
        C16 = dict(D=D, junk=junk, st=st, hs=hs, ident=ident, ps=ps, epsc=epsc)
        C4 = dict(D=512, junk=junk, st=st, hs=hs, ident=ident, ps=ps, epsc=epsc)
        hkeys = lambda i: [('h', i % 2)]
        cqkeys = lambda i: [('cqb', i % 2)]
        A('pool', lambda e: e.memset(carry[:], 0.0), [], ['carry'])
        cnt = dict(f=0, s=0, sT=0, c=0, w2=0)
        QSC = 128 ** -0.5
        MSC = 192 ** -0.5

        def rope64(src_key, blk, dst, dkey):
            A('pe', lambda e: e.matmul(out=ps[5][0:64, 0:T], lhsT=Rm[0:64, 0:64], rhs=raw[0:64, 0:T],
                                       start=True, stop=True), [src_key, 'tab'], [('ps', 5)])
            A('dve', lambda e: e.tensor_tensor(out=r1[0:64, :], in0=raw[0:64, :], in1=cosF[0:64, :], op=ALU.mult),
              [src_key, 'rcos'], ['r1'])
            A('dve', lambda e: e.tensor_tensor(out=r2[0:64, :], in0=ps[5][0:64, 0:T], in1=sinF[0:64, :], op=ALU.mult),
              [('ps', 5), 'rsin'], ['r2'])
            A('dve', lambda e: e.tensor_tensor(out=r1[0:64, :], in0=r1[0:64, :], in1=r2[0:64, :], op=ALU.add),
              ['r1', 'r2'], ['r1'])
            P.dma('sp', dst, r1[0:64, :], reads=['r1'], writes=[dkey], group='scr')

        for blk in range(NB):
            c0 = blk * T
            if blk == 0:
                for i in range(2):
                    P.dma('sp', hbuf[i][:], x[i * 128:(i + 1) * 128, :], writes=hkeys(i))
            for i in range(NTL):
                _rms(P, hbuf, i, hkeys, g_attn, hnT, 'hnT', C16)
                gi_ = blk * NTL + i + 2
                if gi_ < NB * NTL:
                    P.dma('sp', hbuf[i][:], x[gi_ * 128:(gi_ + 1) * 128, :], writes=hkeys(i))
            P.dma('sp', posi[:], pos[c0:c0 + T].partition_broadcast(128), writes=['posi'])
            A('dve', lambda e: e.tensor_copy(out=posf[:], in_=posi[:]), ['posi'], ['posf'])
            sincos_block(P, posf, 0, T, invf, pf, tmpf, tmpi, negpi[:, 0:1], cosF, sinF, 'r')
            def fm_chunk(ci, blk, c0, pp):
                ub, tmp, cs, sq, rn = ubs[pp], tmps[pp], css[pp], sqs[pp], rns[pp]
                b = cnt['f'] % 4
                bank = cnt['f'] % 2
                cnt['f'] += 1
                M = 64 if ci == 12 else 128
                P.dma('pool', wfb[b][:], wfm[ci], writes=[('wfb', b)])
                for k in range(KC):
                    A('pe', lambda e, k=k, b=b, bank=bank, M=M: e.matmul(
                        out=ps[bank][0:M, 0:T], lhsT=wfb[b][:, k, 0:M], rhs=hnT[:, k, 0:T],
                        start=(k == 0), stop=(k == KC - 1)), [('hnT', k), ('wfb', b)], [('ps', bank)])
                if ci == 12:
                    A('act', lambda e, bank=bank: e.copy(out=raw[0:64, :], in_=ps[bank][0:64, 0:T]),
                      [('ps', bank)], ['raw'])
                    yield
                    rope64('raw', blk, kpes[:, c0:c0 + T], ('kpes', blk))
                    return
                qkv, hh = ci // 4, ci % 4
                A('dve', lambda e, ci=ci: e.tensor_copy(out=ub[:, 0:3], in_=carry[:, ci, :]),
                  [('carry', ci), 'carry'], [('ubc', pp)])
                A('act', lambda e, bank=bank: e.copy(out=ub[:, 3:3 + T], in_=ps[bank][:, 0:T]), [('ps', bank)], [('ub', pp)])
                A('dve', lambda e, ci=ci: e.tensor_copy(out=carry[:, ci, :], in_=ub[:, T:T + 3]),
                  [('ub', pp), 'carry'], [('carry', ci)])
                A('dve', lambda e, ci=ci: e.tensor_scalar(out=tmp[:], in0=ub[:, 0:T], scalar1=ctap(ci, 0),
                                                          scalar2=None, op0=ALU.mult), [('ub', pp), ('ubc', pp), 'cst'], [('tmp', pp)])
                for j in (1, 2, 3):
                    A('dve', lambda e, ci=ci, j=j: e.scalar_tensor_tensor(
                        out=tmp[:], in0=ub[:, j:j + T], scalar=ctap(ci, j), in1=tmp[:], op0=ALU.mult, op1=ALU.add),
                      [('ub', pp), ('ubc', pp), ('tmp', pp), 'cst'], [('tmp', pp)])
                A('act', lambda e: e.activation(out=cs[:], in_=tmp[:], func=AF.Silu), [('tmp', pp)], [('cs', pp)])
                if qkv == 2:
                    P.dma('sp', gT[2, hh, :, c0:c0 + T], cs[:], reads=[('cs', pp)], writes=[('gT', 2, hh, blk)], group='scr')
                    return
                A('act', lambda e: e.activation(out=sq[:], in_=cs[:], func=AF.Square), [('cs', pp)], [('sq', pp)])
                yield
                A('pe', lambda e: e.matmul(out=ps[4][:, 0:T], lhsT=ones, rhs=sq[:], start=True, stop=True),
                  [('sq', pp), 'tab'], [('ps', 4)])
                A('act', lambda e: e.activation(out=rn[:], in_=ps[4][:, 0:T], func=AF.Sqrt, bias=epsc[:, 0:1]),
                  [('ps', 4), 'epsc'], [('rn', pp)])
                A('dve', lambda e: e.reciprocal(out=rn[:], in_=rn[:]), [('rn', pp)], [('rn', pp)])
                sb_ = cnt['sT'] % 2
                cnt['sT'] += 1
                A('dve', lambda e, sb_=sb_, qkv=qkv: e.scalar_tensor_tensor(
                    out=stgT[sb_][:], in0=cs[:], scalar=(QSC if qkv == 0 else 1.0), in1=rn[:],
                    op0=ALU.mult, op1=ALU.mult), [('cs', pp), ('rn', pp)], [('stgT', sb_)])
                P.dma('sp', gT[qkv, hh, :, c0:c0 + T], stgT[sb_][:], reads=[('stgT', sb_)],
                      writes=[('gT', qkv, hh, blk)], group='scr')
            def run_pipelined(gens):
                prev = None
                for g_ in gens:
                    try:
                        next(g_)
                    except StopIteration:
                        g_ = None
                    if prev is not None:
                        for _ in prev:
                            pass
                    prev = g_
                if prev is not None:
                    for _ in prev:
                        pass
            run_pipelined(fm_chunk(ci, blk, c0, ci % 2) for ci in range(13))
            def tm_tile(wi, i):
                wtb = wtbs[wi]
                bank = 2 + (i % 2)
                for k in range(KC):
                    A('pe', lambda e, k=k, bank=bank, i=i, wtb=wtb: e.matmul(
                        out=ps[bank][:, 0:512], lhsT=hnT[:, k, i * 128:(i + 1) * 128], rhs=wtb[:, k, :],
                        start=(k == 0), stop=(k == KC - 1)), [('hnT', k), ('wtb', wi)], [('ps', bank)])
                if wi == 0:
                    sb_ = cnt['s'] % 2
                    cnt['s'] += 1
                    A('act', lambda e, bank=bank, sb_=sb_: e.activation(out=stg[sb_][:], in_=ps[bank][:, 0:512],
                                                                        func=AF.Silu), [('ps', bank)], [('stg', sb_)])
                    P.dma('sp', zs[c0 + i * 128: c0 + (i + 1) * 128, :], stg[sb_][:], reads=[('stg', sb_)],
                          writes=[('zs', blk * NTL + i)], group='scr')
                    return
                pp = cnt['c'] % 2
                cnt['c'] += 1
                cq, s4, h4 = cqb[pp], st4[pp], hs4[pp]
                A('act', lambda e, bank=bank: e.copy(out=cq[:], in_=ps[bank][:, 0:512]), [('ps', bank)], [('cqb', pp)])
                A('act', lambda e: e.activation(out=junk[:, 0:512], in_=cq[:], func=AF.Square, accum_out=s4[:, 0:1]),
                  [('cqb', pp)], [('st4', pp, 0), 'junk'])
                A('act', lambda e: e.activation(out=s4[:, 1:2], in_=s4[:, 0:1], func=AF.Sqrt, scale=1.0 / 512,
                                                bias=epsc[:, 0:1]), [('st4', pp, 0), 'epsc'], [('st4', pp, 1)])
                A('dve', lambda e: e.reciprocal(out=s4[:, 2:3], in_=s4[:, 1:2]), [('st4', pp, 1)], [('st4', pp, 2)])
                A('dve', lambda e: e.tensor_scalar(out=h4[:], in0=cq[:], scalar1=s4[:, 2:3], scalar2=None, op0=ALU.mult),
                  [('cqb', pp), ('st4', pp, 2)], [('hs4', pp)])
                yield
                gcol = g_q if wi == 1 else g_kv
                dstT, dkey = (cqnT, 'cqnT') if wi == 1 else (cknT, 'cknT')
                tb = 6 + (i % 2)
                for k in range(4):
                    A('pe', lambda e, k=k, tb=tb: e.transpose(out=ps[tb][:, k * 128:(k + 1) * 128],
                                                              in_=h4[:, k * 128:(k + 1) * 128], identity=ident),
                      [('hs4', pp), 'tab'], [('ps', tb)])
                for k in range(4):
                    if tb == 6:
                        A('act', lambda e, k=k, tb=tb: e.activation(
                            out=dstT[:, k, i * 128:(i + 1) * 128], in_=ps[tb][:, k * 128:(k + 1) * 128], func=AF.Copy,
                            scale=gcol[:, k:k + 1]), [('ps', tb), 'cst'], [(dkey, k)])
                    else:
                        A('dve', lambda e, k=k, tb=tb: e.tensor_scalar(
                            out=dstT[:, k, i * 128:(i + 1) * 128], in0=ps[tb][:, k * 128:(k + 1) * 128],
                            scalar1=gcol[:, k:k + 1], scalar2=None, op0=ALU.mult), [('ps', tb), 'cst'], [(dkey, k)])
            for wi in range(3):
                P.dma('pool', wtbs[wi][:], wtm[wi], writes=[('wtb', wi)])
                run_pipelined(tm_tile(wi, i) for i in range(NTL))
            P.dma('pool', wabb[:], wab, writes=['wabb'])
            for i in range(NTL):
                for k in range(KC):
                    A('pe', lambda e, k=k, i=i: e.matmul(out=ps[4][:, 0:8], lhsT=hnT[:, k, i * 128:(i + 1) * 128],
                                                         rhs=wabb[:, k, :], start=(k == 0), stop=(k == KC - 1)),
                      [('hnT', k), 'wabb'], [('ps', 4)])
                A('act', lambda e: e.copy(out=abt[:], in_=ps[4][:, 0:8]), [('ps', 4)], ['abt'])
                A('dve', lambda e: e.tensor_tensor(out=abw[:, 0:4], in0=abt[:, 0:4], in1=dtb, op=ALU.add),
                  ['abt', 'hvb'], ['abw'])
                A('act', lambda e: e.activation(out=abw[:, 0:4], in_=abw[:, 0:4], func=AF.Exp), ['abw'], ['abw'])
                A('act', lambda e: e.activation(out=abw[:, 0:4], in_=abw[:, 0:4], func=AF.Ln, bias=onec[:, 0:1]),
                  ['abw', 'onec'], ['abw'])
                A('dve', lambda e: e.tensor_tensor(out=gbt[:, 0:4], in0=abw[:, 0:4], in1=negA[:], op=ALU.mult),
                  ['abw', 'negA'], ['gbt0'])
                A('act', lambda e: e.activation(out=gbt[:, 4:8], in_=abt[:, 4:8], func=AF.Sigmoid), ['abt'], ['gbt1'])
                P.dma('sp', gbs[c0 + i * 128: c0 + (i + 1) * 128, :], gbt[:], reads=['gbt0', 'gbt1'],
                      writes=[('gbs', blk * NTL + i)], group='scr')
            def l2_chunk(hh, kind):
                b = cnt['w2'] % 4
                bank = cnt['w2'] % 2
                cnt['w2'] += 1
                M = 64 if kind == 1 else 128
                src, skey = (cknT, 'cknT') if kind == 2 else (cqnT, 'cqnT')
                P.dma('pool', w2b[b][:], w2fm[hh * 3 + kind], writes=[('w2b', b)])
                for k in range(4):
                    A('pe', lambda e, k=k, b=b, bank=bank, M=M, src=src: e.matmul(
                        out=ps[bank][0:M, 0:T], lhsT=w2b[b][:, k, 0:M], rhs=src[:, k, 0:T],
                        start=(k == 0), stop=(k == 3)), [(skey, k), ('w2b', b)], [('ps', bank)])
                if kind == 1:
                    A('act', lambda e, bank=bank: e.activation(out=raw[0:64, :], in_=ps[bank][0:64, 0:T],
                                                               func=AF.Copy, scale=MSC), [('ps', bank)], ['raw'])
                    yield
                    rope64('raw', blk, qpTs[hh, :, c0:c0 + T], ('qpTs', hh, blk))
                else:
                    sb_ = cnt['sT'] % 2
                    cnt['sT'] += 1
                    A('act', lambda e, bank=bank, sb_=sb_, kind=kind: e.activation(
                        out=stgT[sb_][:], in_=ps[bank][:, 0:T], func=AF.Copy, scale=(MSC if kind == 0 else 1.0)),
                      [('ps', bank)], [('stgT', sb_)])
                    dst = qnTs if kind == 0 else knTs
                    P.dma('sp', dst[hh, :, c0:c0 + T], stgT[sb_][:], reads=[('stgT', sb_)],
                          writes=[('qnTs' if kind == 0 else 'knTs', hh, blk)], group='scr')
                return
                yield
            run_pipelined(l2_chunk(hh, kind) for hh in range(H) for kind in range(3))
            P.dma('pool', w2t[:], w2tm, writes=['w2t'])
            for i in range(NTL):
                bank = 2 + (i % 2)
                for k in range(4):
                    A('pe', lambda e, k=k, bank=bank, i=i: e.matmul(
                        out=ps[bank][:, 0:512], lhsT=cknT[:, k, i * 128:(i + 1) * 128], rhs=w2t[:, k, :],
                        start=(k == 0), stop=(k == 3)), [('cknT', k), 'w2t'], [('ps', bank)])
                sb_ = cnt['s'] % 2
                cnt['s'] += 1
                A('act', lambda e, bank=bank, sb_=sb_: e.copy(out=stg[sb_][:], in_=ps[bank][:, 0:512]),
                  [('ps', bank)], [('stg', sb_)])
                P.dma('sp', mvs[c0 + i * 128: c0 + (i + 1) * 128, :], stg[sb_][:], reads=[('stg', sb_)],
                      writes=[('mvs', blk * NTL + i)], group='scr')
        E.reset(mark)

    if 2 in phases:
        G = []
        for hh in range(H):
            g = dict(
                qT=AFa.get([128, 128]), kT=AFa.get([128, 128]), vT=AFa.get([128, 128]), zt=AFa.get([128, 128]),
                gb=AFa.get([128, 8]), kv=AFa.get([128, 256]), gbc=AFa.get([128, 128]), sc=AFa.get([128, 16]),
                Dm=AFa.get([128, 128]), DmT=AFa.get([128, 128]), gam=AFa.get([128, 128]), gamT=AFa.get([128, 128]),
                Am=[AFa.get([128, 128]) for _ in range(2)], AmT=[AFa.get([128, 128]) for _ in range(2)],
                Q=[AFa.get([128, 128]) for _ in range(2)], attnT=AFa.get([128, 128]), rhsM=AFa.get([128, 256]),
                u=AFa.get([128, 128]), wT=AFa.get([128, 128]), vnew=AFa.get([128, 128]), eGr=AFa.get([128, 128]),
                qd=AFa.get([128, 128]), kd=AFa.get([128, 128]), state=AFa.get([128, 128]), yt=AFa.get([128, 128]),
                jk=ABa.get([128, 128]),
            )
            G.append(g)
            A('pool', lambda e, g=g: e.memset(g['state'], 0.0), [], [('state', hh)])
        def gdn_body(n, hh):
            t0 = n * 128
            blk = t0 // T
            g = G[hh]
            K = lambda nm, hh=hh: (nm, hh)
            bX, bY = 2 * hh, 2 * hh + 1
            sc = g['sc']
            P.dma('sp', g['qT'], gT[0, hh, :, t0:t0 + 128], reads=[('gT', 0, hh, blk)], writes=[K('qT')])
            P.dma('sp', g['kT'], gT[1, hh, :, t0:t0 + 128], reads=[('gT', 1, hh, blk)], writes=[K('kT')])
            P.dma('sp', g['vT'], gT[2, hh, :, t0:t0 + 128], reads=[('gT', 2, hh, blk)], writes=[K('vT')])
            P.dma('sp', g['zt'], zs[t0:t0 + 128, hh * 128:(hh + 1) * 128], reads=[('zs', n)], writes=[K('zt')])
            P.dma('sp', g['gb'], gbs[t0:t0 + 128, :], reads=[('gbs', n)], writes=[K('gb')])
            gcol = g['gb'][:, hh:hh + 1]
            bcol = g['gb'][:, 4 + hh:5 + hh]
            A('pe', lambda e, g=g, bX=bX: e.transpose(out=ps[bX][:, 0:128], in_=g['kT'], identity=ident),
              [K('kT'), 'tab'], [('ps', bX)])
            A('pe', lambda e, g=g, bX=bX: e.transpose(out=ps[bX][:, 128:256], in_=g['vT'], identity=ident),
              [K('vT'), 'tab'], [('ps', bX)])
            A('act', lambda e, g=g, bX=bX: e.copy(out=g['kv'], in_=ps[bX][:, 0:256]), [('ps', bX)], [K('kv')])
            A('dve', lambda e, g=g, gcol=gcol: e.tensor_scalar(out=g['gbc'], in0=ones, scalar1=gcol, scalar2=None,
                                                              op0=ALU.mult), [K('gb'), 'tab'], [K('gbc')])
            A('pe', lambda e, g=g, bY=bY: e.matmul(out=ps[bY][:, 0:8], lhsT=Umat, rhs=g['gb'],
                                                   start=True, stop=True), [K('gb'), 'tab'], [('ps', bY)])
            A('pe', lambda e, g=g, bY=bY: e.matmul(out=ps[bY][:, 128:256], lhsT=g['gbc'], rhs=Umat,
                                                   start=True, stop=True), [K('gbc'), 'tab'], [('ps', bY)])
            A('act', lambda e, g=g, bY=bY, hh=hh: e.copy(out=g['sc'][:, 0:1], in_=ps[bY][:, hh:hh + 1]), [('ps', bY)],
              [K('Gc')])
            A('act', lambda e, g=g, bY=bY: e.copy(out=g['sc'][:, 1:2], in_=ps[bY][:, 255:256]), [('ps', bY)],
              [K('gl')])
            A('dve', lambda e, g=g, bY=bY: e.tensor_scalar(out=g['Dm'], in0=ps[bY][:, 128:256],
                                                           scalar1=g['sc'][:, 0:1], scalar2=-1.0,
                                                           op0=ALU.subtract, op1=ALU.mult),
              [('ps', bY), K('Gc')], [K('Dm')])
            A('dve', lambda e, g=g: e.tensor_tensor(out=g['Dm'], in0=g['Dm'], in1=maskI, op=ALU.add),
              [K('Dm'), 'tab'], [K('Dm')])
            A('act', lambda e, g=g: e.activation(out=g['gam'], in_=g['Dm'], func=AF.Exp), [K('Dm')], [K('gam')])
            A('dve', lambda e, g=g, bY=bY: e.tensor_scalar(out=g['DmT'], in0=ps[bY][:, 128:256],
                                                           scalar1=g['sc'][:, 0:1], scalar2=None,
                                                           op0=ALU.subtract), [('ps', bY), K('Gc')], [K('DmT')])
            A('dve', lambda e, g=g: e.tensor_tensor(out=g['DmT'], in0=g['DmT'], in1=maskIT, op=ALU.add),
              [K('DmT'), 'tab'], [K('DmT')])
            A('act', lambda e, g=g: e.activation(out=g['gamT'], in_=g['DmT'], func=AF.Exp), [K('DmT')], [K('gamT')])
            A('act', lambda e, g=g, bY=bY: e.activation(out=g['eGr'], in_=ps[bY][:, 128:256], func=AF.Exp),
              [('ps', bY)], [K('eGr')])
            A('act', lambda e, g=g: e.activation(out=g['sc'][:, 2:3], in_=g['sc'][:, 0:1], func=AF.Exp),
              [K('Gc')], [K('eG')])
            A('dve', lambda e, g=g, bcol=bcol: e.tensor_tensor(out=g['sc'][:, 3:4], in0=g['sc'][:, 2:3], in1=bcol,
                                                              op=ALU.mult), [K('eG'), K('gb')], [K('bE')])
            A('act', lambda e, g=g: e.activation(out=g['sc'][:, 4:5], in_=g['sc'][:, 0:1], func=AF.Exp, scale=-1.0,
                                                 bias=g['sc'][:, 1:2]), [K('Gc'), K('gl')], [K('kdsc')])
            A('act', lambda e, g=g: e.activation(out=g['sc'][:, 5:6], in_=g['sc'][:, 1:2], func=AF.Exp),
              [K('gl')], [K('egl')])
            yield
            A('pe', lambda e, g=g, bX=bX: e.matmul(out=ps[bX][:, 0:128], lhsT=g['kT'], rhs=g['kT'],
                                                   start=True, stop=True), [K('kT')], [('ps', bX)])
            A('pe', lambda e, g=g, bX=bX: e.matmul(out=ps[bX][:, 128:256], lhsT=g['kT'], rhs=g['qT'],
                                                   start=True, stop=True), [K('kT'), K('qT')], [('ps', bX)])
            A('dve', lambda e, g=g, bX=bX, bcol=bcol: e.scalar_tensor_tensor(
                out=g['Am'][0], in0=ps[bX][:, 0:128], scalar=bcol, in1=g['gam'], op0=ALU.mult, op1=ALU.mult),
              [('ps', bX), K('gb'), K('gam')], [K('Am0')])
            A('dve', lambda e, g=g: e.tensor_tensor(out=g['Am'][0], in0=g['Am'][0], in1=strict01, op=ALU.mult),
              [K('Am0'), 'tab'], [K('Am0')])
            A('dve', lambda e, g=g, bX=bX: e.tensor_tensor(out=g['attnT'], in0=ps[bX][:, 128:256], in1=g['gamT'],
                                                           op=ALU.mult), [('ps', bX), K('gamT')], [K('attnT')])
            yield
            A('pe', lambda e, g=g, bY=bY: e.transpose(out=ps[bY][:, 0:128], in_=g['Am'][0], identity=ident),
              [K('Am0'), 'tab'], [('ps', bY)])
            A('act', lambda e, g=g, bY=bY: e.copy(out=g['AmT'][0], in_=ps[bY][:, 0:128]), [('ps', bY)], [K('AmT0')])
            A('dve', lambda e, g=g, bY=bY: e.scalar_tensor_tensor(
                out=g['Q'][0], in0=ps[bY][:, 0:128], scalar=-1.0, in1=ident, op0=ALU.mult, op1=ALU.add),
              [('ps', bY), 'tab'], [K('Q0')])
            for lv in range(1, 7):
                yield
                pa, ca = (lv - 1) % 2, lv % 2
                Ap, ATp = g['Am'][pa], g['AmT'][pa]
                Ac, ATc = g['Am'][ca], g['AmT'][ca]
                Qp, Qc = g['Q'][pa], g['Q'][ca]
                kAp, kATp, kAc, kATc = K('Am%d' % pa), K('AmT%d' % pa), K('Am%d' % ca), K('AmT%d' % ca)
                kQp, kQc = K('Q%d' % pa), K('Q%d' % ca)
                A('pe', lambda e, bX=bX, Ap=Ap, ATp=ATp: e.matmul(out=ps[bX][:, 0:128], lhsT=ATp, rhs=Ap,
                                                                  start=True, stop=True), [kAp, kATp], [('ps', bX)])
                if lv < 6:
                    A('pe', lambda e, bX=bX, Ap=Ap, ATp=ATp: e.matmul(out=ps[bX][:, 128:256], lhsT=Ap, rhs=ATp,
                                                                      start=True, stop=True), [kAp, kATp],
                      [('ps', bX)])
                A('act', lambda e, bX=bX, Ac=Ac: e.copy(out=Ac, in_=ps[bX][:, 0:128]), [('ps', bX)], [kAc])
                if lv < 6:
                    A('dve', lambda e, bX=bX, ATc=ATc: e.tensor_copy(out=ATc, in_=ps[bX][:, 128:256]),
                      [('ps', bX)], [kATc])
                yield
                A('pe', lambda e, bY=bY, Ac=Ac, Qp=Qp: e.matmul(out=ps[bY][:, 0:128], lhsT=Ac, rhs=Qp,
                                                                start=True, stop=True), [kAc, kQp], [('ps', bY)])
                A('dve', lambda e, bY=bY, Qp=Qp, Qc=Qc: e.tensor_tensor(out=Qc, in0=Qp, in1=ps[bY][:, 0:128],
                                                                        op=ALU.add), [('ps', bY), kQp], [kQc])
            Qf, kQf = g['Q'][0], K('Q0')
            yield
            A('dve', lambda e, g=g, bcol=bcol: e.tensor_scalar(out=g['rhsM'][:, 0:128], in0=g['kv'][:, 128:256],
                                                              scalar1=bcol, scalar2=None, op0=ALU.mult),
              [K('kv'), K('gb')], [K('rhsv')])
            A('dve', lambda e, g=g: e.tensor_scalar(out=g['rhsM'][:, 128:256], in0=g['kv'][:, 0:128],
                                                    scalar1=g['sc'][:, 3:4], scalar2=None, op0=ALU.mult),
              [K('kv'), K('bE')], [K('rhsk')])
            A('pe', lambda e, g=g, bX=bX, Qf=Qf: e.matmul(out=ps[bX][:, 0:128], lhsT=Qf, rhs=g['rhsM'][:, 0:128],
                                                          start=True, stop=True), [kQf, K('rhsv')], [('ps', bX)])
            A('pe', lambda e, g=g, bX=bX, Qf=Qf: e.matmul(out=ps[bX][:, 128:256], lhsT=g['rhsM'][:, 128:256], rhs=Qf,
                                                          start=True, stop=True), [kQf, K('rhsk')], [('ps', bX)])
            A('act', lambda e, g=g, bX=bX: e.copy(out=g['u'], in_=ps[bX][:, 0:128]), [('ps', bX)], [K('u')])
            A('dve', lambda e, g=g, bX=bX: e.tensor_copy(out=g['wT'], in_=ps[bX][:, 128:256]), [('ps', bX)],
              [K('wT')])
            yield
            A('pe', lambda e, g=g, bY=bY: e.matmul(out=ps[bY][:, 0:128], lhsT=g['wT'], rhs=g['state'],
                                                   start=True, stop=True), [K('wT'), ('state', hh)], [('ps', bY)])
            A('dve', lambda e, g=g, bY=bY: e.tensor_tensor(out=g['vnew'], in0=g['u'], in1=ps[bY][:, 0:128],
                                                           op=ALU.subtract), [K('u'), ('ps', bY)], [K('vnew')])
            A('dve', lambda e, g=g: e.tensor_tensor(out=g['qd'], in0=g['qT'], in1=g['eGr'], op=ALU.mult),
              [K('qT'), K('eGr')], [K('qd')])
            A('dve', lambda e, g=g: e.tensor_scalar(out=g['kd'], in0=g['kv'][:, 0:128], scalar1=g['sc'][:, 4:5],
                                                    scalar2=None, op0=ALU.mult), [K('kv'), K('kdsc')], [K('kd')])
            yield
            A('pe', lambda e, g=g, bX=bX: e.matmul(out=ps[bX][:, 0:128], lhsT=g['qd'], rhs=g['state'],
                                                   start=True, stop=False), [K('qd'), ('state', hh)], [('ps', bX)])
            A('pe', lambda e, g=g, bX=bX: e.matmul(out=ps[bX][:, 0:128], lhsT=g['attnT'], rhs=g['vnew'],
                                                   start=False, stop=True), [K('attnT'), K('vnew')], [('ps', bX)])
            A('pe', lambda e, g=g, bY=bY: e.matmul(out=ps[bY][:, 128:256], lhsT=g['kd'], rhs=g['vnew'],
                                                   start=True, stop=True), [K('kd'), K('vnew')], [('ps', bY)])
            A('dve', lambda e, g=g, bY=bY: e.scalar_tensor_tensor(
                out=g['state'], in0=g['state'], scalar=g['sc'][:, 5:6], in1=ps[bY][:, 128:256],
                op0=ALU.mult, op1=ALU.add), [('state', hh), K('egl'), ('ps', bY)], [('state', hh)])
            yield
            A('act', lambda e, g=g, bX=bX: e.activation(out=g['jk'], in_=ps[bX][:, 0:128], func=AF.Square,
                                                        accum_out=g['sc'][:, 6:7]), [('ps', bX)],
              [K('ss'), K('jk')])
            A('act', lambda e, g=g: e.activation(out=g['sc'][:, 7:8], in_=g['sc'][:, 6:7], func=AF.Sqrt,
                                                 scale=1.0 / 128, bias=epsc[:, 0:1]), [K('ss'), 'epsc'], [K('sd')])
            A('dve', lambda e, g=g: e.reciprocal(out=g['sc'][:, 8:9], in_=g['sc'][:, 7:8]), [K('sd')], [K('rs')])
            A('dve', lambda e, g=g, bX=bX: e.scalar_tensor_tensor(
                out=g['yt'], in0=ps[bX][:, 0:128], scalar=g['sc'][:, 8:9], in1=gnw, op0=ALU.mult, op1=ALU.mult),
              [('ps', bX), K('rs'), 'hvb'], [K('yt')])
            A('dve', lambda e, g=g: e.tensor_tensor(out=g['yt'], in0=g['yt'], in1=g['zt'], op=ALU.mult),
              [K('yt'), K('zt')], [K('yt')])
            P.dma('pool', y[t0:t0 + 128, ycols[0] + hh * 128: ycols[0] + (hh + 1) * 128], g['yt'], reads=[K('yt')],
                  writes=[(pfx + 'y', 0, n, hh)], group='y')
        for n in range(NT):
            alive = [gdn_body(n, hh) for hh in range(H)]
            while alive:
                for g_ in list(alive):
                    try:
                        next(g_)
                    except StopIteration:
                        alive.remove(g_)
        E.reset(mark)

    if 3 in phases:
        knT = E.bf16([128, S]); kpe = E.bf16([128, S]); Vb = E.bf16([128, NT, 128])
        Srow = [E.f32([128, S]) for _ in range(2)]
        Pb = [E.bf16([128, S]) for _ in range(2)]
        qn = [E.bf16([128, 128]) for _ in range(2)]
        qp = [E.bf16([128, 128]) for _ in range(2)]
        PT = [E.bf16([128, 512]) for _ in range(4)]
        identb = E.bf16([128, 128])
        yo = [E.f32([128, 128]) for _ in range(2)]
        sm = [E.f32([128, 4]) for _ in range(2)]
        tbanks = (2, 3, 6, 7)
        psb = {t: ps[t].bitcast(BF16) for t in tbanks}
        A('dve', lambda e: e.tensor_copy(out=identb, in_=ident), ['tab'], ['identb'])
        P.dma('pool', kpe[0:64, :], kpes, reads=[('kpes', b) for b in range(NB)], writes=['kpe'])
        qi = 0
        gi = 0
        for hh in range(H):
            P.dma('pool', knT, knTs[hh], reads=[('knTs', hh, b) for b in range(NB)], writes=['knT'])
            P.dma('pool', Vb, mvs[:, hh * 128:(hh + 1) * 128].rearrange("(n p) e -> p n e", p=128),
                  reads=[('mvs', n) for n in range(NT)], writes=['Vb'])
            for i in range(NT):
                t0 = i * 128
                L = t0 + 128
                blk = t0 // T
                b = qi % 2
                qi += 1
                Sr = Srow[b]
                Pr = Pb[b]
                P.dma('pool', qn[b], qnTs[hh, :, t0:t0 + 128], reads=[('qnTs', hh, blk)], writes=[('qn', b)])
                P.dma('pool', qp[b][0:64, :], qpTs[hh, :, t0:t0 + 128], reads=[('qpTs', hh, blk)], writes=[('qp', b)])
                nkb = (L + 511) // 512
                for kb in range(nkb):
                    w = min(512, L - kb * 512)
                    bank = kb % 2
                    A('pe', lambda e, b=b, kb=kb, w=w, bank=bank: e.matmul(
                        out=ps[bank][:, 0:w], lhsT=qn[b], rhs=knT[:, kb * 512: kb * 512 + w], start=True, stop=False),
                      [('qn', b), 'knT'], [('ps', bank)])
                    A('pe', lambda e, b=b, kb=kb, w=w, bank=bank: e.matmul(
                        out=ps[bank][:, 0:w], lhsT=qp[b][0:64, :], rhs=kpe[0:64, kb * 512: kb * 512 + w],
                        start=False, stop=True), [('qp', b), 'kpe'], [('ps', bank)])
                    A('act', lambda e, Sr=Sr, kb=kb, w=w, bank=bank: e.copy(out=Sr[:, kb * 512: kb * 512 + w],
                                                                           in_=ps[bank][:, 0:w]),
                      [('ps', bank)], [('Sr', b)])
                A('dve', lambda e, Sr=Sr, L=L: e.tensor_tensor(out=Sr[:, L - 128:L], in0=Sr[:, L - 128:L], in1=maskI,
                                                               op=ALU.add), [('Sr', b), 'tab'], [('Sr', b)])
                A('dve', lambda e, Sr=Sr, L=L, b=b: e.reduce_max(out=sm[b][:, 0:1], in_=Sr[:, 0:L], axis=AX.X),
                  [('Sr', b)], [('mx', b)])
                A('dve', lambda e, b=b: e.tensor_scalar(out=sm[b][:, 1:2], in0=sm[b][:, 0:1], scalar1=-1.0, scalar2=None,
                                                        op0=ALU.mult), [('mx', b)], [('nmx', b)])
                A('act', lambda e, Sr=Sr, Pr=Pr, L=L, b=b: e.activation(out=Pr[:, 0:L], in_=Sr[:, 0:L], func=AF.Exp,
                                                                        bias=sm[b][:, 1:2], accum_out=sm[b][:, 2:3]),
                  [('Sr', b), ('nmx', b)], [('Pb', b), ('rsum', b)])
                ob = 4 + b
                ng = (i + 4) // 4
                grp = []
                for g in range(ng):
                    slot = gi % 4
                    gi += 1
                    grp.append((g, slot, tbanks[slot], list(range(4 * g, min(4 * g + 4, i + 1)))))

                def emit_T(g, slot, tb, js):
                    for jj, j in enumerate(js):
                        A('pe', lambda e, Pr=Pr, j=j, jj=jj, tb=tb: e.transpose(
                            out=psb[tb][:, jj * 128:(jj + 1) * 128], in_=Pr[:, j * 128:(j + 1) * 128], identity=identb),
                          [('Pb', b), 'identb'], [('ps', tb)])
                    w = len(js) * 128
                    if tb in (2, 6):
                        A('act', lambda e, slot=slot, tb=tb, w=w: e.copy(out=PT[slot][:, 0:w], in_=psb[tb][:, 0:w]),
                          [('ps', tb)], [('PT', slot)])
                    else:
                        A('dve', lambda e, slot=slot, tb=tb, w=w: e.tensor_copy(out=PT[slot][:, 0:w], in_=psb[tb][:, 0:w]),
                          [('ps', tb)], [('PT', slot)])

                def emit_M(g, slot, tb, js):
                    for jj, j in enumerate(js):
                        A('pe', lambda e, slot=slot, j=j, jj=jj, ob=ob, i=i: e.matmul(
                            out=ps[ob][:, 0:128], lhsT=PT[slot][:, jj * 128:(jj + 1) * 128], rhs=Vb[:, j, :],
                            start=(j == 0), stop=(j == i)), [('PT', slot), 'Vb'], [('ps', ob)])
                emit_T(*grp[0])
                for g in range(1, ng):
                    emit_T(*grp[g])
                    emit_M(*grp[g - 1])
                emit_M(*grp[ng - 1])
                A('dve', lambda e, b=b: e.reciprocal(out=sm[b][:, 3:4], in_=sm[b][:, 2:3]), [('rsum', b)], [('rinv', b)])
                A('dve', lambda e, b=b, ob=ob: e.tensor_scalar(out=yo[b], in0=ps[ob][:, 0:128], scalar1=sm[b][:, 3:4],
                                                               scalar2=None, op0=ALU.mult),
                  [('ps', ob), ('rinv', b)], [('yo', b)])
                P.dma('sp', y[t0:t0 + 128, ycols[1] + hh * 128: ycols[1] + (hh + 1) * 128], yo[b], reads=[('yo', b)],
                      writes=[(pfx + 'y', 1, i, hh)], group='y')
    ykeys = []
    if 2 in phases:
        ykeys += [(pfx + 'y', 0, n, hh) for n in range(NT) for hh in range(H)]
    if 3 in phases:
        ykeys += [(pfx + 'y', 1, n, hh) for n in range(NT) for hh in range(H)]
    if not own:
        return ykeys
    P.fence('sp', ykeys)
    P.emit()
    return nc


def l0_tabs():
    i = np.arange(128)
    ident = np.eye(128)
    ones = np.ones((128, 128))
    U = (i[:, None] <= i[None, :]).astype(np.float64)
    maskI = np.where(i[:, None] >= i[None, :], 0.0, NEG)
    maskIT = np.where(i[None, :] >= i[:, None], 0.0, NEG)
    strict = (i[:, None] > i[None, :]).astype(np.float64)
    Rm = np.zeros((128, 128))
    for m in range(32):
        Rm[m + 32, m] = -1.0
        Rm[m, m + 32] = 1.0
    return np.ascontiguousarray(np.stack([ident, ones, U, maskI, maskIT, strict, Rm]).astype(np.float32))


def l0_inputs(xb, posb, attn_norm, w_in, gdn_conv, A_log, dt_bias, gdn_norm, q_norm, w_uq, kv_norm, w_ukv, hg):
    D = xb.shape[1]
    KC = D // 128
    H = 4
    hs_ = [4 * hg + i for i in range(H)]
    ck = lambda W, c0, n, kc: np.pad(W[:, c0:c0 + n], ((0, 0), (0, 128 - n))).reshape(kc, 128, 128).transpose(1, 0, 2)
    chunks = [ck(w_in, qkv * 1024 + h * 128, 128, KC) for qkv in range(3) for h in hs_]
    chunks.append(ck(w_in, 5136, 64, KC))
    wfm = np.ascontiguousarray(np.stack(chunks))
    tm = lambda W, c0, kc: W[:, c0:c0 + 512].reshape(kc, 128, 512).transpose(1, 0, 2)
    wtm = np.ascontiguousarray(np.stack([tm(w_in, 3072 + 4 * hg * 128, KC), tm(w_in, 4112, KC), tm(w_in, 4624, KC)]))
    abcols = [4096 + h for h in hs_] + [4104 + h for h in hs_]
    wab = np.ascontiguousarray(w_in[:, abcols].reshape(KC, 128, 8).transpose(1, 0, 2))
    c2 = []
    for h in hs_:
        c2.append(ck(w_uq, h * 192, 128, 4))
        c2.append(ck(w_uq, h * 192 + 128, 64, 4))
        c2.append(ck(w_ukv, h * 256, 128, 4))
    w2fm = np.ascontiguousarray(np.stack(c2))
    vcols = np.concatenate([np.arange(h * 256 + 128, h * 256 + 256) for h in hs_])
    w2tm = np.ascontiguousarray(w_ukv[:, vcols].reshape(4, 128, 512).transpose(1, 0, 2))
    taps = np.zeros((128, 12, 4), np.float32)
    for qkv in range(3):
        for i, h in enumerate(hs_):
            ch0 = qkv * 1024 + h * 128
            taps[:, qkv * 4 + i, :] = gdn_conv[:, ch0:ch0 + 128].T
    invf32 = (10000.0 ** (-np.arange(0, 64, 2, dtype=np.float32) / 64)).astype(np.float32)
    invf = np.zeros((128, 1), np.float32)
    invf[:, 0] = np.tile(invf32, 4)
    consts = np.concatenate([arr_vec(attn_norm), arr_vec(q_norm), arr_vec(kv_norm), taps.reshape(128, 48), invf],
                            axis=1).astype(np.float32)
    hv = np.zeros((3, 128), np.float32)
    hv[0, 0:4] = dt_bias[hs_]
    hv[1, 0:4] = A_log[hs_]
    hv[2, :] = gdn_norm
    return dict(x=np.ascontiguousarray(xb), pos=np.ascontiguousarray(posb.astype(np.int32)),
                consts=np.ascontiguousarray(consts), tabs=l0_tabs(), wfm=wfm, wtm=wtm, wab=wab, w2fm=w2fm, w2tm=w2tm,
                hv=hv)


def emit_select(E, flag, h1s, y1s, hsel, ysel, S, D, KM):
    P = E.P
    fl = E.f32([128, 2])
    P.dma('sp', fl, flag, writes=['selflag'])
    half = S // 2
    lo = [E.f32([128, 2048]) for _ in range(2)]
    hi = [E.f32([128, 2048]) for _ in range(2)]
    P.slots['selst'] = 4
    cnt = 0
    for (src, dst, W) in ((h1s, hsel, D), (y1s, ysel, KM)):
        for q in range((W + 2047) // 2048):
            w = min(2048, W - q * 2048)
            cs = slice(q * 2048, q * 2048 + w)
            b = cnt % 2
            cnt += 1
            P.dma('sp', lo[b][:, 0:w], src[half - 128:half, cs], writes=[('sello', b)])
            P.dma('pool', dst[0:128, cs], lo[b][:, 0:w], reads=[('sello', b)], writes=[('selst', cnt)], group='selst')
            for i in range(half // 128):
                b = cnt % 2
                cnt += 1
                P.dma('sp', lo[b][:, 0:w], src[i * 128:(i + 1) * 128, cs], writes=[('sello', b)])
                P.dma('sp', hi[b][:, 0:w], src[half + i * 128: half + (i + 1) * 128, cs], writes=[('selhi', b)])
                P.add('dve', lambda e, b=b, w=w: e.tensor_scalar(out=lo[b][:, 0:w], in0=lo[b][:, 0:w], scalar1=fl[:, 1:2],
                                                                scalar2=None, op0=ALU.mult),
                      reads=[('sello', b), 'selflag'], writes=[('sello', b)])
                P.add('dve', lambda e, b=b, w=w: e.scalar_tensor_tensor(
                    out=lo[b][:, 0:w], in0=hi[b][:, 0:w], scalar=fl[:, 0:1], in1=lo[b][:, 0:w], op0=ALU.mult,
                    op1=ALU.add), reads=[('sello', b), ('selhi', b), 'selflag'], writes=[('sello', b)])
                P.dma('pool', dst[128 + i * 128: 256 + i * 128, cs], lo[b][:, 0:w], reads=[('sello', b)],
                      writes=[('selst', cnt)], group='selst')


def build_fused(D, S, FF, PD, split_last=True):
    nc = bass.Bass("TRN2", target_bir_lowering=False)
    E = Env(nc)
    dr = lambda n, s, dt=F32, kind="Internal": nc.dram_tensor(n, list(s), dt, kind=kind).ap()
    x = dr("x", [S, D], kind="ExternalInput")
    NTD = S // 2 if split_last else S
    out = dr("out", [NTD, D], kind="ExternalOutput")
    y0s = dr("y0s", [S, 2048])
    h1s = dr("h1s", [S, D])
    y1s = dr("y1s", [S, 4096])
    for hg in range(2):
        build_l0(D, S, E=E, io=dict(x=x, y=y0s), pfx="a%d_" % hg, ycols=(hg * 512, 1024 + hg * 512))
        E.reset()
    build_ffn(D, FF, 2048, PD, S, False, E=E, io=dict(hin=x, ysrc=y0s, out=h1s), halo=False, pfx="b_")
    E.reset()
    for hg in range(2):
        build_ret(D, S, E=E, io=dict(x=h1s, y=y1s), pfx="c%d_" % hg, ycol0=hg * 2048)
        E.reset()
    if split_last:
        flag = dr("flag", [128, 2], kind="ExternalInput")
        hsel = dr("hsel", [NTD + 128, D])
        ysel = dr("ysel", [NTD + 128, 4096])
        emit_select(E, flag, h1s, y1s, hsel, ysel, S, D, 4096)
        E.reset()
        fl2 = E.f32([128, 2])
        E.P.dma('sp', fl2, flag, writes=['cscale'])
        okeys = build_ffn(D, FF, 4096, PD, NTD, True, E=E, io=dict(hin=hsel, ysrc=ysel, out=out), halo=True, pfx="d_",
                          cscale=fl2)
    else:
        okeys = build_ffn(D, FF, 4096, PD, S, True, E=E, io=dict(hin=h1s, ysrc=y1s, out=out), halo=False, pfx="d_")
    E.P.fence('sp', okeys)
    E.P.emit()
    return nc


def _pref(d, pfx, drop=()):
    return {pfx + k: v for k, v in d.items() if k not in drop}


def fused_weights(W):
    m = {}
    dummy_x = np.zeros((1, W['l0_w_in'].shape[0]), np.float32)
    dummy_pos = np.zeros((1,), np.int32)
    for hg in range(2):
        d = l0_inputs(dummy_x, dummy_pos, W['l0_attn_norm'], W['l0_w_in'], W['l0_gdn_conv'], W['l0_gdn_A_log'],
                      W['l0_gdn_dt_bias'], W['l0_gdn_norm'], W['l0_mla_q_norm'], W['l0_mla_w_uq'],
                      W['l0_mla_kv_norm'], W['l0_mla_w_ukv'], hg)
        m.update(_pref(d, "a%d_" % hg, drop=('x', 'pos')))
        d = ret_inputs(dummy_x, dummy_pos, W['l1_attn_norm'], W['l1_w_in'], W['l1_ret_norm'], hg)
        m.update(_pref(d, "c%d_" % hg, drop=('x', 'pos')))
    layers = (
        ("b_", W['l0_w_out'], W['l0_ffn_norm'], W['l0_ffn_w_up'], W['l0_ffn_conv_w'], W['l0_ffn_conv_b'],
         W['l0_ffn_w_down'], W['l0_ple_proj'], W['l0_ple_gate_norm'], W['l0_ple_gate']),
        ("d_", W['l1_w_out'], W['l1_ffn_norm'], W['l1_ffn_w_up'], W['l1_ffn_conv_w'], W['l1_ffn_conv_b'],
         W['l1_ffn_w_down'], W['l1_ple_proj'], W['l1_ple_gate_norm'], W['l1_ple_gate']),
    )
    for pfx, w_out, ffn_norm, w_up, conv_w, conv_b, w_down, ple_proj, gate_norm, ple_gate in layers:
        consts = np.concatenate([arr_vec(ffn_norm), arr_vec(gate_norm)] + [arr_vec(conv_w[j]) for j in range(3)]
                                + [arr_vec(conv_b)], axis=1).astype(np.float32)
        d = dict(consts=np.ascontiguousarray(consts), idn=np.eye(128, dtype=np.float32),
                 fng=np.ascontiguousarray(W['final_norm'].astype(np.float32)))
        d.update(ffn_weights(w_out, w_up, w_down, ple_proj, ple_gate))
        m.update(_pref(d, pfx))
    return m


def fused_acts(xb, pb, posb, half=None):
    m = dict(x=np.ascontiguousarray(xb))
    pos = np.ascontiguousarray(posb.astype(np.int32))
    for pfx in ("a0_", "a1_", "c0_", "c1_"):
        m[pfx + "pos"] = pos
    S = xb.shape[0]
    for i, pfx in enumerate(("b_", "d_")):
        p = pb[i]
        if half is not None and i == 1:
            p = p[half * (S // 2):(half + 1) * (S // 2)]
        m[pfx + "pT"] = np.ascontiguousarray(p.T.reshape(p.shape[1] // 128, 128, -1).transpose(1, 0, 2))
    if half is not None:
        fl = np.zeros((128, 2), np.float32)
        fl[:, 0] = float(half)
        fl[:, 1] = 1.0 - float(half)
        m["flag"] = fl
    return m

_NC_CACHE = {}


def kernel(x, p, positions,
           l0_attn_norm, l0_w_in, l0_gdn_conv, l0_gdn_A_log, l0_gdn_dt_bias, l0_gdn_norm,
           l0_mla_q_norm, l0_mla_w_uq, l0_mla_kv_norm, l0_mla_w_ukv, l0_w_out,
           l0_ffn_norm, l0_ffn_w_up, l0_ffn_conv_w, l0_ffn_conv_b, l0_ffn_w_down,
           l0_ple_proj, l0_ple_gate_norm, l0_ple_gate,
           l1_attn_norm, l1_w_in, l1_ret_norm, l1_w_out,
           l1_ffn_norm, l1_ffn_w_up, l1_ffn_conv_w, l1_ffn_conv_b, l1_ffn_w_down,
           l1_ple_proj, l1_ple_gate_norm, l1_ple_gate,
           final_norm):
    inputs = dict(
        x=x, p=p, positions=positions,
        l0_attn_norm=l0_attn_norm, l0_w_in=l0_w_in, l0_gdn_conv=l0_gdn_conv, l0_gdn_A_log=l0_gdn_A_log,
        l0_gdn_dt_bias=l0_gdn_dt_bias, l0_gdn_norm=l0_gdn_norm, l0_mla_q_norm=l0_mla_q_norm,
        l0_mla_w_uq=l0_mla_w_uq, l0_mla_kv_norm=l0_mla_kv_norm, l0_mla_w_ukv=l0_mla_w_ukv, l0_w_out=l0_w_out,
        l0_ffn_norm=l0_ffn_norm, l0_ffn_w_up=l0_ffn_w_up, l0_ffn_conv_w=l0_ffn_conv_w, l0_ffn_conv_b=l0_ffn_conv_b,
        l0_ffn_w_down=l0_ffn_w_down, l0_ple_proj=l0_ple_proj, l0_ple_gate_norm=l0_ple_gate_norm,
        l0_ple_gate=l0_ple_gate, l1_attn_norm=l1_attn_norm, l1_w_in=l1_w_in, l1_ret_norm=l1_ret_norm,
        l1_w_out=l1_w_out, l1_ffn_norm=l1_ffn_norm, l1_ffn_w_up=l1_ffn_w_up, l1_ffn_conv_w=l1_ffn_conv_w,
        l1_ffn_conv_b=l1_ffn_conv_b, l1_ffn_w_down=l1_ffn_w_down, l1_ple_proj=l1_ple_proj,
        l1_ple_gate_norm=l1_ple_gate_norm, l1_ple_gate=l1_ple_gate, final_norm=final_norm)
    W = {}
    for k, v in inputs.items():
        a = np.asarray(v)
        W[k] = a.astype(np.int32) if k == 'positions' else a.astype(np.float32)
    x, p, positions = W['x'], W['p'], W['positions']
    B, S, D = x.shape
    FF = W['l0_ffn_w_down'].shape[0]
    PD = p.shape[3]
    key = (D, S, FF, PD)
    if key not in _NC_CACHE:
        _NC_CACHE[key] = build_fused(D, S, FF, PD, split_last=True)
    nc = _NC_CACHE[key]
    wts = fused_weights(W)
    in_maps = []
    for c in range(2 * B):
        b, half = c // 2, c % 2
        m = dict(wts)
        m.update(fused_acts(x[b], p[:, b], positions[b], half=half))
        in_maps.append(m)
    res = run_bass_kernel_spmd(nc, in_maps, core_ids=list(range(2 * B)))
    out = np.empty((B, S, D), np.float32)
    for c in range(2 * B):
        b, half = c // 2, c % 2
        out[b, half * (S // 2):(half + 1) * (S // 2)] = np.asarray(res.results[c]["out"], dtype=np.float32)
    return out
```

```python
import contextlib
import numpy as np
import concourse.bass as bass
import concourse.mybir as mybir
from concourse.bass_utils import run_bass_kernel_spmd

F32 = mybir.dt.float32
BF16 = mybir.dt.bfloat16
I32 = mybir.dt.int32
ALU = mybir.AluOpType
AF = mybir.ActivationFunctionType
AX = mybir.AxisListType

ENGS = ('pe', 'act', 'dve', 'pool', 'sp')
GEN = 8192
DGEN = 512


class Prog:
    def __init__(self, nc):
        self.nc = nc
        self.ops = []
        self.last_w = {}
        self.readers = {}
        self.stack = contextlib.ExitStack()
        self.dma_groups = {}
        self.slots = {}
        self.last_dma = {}
        self.lane_of = {}
        self.lane_cnt = {}

    def sb(self, name, shape, dt=F32):
        return self.stack.enter_context(self.nc.sbuf_tensor(name, list(shape), dt))

    def ps(self, name, shape, dt=F32):
        return self.stack.enter_context(self.nc.psum_tensor(name, list(shape), dt))

    def barrier(self):
        lasts = {}
        for oid, op in enumerate(self.ops):
            if op['dma'] is None and op['fn'] is not None:
                lasts[op['eng']] = oid
        dmas = list(self.last_dma.values())
        for e in ENGS:
            self.add(e, None, xdeps=[o for en, o in lasts.items() if en != e] + dmas)
        self.lane_of = {}

    def add(self, eng, fn, reads=(), writes=(), dma=None, xdeps=()):
        oid = len(self.ops)
        is_dma = dma is not None
        deps = set((d, 0) for d in xdeps)
        for k in reads:
            w = self.last_w.get(k)
            if w is not None:
                deps.add((w, 0))
            if isinstance(k, tuple) and k[0] == 'ps':
                for r in self.readers.get(k, ()):
                    if self.ops[r]['eng'] != eng:
                        deps.add((r, 3))
        for k in writes:
            w = self.last_w.get(k)
            if w is not None:
                deps.add((w, 1))
            for r in self.readers.get(k, ()):
                deps.add((r, 2))
        fdeps = set()
        for d, kind in deps:
            o = self.ops[d]
            if o['dma'] is None and not is_dma and o['eng'] == eng:
                if eng == 'pe' or fn is None:
                    continue
            fdeps.add(d)
        for d in fdeps:
            self.ops[d]['sig'] = True
        op = dict(eng=eng, fn=fn, deps=sorted(fdeps), dma=dma, sig=False)
        if is_dma:
            g = self.dma_groups.setdefault(dma, 0)
            self.dma_groups[dma] = g + 1
            ns = self.slots.get(dma, 1)
            slot = g % ns
            lk = (dma, slot, eng == 'pool')
            lane = self.lane_of.get(lk)
            if lane is None:
                lane = ('sw' if eng == 'pool' else 'hw', sum(1 for q in self.lane_of if q[2] == (eng == 'pool')))
                self.lane_of[lk] = lane
            op['dma'] = lane
            op['dma_idx'] = self.lane_cnt.get(lane, 0)
            self.lane_cnt[lane] = op['dma_idx'] + 1
            op['sig'] = True
            prev = self.last_dma.get(lane)
            if prev is not None and prev not in fdeps:
                op['deps'] = sorted(fdeps | {prev})
            self.last_dma[lane] = oid
        self.ops.append(op)
        for k in reads:
            self.readers.setdefault(k, []).append(oid)
        for k in writes:
            self.last_w[k] = oid
            self.readers[k] = []
        return oid

    def dma(self, eng, out, in_, reads=(), writes=(), group=None):
        g = group if group is not None else writes[0]
        return self.add(eng, lambda e: e.dma_start(out=out, in_=in_), reads, writes, dma=g)

    def fence(self, eng, reads):
        return self.add(eng, None, reads=reads)

    def emit(self):
        nc = self.nc
        cnt = {e: 0 for e in ENGS}
        for op in self.ops:
            if op['dma'] is None and op['sig']:
                op['cidx'] = cnt[op['eng']]
                cnt[op['eng']] += 1
        sems = {}

        def sem(key):
            if key not in sems:
                sems[key] = self.stack.enter_context(nc.semaphore('s%d' % len(sems)))
            return sems[key]

        def dep_target(d):
            o = self.ops[d]
            if o['dma'] is not None:
                gi = o['dma_idx']
                return ('d', o['dma'], gi // DGEN), 16 * (gi % DGEN + 1), ('d', o['dma']), gi
            ci = o['cidx']
            return ('e', o['eng'], ci // GEN), (ci % GEN) + 1, ('e', o['eng']), ci

        waited = {e: {} for e in ENGS}
        for op in self.ops:
            e = op['eng']
            ws = {}
            for d in op['deps']:
                skey, val, stream, pos = dep_target(d)
                if waited[e].get(stream, -1) >= pos:
                    continue
                waited[e][stream] = pos
                if ws.get(skey, 0) < val:
                    ws[skey] = val
            op['waits'] = [(sem(k), v) for k, v in ws.items()]
            if op['sig']:
                if op['dma'] is not None:
                    gi = op['dma_idx']
                    op['inc'] = (sem(('d', op['dma'], gi // DGEN)), 16)
                else:
                    op['inc'] = (sem(('e', e, op['cidx'] // GEN)), 1)
        self.nsems = len(sems)
        engobj = dict(pe='tensor', act='scalar', dve='vector', pool='gpsimd', sp='sync')
        with nc.Block() as block:
            for e in ENGS:
                mine = [op for op in self.ops if op['eng'] == e]
                if not mine:
                    continue

                def body(eng, mine=mine):
                    for op in mine:
                        for s, v in op['waits']:
                            eng.wait_ge(s, v)
                        if op['fn'] is None:
                            continue
                        ins = op['fn'](eng)
                        if op['sig']:
                            ins.then_inc(op['inc'][0], op['inc'][1])
                getattr(block, engobj[e])(body)
        self.stack.close()


class Arena:
    def __init__(self, P, name, cols, dt):
        self.t = P.sb(name, [128, cols], dt)
        self.cols = cols
        self.off = 0

    def get(self, shape):
        n = 1
        for d in shape[1:]:
            n *= d
        assert self.off + n <= self.cols, (self.off, n, self.cols)
        ap = self.t[:, self.off:self.off + n]
        self.off += n
        if len(shape) == 3:
            ap = ap.rearrange("p (a b) -> p a b", b=shape[2])
        return ap

    def reset(self, off=0):
        self.off = off


class Env:
    def __init__(self, nc, cols=52000):
        self.nc = nc
        self.P = Prog(nc)
        self.arena = Arena(self.P, "arena", cols, F32)
        self.ps = [self.P.ps("ps%d" % i, [128, 512]) for i in range(8)]

    @staticmethod
    def _n(shape):
        n = 1
        for d in shape[1:]:
            n *= d
        return n

    def f32(self, shape):
        return self.arena.get(list(shape))

    def _cast(self, shape, dt, per):
        n = self._n(shape)
        ap = self.arena.get([128, (n + per - 1) // per]).bitcast(dt)[:, 0:n]
        if len(shape) == 3:
            ap = ap.rearrange("p (a b) -> p a b", b=shape[2])
        return ap

    def bf16(self, shape):
        return self._cast(list(shape), BF16, 2)

    def i32(self, shape):
        return self._cast(list(shape), I32, 1)

    def mark(self):
        return self.arena.off

    def reset(self, to=0):
        self.P.barrier()
        self.arena.reset(to)


EPS = 1e-6


def arr_cols(W, cb):
    K, N = W.shape
    return np.ascontiguousarray(W.reshape(K // 128, 128, N // cb, cb).transpose(2, 1, 0, 3))


def arr_vec(v):
    return np.ascontiguousarray(v.reshape(-1, 128).T)


def build_ffn(D, FF, KM, PD, NT, final, T=512, CB=256, E=None, io=None, halo=True, pfx="", cscale=None):
    own = E is None
    if own:
        E = Env(bass.Bass("TRN2", target_bir_lowering=False))
    nc = E.nc
    io = io or {}
    KC, KMC, FC, PC = D // 128, KM // 128, FF // 128, PD // 128
    NCB = D // CB
    H0 = 128 if halo else 0
    NTOT = NT + H0
    dr = lambda n, s, dt=F32, kind="ExternalInput": nc.dram_tensor(pfx + n, list(s), dt, kind=kind).ap()
    hin = io['hin'] if 'hin' in io else dr("hin", [NTOT, D])
    ysrc = io.get('ysrc')
    yT = None if ysrc is not None else dr("yT", [128, KMC, NTOT])
    pT = dr("pT", [128, PC, NT])
    w_out = dr("w_out", [NCB, 128, KMC, CB])
    w_up = dr("w_up", [2 * FC, 128, KC, 128])
    w_down = dr("w_down", [NCB, 128, FC, CB])
    w_proj = dr("w_proj", [NCB, 128, PC, CB])
    w_gate = dr("w_gate", [NCB, 128, KC, CB])
    NCONST = 2 * KC + 8 * FC
    consts = dr("consts", [128, NCONST])
    idn = dr("idn", [128, 128])
    fng = dr("fng", [D])
    out = io['out'] if 'out' in io else dr("out", [NT, D], kind="ExternalOutput")

    P = E.P
    P.slots['out'] = 4
    NTL = T // 128
    h = [E.f32([128, D]) for i in range(NTL)]
    big = E.bf16([128, max(FC, KMC), T])
    wcol = [E.bf16([128, max(FC, KMC, KC), CB]) for i in range(2)]
    wp = [E.bf16([128, PC, CB]) for i in range(2)]
    hnT = E.bf16([128, KC, T])
    pTs = E.bf16([128, PC, T])
    NWU = 3
    wup = [[E.bf16([128, KC, 128]) for g in range(2)] for i in range(NWU)]
    ub = [E.f32([128, T + 2]) for g in range(2)]
    tmp = [E.f32([128, T]) for g in range(2)]
    gs = E.f32([128, T])
    carry = E.f32([128, 2 * FC, 2])
    cst = E.f32([128, NCONST])
    ident = E.f32([128, 128])
    junk = E.bf16([128, D])
    hs = E.f32([128, D])
    st = E.f32([128, 4])
    sg = E.f32([128, CB])
    fg = E.f32([128, D]) if final else None
    ps = E.ps
    epsc = E.f32([128, 1])
    P.add('dve', lambda e: e.memset(epsc[:], EPS), writes=['epsc'])
    C = dict(D=D, junk=junk, st=st, hs=hs, ident=ident, ps=ps, epsc=epsc)
    g_ffn = cst[:, 0:KC]
    g_gate = cst[:, KC:2 * KC]
    cw = lambda j, c: cst[:, 2 * KC + j * 2 * FC + c: 2 * KC + j * 2 * FC + c + 1]
    cb_ = lambda c: cst[:, 2 * KC + 6 * FC + c: 2 * KC + 6 * FC + c + 1]

    P.dma('sp', cst[:], consts, writes=['consts'])
    P.dma('sp', ident[:], idn, writes=['ident'])
    if final:
        P.dma('sp', fg[:], fng.partition_broadcast(128), writes=['fg'])
    wcnt = [0]

    def lin_tok(xT, xkey, KCx, Wd, ntl, tile_off, consume, extra=None):
        for c in range(NCB):
            b = wcnt[0] % 2
            wcnt[0] += 1
            P.dma('pool', wcol[b][:, 0:KCx, :], Wd[c], writes=[('wcol', b)])
            if extra is not None:
                P.dma('pool', wp[b][:], w_proj[c], writes=[('wp', b)])
            for i in range(ntl):
                bank = 4 + (i % 2)
                t0 = tile_off + i * 128
                for k in range(KCx):
                    P.add('pe', lambda e, k=k, b=b, bank=bank, t0=t0: e.matmul(
                        out=ps[bank][:, 0:CB], lhsT=xT[:, k, t0:t0 + 128], rhs=wcol[b][:, k, :],
                        start=(k == 0), stop=(k == KCx - 1)),
                        reads=[xkey(k), ('wcol', b)], writes=[('ps', bank)])
                if extra is not None:
                    for k in range(PC):
                        P.add('pe', lambda e, k=k, b=b, bank=bank, t0=t0: e.matmul(
                            out=ps[bank][:, CB:2 * CB], lhsT=pTs[:, k, t0:t0 + 128], rhs=wp[b][:, k, :],
                            start=(k == 0), stop=(k == PC - 1)),
                            reads=['pTs', ('wp', b)], writes=[('ps', bank)])
                consume(i, c, bank)

    def add_to_h(i, c, bank):
        P.add('dve', lambda e: e.tensor_tensor(out=h[i][:, c * CB:(c + 1) * CB], in0=h[i][:, c * CB:(c + 1) * CB],
                                               in1=ps[bank][:, 0:CB], op=ALU.add),
              reads=[('ps', bank), ('h', i, c)], writes=[('h', i, c)])

    def ple_consume(i, c, bank):
        P.add('act', lambda e: e.activation(out=sg[:], in_=ps[bank][:, 0:CB], func=AF.Sigmoid),
              reads=[('ps', bank)], writes=['sg'])
        P.add('dve', lambda e: e.tensor_tensor(out=sg[:], in0=sg[:], in1=ps[bank][:, CB:2 * CB], op=ALU.mult),
              reads=['sg', ('ps', bank)], writes=['sg'])
        P.add('dve', lambda e: e.tensor_tensor(out=h[i][:, c * CB:(c + 1) * CB], in0=h[i][:, c * CB:(c + 1) * CB],
                                               in1=sg[:], op=ALU.add),
              reads=['sg', ('h', i, c)], writes=[('h', i, c)])

    hkeys = lambda i: [('h', i, c) for c in range(NCB)]
    P.add('pool', lambda e: e.memset(carry[:], 0.0), writes=['carry'])

    blocks = ([(0, 1, True)] if halo else []) + [(H0 + b * T, NTL, False) for b in range(NT // T)]
    upc = [0]
    for (tok0, ntl, halo) in blocks:
        TT = ntl * 128
        for i in range(ntl):
            P.dma('sp', h[i][:], hin[tok0 + i * 128: tok0 + (i + 1) * 128, :], writes=hkeys(i))
        if ysrc is None:
            P.dma('pool', big[:, 0:KMC, 0:TT], yT[:, :, tok0:tok0 + TT], writes=[('big', k) for k in range(KMC)],
                  group='bigld')
        else:
            W = min(KM, D)
            for i in range(ntl):
                for q in range(KM // W):
                    P.dma('sp', hs[:, 0:W], ysrc[tok0 + i * 128: tok0 + (i + 1) * 128, q * W:(q + 1) * W],
                          writes=['hs'])
                    for kk in range(W // 128):
                        bank = 6 + (kk // 4) % 2
                        off = (kk % 4) * 128
                        kc = q * (W // 128) + kk
                        P.add('pe', lambda e, kk=kk, bank=bank, off=off: e.transpose(
                            out=ps[bank][:, off:off + 128], in_=hs[:, kk * 128:(kk + 1) * 128], identity=ident[:]),
                            reads=['hs', 'ident'], writes=[('ps', bank)])
                        if bank == 6:
                            P.add('act', lambda e, kc=kc, bank=bank, off=off, i=i: e.copy(
                                out=big[:, kc, i * 128:(i + 1) * 128], in_=ps[bank][:, off:off + 128]),
                                reads=[('ps', bank)], writes=[('big', kc)])
                        else:
                            P.add('dve', lambda e, kc=kc, bank=bank, off=off, i=i: e.tensor_copy(
                                out=big[:, kc, i * 128:(i + 1) * 128], in_=ps[bank][:, off:off + 128]),
                                reads=[('ps', bank)], writes=[('big', kc)])
        lin_tok(big, lambda k: ('big', k), KMC, w_out, ntl, 0, add_to_h)
        for i in range(ntl):
            _rms(P, h, i, hkeys, g_ffn, hnT, 'hnT', C)
        for j in range(FC):
            b = upc[0] % NWU
            pb_ = upc[0] % 2
            upc[0] += 1
            for g in range(2):
                P.dma('pool', wup[b][g][:], w_up[j + g * FC], writes=[('wup', b, g)])
            for g in range(2):
                ch = j + g * FC
                bank = (pb_ * 2 + g)
                if halo:
                    c0, n = 126, 2
                else:
                    c0, n = 0, TT
                for k in range(KC):
                    P.add('pe', lambda e, k=k, b=b, g=g, bank=bank, c0=c0, n=n: e.matmul(
                        out=ps[bank][:, 0:n], lhsT=wup[b][g][:, k, :], rhs=hnT[:, k, c0:c0 + n],
                        start=(k == 0), stop=(k == KC - 1)),
                        reads=[('hnT', k), ('wup', b, g)], writes=[('ps', bank)])
                if halo:
                    P.add('act', lambda e, ch=ch, bank=bank: e.copy(out=carry[:, ch, :], in_=ps[bank][:, 0:2]),
                          reads=[('ps', bank), 'carry'], writes=[('carry', ch)])
                    continue
                P.add('dve', lambda e, ch=ch, g=g: e.tensor_copy(out=ub[g][:, 0:2], in_=carry[:, ch, :]),
                      reads=[('carry', ch), 'carry'], writes=[('ub', g, 'c')])
                P.add('act', lambda e, g=g, bank=bank: e.copy(out=ub[g][:, 2:2 + TT], in_=ps[bank][:, 0:TT]),
                      reads=[('ps', bank)], writes=[('ub', g)])
                P.add('dve', lambda e, ch=ch, g=g: e.tensor_copy(out=carry[:, ch, :], in_=ub[g][:, TT:TT + 2]),
                      reads=[('ub', g), 'carry'], writes=[('carry', ch)])
                P.add('dve', lambda e, ch=ch, g=g: e.tensor_scalar(
                    out=tmp[g][:, 0:TT], in0=ub[g][:, 0:TT], scalar1=cw(0, ch), scalar2=None, op0=ALU.mult),
                    reads=[('ub', g), ('ub', g, 'c'), 'consts'], writes=[('tmp', g)])
                for tap in (1, 2):
                    P.add('dve', lambda e, ch=ch, g=g, tap=tap: e.scalar_tensor_tensor(
                        out=tmp[g][:, 0:TT], in0=ub[g][:, tap:tap + TT], scalar=cw(tap, ch), in1=tmp[g][:, 0:TT],
                        op0=ALU.mult, op1=ALU.add),
                        reads=[('ub', g), ('ub', g, 'c'), ('tmp', g), 'consts'], writes=[('tmp', g)])
            if halo:
                continue
            P.add('act', lambda e, j=j: e.activation(out=gs[:, 0:TT], in_=tmp[0][:, 0:TT], func=AF.Silu,
                                                     bias=cb_(j)), reads=[('tmp', 0), 'consts'], writes=['gs'])
            P.add('dve', lambda e, j=j: e.scalar_tensor_tensor(
                out=big[:, j, 0:TT], in0=tmp[1][:, 0:TT], scalar=cb_(j + FC), in1=gs[:, 0:TT],
                op0=ALU.add, op1=ALU.mult), reads=[('tmp', 1), 'gs', 'consts'], writes=[('big', j)])
        if halo:
            if cscale is not None:
                ck = ['carry'] + [('carry', ch) for ch in range(2 * FC)]
                P.add('dve', lambda e: e.tensor_scalar(out=carry[:], in0=carry[:], scalar1=cscale[:, 0:1], scalar2=None,
                                                       op0=ALU.mult), reads=ck + ['cscale'], writes=ck)
            continue
        lin_tok(big, lambda k: ('big', k), FC, w_down, ntl, 0, add_to_h)
        P.dma('pool', pTs[:, :, 0:TT], pT[:, :, tok0 - H0: tok0 - H0 + TT], writes=['pTs'])
        for i in range(ntl):
            _rms(P, h, i, hkeys, g_gate, hnT, 'hnT', C)
        lin_tok(hnT, lambda k: ('hnT', k), KC, w_gate, ntl, 0, ple_consume, extra=True)
        for i in range(ntl):
            orow = tok0 - H0 + i * 128
            if final:
                P.add('act', lambda e, i=i: e.activation(out=junk[:, 0:D], in_=h[i][:], func=AF.Square,
                                                         accum_out=st[:, 0:1]),
                      reads=hkeys(i), writes=['st', 'junk'])
                P.add('act', lambda e: e.activation(out=st[:, 1:2], in_=st[:, 0:1], func=AF.Sqrt, scale=1.0 / D,
                                                    bias=epsc[:, 0:1]), reads=['st', 'epsc'], writes=['st1'])
                P.add('dve', lambda e: e.reciprocal(out=st[:, 2:3], in_=st[:, 1:2]), reads=['st1'], writes=['st2'])
                P.add('dve', lambda e, i=i: e.scalar_tensor_tensor(
                    out=h[i][:], in0=h[i][:], scalar=st[:, 2:3], in1=fg[:], op0=ALU.mult, op1=ALU.mult),
                    reads=hkeys(i) + ['st2', 'fg'], writes=hkeys(i))
            P.dma('sp', out[orow:orow + 128, :], h[i][:], reads=hkeys(i), writes=[(pfx + 'out', orow)], group='out')
    okeys = [(pfx + 'out', r) for r in range(0, NT, 128)]
    if not own:
        return okeys
    P.fence('sp', okeys)
    P.emit()
    return nc


def _rms(P, h, i, hkeys, gcol, dstT, dkey, C):
    D = C['D']
    KC = D // 128
    junk, st, hs, ident, pst = C['junk'], C['st'], C['hs'], C['ident'], C['ps']
    P.add('act', lambda e: e.activation(out=junk[:, 0:D], in_=h[i][:], func=AF.Square, accum_out=st[:, 0:1]),
          reads=hkeys(i), writes=['st', 'junk'])
    P.add('act', lambda e: e.activation(out=st[:, 1:2], in_=st[:, 0:1], func=AF.Sqrt, scale=1.0 / D,
                                        bias=C['epsc'][:, 0:1]), reads=['st', 'epsc'], writes=['st1'])
    P.add('dve', lambda e: e.reciprocal(out=st[:, 2:3], in_=st[:, 1:2]), reads=['st1'], writes=['st2'])
    P.add('dve', lambda e: e.tensor_scalar(out=hs[:, 0:D], in0=h[i][:], scalar1=st[:, 2:3], scalar2=None,
                                           op0=ALU.mult), reads=hkeys(i) + ['st2'], writes=['hs'])
    for k in range(KC):
        bank = 6 + (k // 4) % 2
        off = (k % 4) * 128
        P.add('pe', lambda e, k=k, bank=bank, off=off: e.transpose(
            out=pst[bank][:, off:off + 128], in_=hs[:, k * 128:(k + 1) * 128], identity=ident[:]),
            reads=['hs', 'ident'], writes=[('ps', bank)])
        if bank == 6:
            P.add('act', lambda e, k=k, bank=bank, off=off: e.activation(
                out=dstT[:, k, i * 128:(i + 1) * 128], in_=pst[bank][:, off:off + 128], func=AF.Copy,
                scale=gcol[:, k:k + 1]), reads=[('ps', bank), 'consts'], writes=[(dkey, k)])
        else:
            P.add('dve', lambda e, k=k, bank=bank, off=off: e.tensor_scalar(
                out=dstT[:, k, i * 128:(i + 1) * 128], in0=pst[bank][:, off:off + 128], scalar1=gcol[:, k:k + 1],
                scalar2=None, op0=ALU.mult), reads=[('ps', bank), 'consts'], writes=[(dkey, k)])


def ffn_inputs(hin, y, p, w_out, ffn_norm, w_up, conv_w, conv_b, w_down, w_proj, gate_norm, w_gate, final_norm, CB=256):
    D = hin.shape[1]
    FF = w_down.shape[0]
    KM = y.shape[1]
    KC, FC = D // 128, FF // 128
    consts = np.concatenate([arr_vec(ffn_norm), arr_vec(gate_norm)] + [arr_vec(conv_w[j]) for j in range(3)]
                            + [arr_vec(conv_b)], axis=1).astype(np.float32)
    d = dict(
        hin=np.ascontiguousarray(hin),
        yT=np.ascontiguousarray(y.T.reshape(KM // 128, 128, -1).transpose(1, 0, 2)),
        pT=np.ascontiguousarray(p.T.reshape(p.shape[1] // 128, 128, -1).transpose(1, 0, 2)),
        consts=np.ascontiguousarray(consts), idn=np.eye(128, dtype=np.float32),
        fng=np.ascontiguousarray(final_norm.astype(np.float32)),
    )
    return d


def ffn_weights(w_out, w_up, w_down, w_proj, w_gate, CB=256):
    D = w_out.shape[1]
    KC = D // 128
    return dict(
        w_out=arr_cols(w_out, CB), w_down=arr_cols(w_down, CB), w_proj=arr_cols(w_proj, CB),
        w_gate=arr_cols(w_gate, CB),
        w_up=np.ascontiguousarray(w_up.reshape(KC, 128, -1, 128).transpose(2, 1, 0, 3)),
    )


PI = float(np.pi)
RET_HEADS_ALL = 8


def sincos_block(P, posf, c0, T, invf, pf, tmpf, tmpi, negpi, cosT, sinT, tag):
    for (dst, shift, nm) in ((sinT, PI, 'sin'), (cosT, 1.5 * PI, 'cos')):
        P.add('dve', lambda e, shift=shift: e.tensor_scalar(out=pf[:, 0:T], in0=posf[:, c0:c0 + T], scalar1=invf,
                                                            scalar2=shift, op0=ALU.mult, op1=ALU.add),
              reads=['posf', 'cst'], writes=[tag + 'pf'])
        P.add('dve', lambda e: e.tensor_scalar(out=tmpf[:, 0:T], in0=pf[:, 0:T], scalar1=1.0 / (2 * PI), scalar2=None,
                                               op0=ALU.mult), reads=[tag + 'pf'], writes=[tag + 'tmpf'])
        P.add('dve', lambda e: e.tensor_copy(out=tmpi[:, 0:T], in_=tmpf[:, 0:T]), reads=[tag + 'tmpf'],
              writes=[tag + 'tmpi'])
        P.add('dve', lambda e: e.tensor_copy(out=tmpf[:, 0:T], in_=tmpi[:, 0:T]), reads=[tag + 'tmpi'],
              writes=[tag + 'tmpf'])
        P.add('dve', lambda e: e.scalar_tensor_tensor(out=pf[:, 0:T], in0=tmpf[:, 0:T], scalar=-2 * PI, in1=pf[:, 0:T],
                                                      op0=ALU.mult, op1=ALU.add),
              reads=[tag + 'pf', tag + 'tmpf'], writes=[tag + 'pf'])
        P.add('dve', lambda e: e.tensor_scalar(out=tmpf[:, 0:T], in0=pf[:, 0:T], scalar1=0.0, scalar2=2 * PI,
                                               op0=ALU.is_lt, op1=ALU.mult), reads=[tag + 'pf'], writes=[tag + 'tmpf'])
        P.add('dve', lambda e: e.tensor_tensor(out=pf[:, 0:T], in0=pf[:, 0:T], in1=tmpf[:, 0:T], op=ALU.add),
              reads=[tag + 'pf', tag + 'tmpf'], writes=[tag + 'pf'])
        P.add('act', lambda e, dst=dst: e.activation(out=dst[:, 0:T], in_=pf[:, 0:T], func=AF.Sin, bias=negpi),
              reads=[tag + 'pf', 'negpi'], writes=[tag + nm])


def build_ret(D, S, T=512, E=None, io=None, pfx="", ycol0=0):
    own = E is None
    if own:
        E = Env(bass.Bass("TRN2", target_bir_lowering=False))
    nc = E.nc
    io = io or {}
    KC = D // 128
    H = 4
    DK, DV = 256, 512
    NB = S // T
    NTL = T // 128
    dr = lambda n, s, dt=F32, kind="ExternalInput": nc.dram_tensor(pfx + n, list(s), dt, kind=kind).ap()
    x = io['x'] if 'x' in io else dr("x", [S, D])
    pos = dr("pos", [S], I32)
    consts = dr("consts", [128, KC + 1 + H])
    wfm = dr("wfm", [4 * H, 128, KC, 128])
    wtm = dr("wtm", [2 * H, 128, KC, DV])
    decayT = dr("decayT", [H, 128, 128])
    xi2 = dr("xi2", [H, 128, 2, 128])
    normw = dr("normw", [H * DV])
    idn = dr("idn", [128, 128])
    gam = dr("gam", [128, H])
    y = io['y'] if 'y' in io else dr("y", [S, H * DV], kind="ExternalOutput")
    qTs = dr("qTs", [2, H, 2, 128, S], kind="Internal")
    vs = dr("vs", [2, H, S, DV], kind="Internal")

    P = E.P
    P.slots['scr'] = 12
    P.slots['y'] = 4
    A = lambda eng, fn, r, w: P.add(eng, fn, reads=r, writes=w)
    cst = E.f32([128, KC + 1 + H])
    gm = E.f32([128, H])
    ident = E.f32([128, 128])
    st = E.f32([128, 4])
    epsc = E.f32([128, 1])
    negpi = E.f32([128, 1])
    mark = E.mark()
    h2 = [E.f32([128, D]) for i in range(2)]
    h = [h2[i % 2] for i in range(NTL)]
    hnT = E.bf16([128, KC, T])
    junk = E.bf16([128, max(D, 512)])
    hs = E.f32([128, D])
    posi = E.i32([128, T])
    posf = E.f32([128, T])
    pf = E.f32([128, T]); tmpf = E.f32([128, T]); tmpi = E.i32([128, T])
    cosT = E.f32([128, T]); sinT = E.f32([128, T])
    wfb = [E.bf16([128, KC, 128]) for i in range(4)]
    wtb = [E.bf16([128, KC, DV]) for i in range(3)]
    raws = [[E.f32([128, T]) for i in range(2)] for _ in range(2)]
    rts = [[E.f32([128, T]) for i in range(4)] for _ in range(2)]
    o12s = [[E.f32([128, T]) for i in range(2)] for _ in range(2)]
    stg = [E.f32([128, DV]) for i in range(2)]
    ps = E.ps
    C = dict(D=D, junk=junk, st=st, hs=hs, ident=ident, ps=ps, epsc=epsc)
    g_attn = cst[:, 0:KC]
    invf = cst[:, KC:KC + 1]

    P.dma('sp', cst[:], consts, writes=['cst'])
    P.dma('sp', gm[:], gam, writes=['gm'])
    P.dma('sp', ident[:], idn, writes=['ident'])
    A('dve', lambda e: e.memset(epsc[:], EPS), [], ['epsc'])
    A('dve', lambda e: e.memset(negpi[:], -PI), [], ['negpi'])
    hkeys = lambda i: [('h', i % 2)]
    P.last_w['consts'] = P.last_w['cst']
    P.readers['consts'] = []

    fcnt = [0]
    tcnt = [0]
    scnt = [0]
    for blk in range(NB):
        c0 = blk * T
        if blk == 0:
            for i in range(2):
                P.dma('sp', h[i][:], x[i * 128:(i + 1) * 128, :], writes=hkeys(i))
        for i in range(NTL):
            _rms(P, h, i, hkeys, g_attn, hnT, 'hnT', C)
            gi_ = blk * NTL + i + 2
            if gi_ < NB * NTL:
                P.dma('sp', h[i][:], x[gi_ * 128:(gi_ + 1) * 128, :], writes=hkeys(i))
        P.dma('sp', posi[:], pos[c0:c0 + T].partition_broadcast(128), writes=['posi'])
        A('dve', lambda e: e.tensor_copy(out=posf[:], in_=posi[:]), ['posi'], ['posf'])
        sincos_block(P, posf, 0, T, invf, pf, tmpf, tmpi, negpi[:, 0:1], cosT, sinT, 'r')
        def qk_chunk(qk, hh, blk, c0, pp):
            raw, rt, o12 = raws[pp], rts[pp], o12s[pp]
            for c in range(2):
                ci = qk * 2 * H + hh * 2 + c
                b = fcnt[0] % 4
                bank = fcnt[0] % 2
                fcnt[0] += 1
                P.dma('pool', wfb[b][:], wfm[ci], writes=[('wfb', b)])
                for k in range(KC):
                    A('pe', lambda e, k=k, b=b, bank=bank: e.matmul(
                        out=ps[bank][:, 0:T], lhsT=wfb[b][:, k, :], rhs=hnT[:, k, 0:T],
                        start=(k == 0), stop=(k == KC - 1)), [('hnT', k), ('wfb', b)], [('ps', bank)])
                A('act', lambda e, c=c, bank=bank: e.copy(out=raw[c][:, 0:T], in_=ps[bank][:, 0:T]),
                  [('ps', bank)], [('raw', c, pp)])
            A('dve', lambda e: e.tensor_tensor(out=rt[0][:], in0=raw[0][:], in1=cosT[:], op=ALU.mult),
              [('raw', 0, pp), 'rcos'], [('rt', 0, pp)])
            A('dve', lambda e: e.tensor_tensor(out=rt[1][:], in0=raw[1][:], in1=sinT[:], op=ALU.mult),
              [('raw', 1, pp), 'rsin'], [('rt', 1, pp)])
            A('dve', lambda e: e.tensor_tensor(out=rt[2][:], in0=raw[1][:], in1=cosT[:], op=ALU.mult),
              [('raw', 1, pp), 'rcos'], [('rt', 2, pp)])
            A('dve', lambda e: e.tensor_tensor(out=rt[3][:], in0=raw[0][:], in1=sinT[:], op=ALU.mult),
              [('raw', 0, pp), 'rsin'], [('rt', 3, pp)])
            A('dve', lambda e: e.tensor_tensor(out=o12[0][:], in0=rt[0][:], in1=rt[1][:], op=ALU.subtract),
              [('rt', 0, pp), ('rt', 1, pp)], [('o12', 0, pp)])
            A('dve', lambda e: e.tensor_tensor(out=o12[1][:], in0=rt[2][:], in1=rt[3][:], op=ALU.add),
              [('rt', 2, pp), ('rt', 3, pp)], [('o12', 1, pp)])
            for c in range(2):
                P.dma('sp', qTs[qk, hh, c, :, c0:c0 + T], o12[c][:], reads=[('o12', c, pp)],
                      writes=[('qTs', qk, hh, c, blk)], group='scr')
        for qk in range(2):
            for hh in range(H):
                qk_chunk(qk, hh, blk, c0, (qk * H + hh) % 2)
        for vg in range(2):
            for hh in range(H):
                b = tcnt[0] % 3
                tcnt[0] += 1
                P.dma('pool', wtb[b][:], wtm[vg * H + hh], writes=[('wtb', b)])
                for i in range(NTL):
                    bank = 2 + (i % 2)
                    for k in range(KC):
                        A('pe', lambda e, k=k, b=b, bank=bank, i=i: e.matmul(
                            out=ps[bank][:, 0:DV], lhsT=hnT[:, k, i * 128:(i + 1) * 128], rhs=wtb[b][:, k, :],
                            start=(k == 0), stop=(k == KC - 1)), [('hnT', k), ('wtb', b)], [('ps', bank)])
                    sb_ = scnt[0] % 2
                    scnt[0] += 1
                    A('act', lambda e, bank=bank, sb_=sb_, vg=vg: e.activation(
                        out=stg[sb_][:], in_=ps[bank][:, 0:DV], func=(AF.Silu if vg else AF.Copy)),
                      [('ps', bank)], [('stg', sb_)])
                    P.dma('sp', vs[vg, hh, c0 + i * 128: c0 + (i + 1) * 128, :], stg[sb_][:], reads=[('stg', sb_)],
                          writes=[('vs', vg, hh, blk * NTL + i)], group='scr')

    E.reset(mark)
    dcy = [E.f32([128, 128]) for i in range(H)]
    xit = [E.f32([128, 2, 128]) for i in range(H)]
    nw = [E.f32([128, DV]) for i in range(H)]
    state = [[E.f32([128, DV]) for j in range(2)] for i in range(H)]
    qt = [E.f32([128, 2, 128]) for i in range(H)]
    kt = [E.f32([128, 2, 128]) for i in range(H)]
    vt = [E.f32([128, DV]) for i in range(H)]
    gt = [E.f32([128, DV]) for i in range(H)]
    qd = [E.f32([128, 2, 128]) for i in range(H)]
    kz = [E.f32([128, 256]) for i in range(H)]
    AT = [E.f32([128, 128]) for i in range(H)]
    yt = [E.f32([128, DV]) for i in range(H)]
    sm = [E.f32([128, 8]) for i in range(H)]
    jk = [E.bf16([128, DV]) for i in range(H)]
    jk2 = [E.bf16([128, DV]) for i in range(H)]
    for hh in range(H):
        P.dma('sp', dcy[hh][:], decayT[hh], writes=[('dcy', hh)])
        P.dma('sp', xit[hh][:], xi2[hh], writes=[('xit', hh)])
        P.dma('sp', nw[hh][:], normw[hh * DV:(hh + 1) * DV].partition_broadcast(128), writes=[('nw', hh)])
        for j in range(2):
            A('pool', lambda e, hh=hh, j=j: e.memset(state[hh][j][:], 0.0), [], [('state', hh, j)])
    scr_q = lambda qk, hh: [('qTs', qk, hh, b) for b in range(NB)]
    scr_v = lambda vg, hh: [('vs', vg, hh, b) for b in range(NB)]
    def ret_body(n, hh):
        t0 = n * 128
        blk = t0 // T
        bA, bB = 2 * hh, 2 * hh + 1
        P.dma('sp', qt[hh][:], qTs[0, hh, :, :, t0:t0 + 128].rearrange("c p t -> p c t"),
              reads=[('qTs', 0, hh, 0, blk), ('qTs', 0, hh, 1, blk)], writes=[('qt', hh)])
        P.dma('sp', kt[hh][:], qTs[1, hh, :, :, t0:t0 + 128].rearrange("c p t -> p c t"),
              reads=[('qTs', 1, hh, 0, blk), ('qTs', 1, hh, 1, blk)], writes=[('kt', hh)])
        P.dma('sp', vt[hh][:], vs[0, hh, t0:t0 + 128, :], reads=[('vs', 0, hh, n)], writes=[('vt', hh)])
        P.dma('sp', gt[hh][:], vs[1, hh, t0:t0 + 128, :], reads=[('vs', 1, hh, n)], writes=[('gt', hh)])
        for j in range(2):
            A('pe', lambda e, hh=hh, j=j, bA=bA: e.matmul(
                out=ps[bA][:, 0:128], lhsT=kt[hh][:, j, :], rhs=qt[hh][:, j, :], start=(j == 0), stop=(j == 1)),
              [('kt', hh), ('qt', hh)], [('ps', bA)])
        for j in range(2):
            A('pe', lambda e, hh=hh, j=j, bA=bA: e.transpose(
                out=ps[bA][:, 128 + j * 128: 256 + j * 128], in_=kt[hh][:, j, :], identity=ident[:]),
              [('kt', hh), 'ident'], [('ps', bA)])
        A('dve', lambda e, hh=hh, bA=bA: e.tensor_tensor(out=AT[hh][:], in0=ps[bA][:, 0:128], in1=dcy[hh][:],
                                                         op=ALU.mult), [('ps', bA), ('dcy', hh)], [('AT', hh)])
        A('act', lambda e, hh=hh, bA=bA: e.activation(out=kz[hh][:], in_=ps[bA][:, 128:384], func=AF.Copy,
                                                      scale=cst[:, KC + 1 + hh: KC + 2 + hh]),
          [('ps', bA), 'cst'], [('kz', hh)])
        A('dve', lambda e, hh=hh: e.tensor_tensor(out=qd[hh][:], in0=qt[hh][:], in1=xit[hh][:], op=ALU.mult),
          [('qt', hh), ('xit', hh)], [('qd', hh)])
        yield
        A('pe', lambda e, hh=hh, bB=bB: e.matmul(out=ps[bB][:, 0:DV], lhsT=AT[hh][:], rhs=vt[hh][:],
                                                 start=True, stop=False),
          [('AT', hh), ('vt', hh)], [('ps', bB)])
        for j in range(2):
            A('pe', lambda e, hh=hh, j=j, bB=bB: e.matmul(out=ps[bB][:, 0:DV], lhsT=qd[hh][:, j, :],
                                                          rhs=state[hh][j][:], start=False, stop=(j == 1)),
              [('qd', hh), ('state', hh, j)], [('ps', bB)])
        yield
        s_ = sm[hh]
        A('act', lambda e, hh=hh, bB=bB: e.activation(out=jk[hh][:], in_=ps[bB][:, 0:DV], func=AF.Copy,
                                                      accum_out=sm[hh][:, 0:1]), [('ps', bB)], [('sm0', hh), ('jk', hh)])
        A('act', lambda e, hh=hh, bB=bB: e.activation(out=jk2[hh][:], in_=ps[bB][:, 0:DV], func=AF.Square,
                                                      accum_out=sm[hh][:, 1:2]), [('ps', bB)], [('sm1', hh), ('jk2', hh)])
        A('dve', lambda e, hh=hh: e.tensor_scalar(out=sm[hh][:, 2:3], in0=sm[hh][:, 0:1], scalar1=1.0 / DV,
                                                  scalar2=None, op0=ALU.mult), [('sm0', hh)], [('sm2', hh)])
        A('dve', lambda e, hh=hh: e.tensor_tensor(out=sm[hh][:, 3:4], in0=sm[hh][:, 2:3], in1=sm[hh][:, 2:3],
                                                  op=ALU.mult), [('sm2', hh)], [('sm3', hh)])
        A('dve', lambda e, hh=hh: e.scalar_tensor_tensor(out=sm[hh][:, 4:5], in0=sm[hh][:, 1:2], scalar=1.0 / DV,
                                                         in1=sm[hh][:, 3:4], op0=ALU.mult, op1=ALU.subtract),
          [('sm1', hh), ('sm3', hh)], [('sm4', hh)])
        A('act', lambda e, hh=hh: e.activation(out=sm[hh][:, 5:6], in_=sm[hh][:, 4:5], func=AF.Sqrt,
                                               bias=epsc[:, 0:1]), [('sm4', hh), 'epsc'], [('sm5', hh)])
        A('dve', lambda e, hh=hh: e.reciprocal(out=sm[hh][:, 6:7], in_=sm[hh][:, 5:6]), [('sm5', hh)],
          [('sm6', hh)])
        A('dve', lambda e, hh=hh, bB=bB: e.tensor_scalar(out=yt[hh][:], in0=ps[bB][:, 0:DV],
                                                         scalar1=sm[hh][:, 2:3], scalar2=sm[hh][:, 6:7],
                                                         op0=ALU.subtract, op1=ALU.mult),
          [('ps', bB), ('sm2', hh), ('sm6', hh)], [('yt', hh)])
        A('dve', lambda e, hh=hh: e.tensor_tensor(out=yt[hh][:], in0=yt[hh][:], in1=nw[hh][:], op=ALU.mult),
          [('yt', hh), ('nw', hh)], [('yt', hh)])
        A('dve', lambda e, hh=hh: e.tensor_tensor(out=yt[hh][:], in0=yt[hh][:], in1=gt[hh][:], op=ALU.mult),
          [('yt', hh), ('gt', hh)], [('yt', hh)])
        P.dma('pool', y[t0:t0 + 128, ycol0 + hh * DV: ycol0 + (hh + 1) * DV], yt[hh][:], reads=[('yt', hh)],
              writes=[(pfx + 'y', n, hh)], group='y')
        yield
        for j, bk in ((0, bA), (1, bB)):
            A('pe', lambda e, hh=hh, j=j, bk=bk: e.matmul(out=ps[bk][:, 0:DV], lhsT=kz[hh][:, j * 128:(j + 1) * 128],
                                                          rhs=vt[hh][:], start=True, stop=True),
              [('kz', hh), ('vt', hh)], [('ps', bk)])
            A('dve', lambda e, hh=hh, j=j, bk=bk: e.scalar_tensor_tensor(
                out=state[hh][j][:], in0=state[hh][j][:], scalar=gm[:, hh:hh + 1], in1=ps[bk][:, 0:DV],
                op0=ALU.mult, op1=ALU.add), [('state', hh, j), ('ps', bk), 'gm'], [('state', hh, j)])
    for n in range(S // 128):
        alive = [ret_body(n, hh) for hh in range(H)]
        while alive:
            for g_ in list(alive):
                try:
                    next(g_)
                except StopIteration:
                    alive.remove(g_)
    okeys = [(pfx + 'y', n, hh) for n in range(S // 128) for hh in range(H)]
    if not own:
        return okeys
    P.fence('sp', okeys)
    P.emit()
    return nc


def ret_consts(hg):
    c = 128
    hs_ = np.arange(4) + 4 * hg
    lg = np.log1p(-np.power(2.0, -5.0 - hs_.astype(np.float64)))
    pos = np.arange(c, dtype=np.float64)
    sc = 256 ** -0.5
    dec = np.where(pos[None, :, None] >= pos[None, None, :],
                   np.exp((pos[None, :, None] - pos[None, None, :]) * lg[:, None, None]), 0.0)
    decayT = np.ascontiguousarray(dec.transpose(0, 2, 1) * sc).astype(np.float32)
    xi = np.exp((pos + 1.0)[None, :] * lg[:, None]) * sc
    xi2 = np.ascontiguousarray(np.broadcast_to(xi[:, None, None, :], (4, 128, 2, c))).astype(np.float32)
    zeta = np.exp((c - 1.0 - pos)[None, :] * lg[:, None])
    gam = np.broadcast_to(np.exp(c * lg)[None, :], (128, 4)).astype(np.float32)
    return decayT, xi2, np.ascontiguousarray(zeta.T).astype(np.float32), np.ascontiguousarray(gam)


def ret_inputs(xb, posb, attn_norm, w_in, ret_norm, hg):
    D = xb.shape[0 + 1]
    KC = D // 128
    H, DK, DV = 4, 256, 512
    HA = RET_HEADS_ALL
    decayT, xi2, zetaT, gam = ret_consts(hg)
    invf = (10000.0 ** (-np.arange(0, DK, 2, dtype=np.float32) / DK)).astype(np.float32)
    consts = np.concatenate([arr_vec(attn_norm), invf[:, None], zetaT], axis=1).astype(np.float32)
    qo, ko, vo, go = 0, HA * DK, 2 * HA * DK, 2 * HA * DK + HA * DV
    chunks = []
    for base in (qo, ko):
        for hh in range(H):
            for c in range(2):
                col = base + (4 * hg + hh) * DK + c * 128
                chunks.append(w_in[:, col:col + 128].reshape(KC, 128, 128).transpose(1, 0, 2))
    wfm = np.ascontiguousarray(np.stack(chunks))
    tch = []
    for base in (vo, go):
        for hh in range(H):
            col = base + (4 * hg + hh) * DV
            tch.append(w_in[:, col:col + DV].reshape(KC, 128, DV).transpose(1, 0, 2))
    wtm = np.ascontiguousarray(np.stack(tch))
    normw = np.ascontiguousarray(ret_norm[4 * hg * DV:(4 * hg + 4) * DV])
    return dict(x=np.ascontiguousarray(xb), pos=np.ascontiguousarray(posb.astype(np.int32)), consts=np.ascontiguousarray(consts),
                wfm=wfm, wtm=wtm, decayT=decayT, xi2=xi2, normw=normw, idn=np.eye(128, dtype=np.float32), gam=gam)


NEG = -1.0e30


def build_l0(D, S, T=512, phases=(1, 2, 3), E=None, io=None, pfx="", ycols=(0, 512)):
    own = E is None
    if own:
        E = Env(bass.Bass("TRN2", target_bir_lowering=False))
    nc = E.nc
    io = io or {}
    KC = D // 128
    H = 4
    NB = S // T
    NTL = T // 128
    NT = S // 128
    dr = lambda n, s, dt=F32, kind="ExternalInput": nc.dram_tensor(pfx + n, list(s), dt, kind=kind).ap()
    x = io['x'] if 'x' in io else dr("x", [S, D])
    pos = dr("pos", [S], I32)
    NCST = KC + 4 + 4 + 48 + 1
    consts = dr("consts", [128, NCST])
    tabs = dr("tabs", [7, 128, 128])
    wfm = dr("wfm", [13, 128, KC, 128])
    wtm = dr("wtm", [3, 128, KC, 512])
    wab = dr("wab", [128, KC, 8])
    w2fm = dr("w2fm", [12, 128, 4, 128])
    w2tm = dr("w2tm", [128, 4, 512])
    hv = dr("hv", [3, 128])
    y = io['y'] if 'y' in io else dr("y", [S, 1024], kind="ExternalOutput")
    gT = dr("gT", [3, H, 128, S], kind="Internal")
    kpes = dr("kpes", [64, S], kind="Internal")
    zs = dr("zs", [S, 512], kind="Internal")
    gbs = dr("gbs", [S, 8], kind="Internal")
    qnTs = dr("qnTs", [H, 128, S], kind="Internal")
    qpTs = dr("qpTs", [H, 64, S], kind="Internal")
    knTs = dr("knTs", [H, 128, S], kind="Internal")
    mvs = dr("mvs", [S, 512], kind="Internal")

    P = E.P
    P.slots['scr'] = 12
    P.slots['y'] = 4
    A = lambda eng, fn, r, w: P.add(eng, fn, reads=r, writes=w)
    cst = E.f32([128, NCST])
    tab = E.f32([128, 7, 128])
    hvb = E.f32([128, 3, 128])
    st = E.f32([128, 4])
    epsc = E.f32([128, 1])
    negpi = E.f32([128, 1])
    onec = E.f32([128, 1])
    negA = E.f32([128, 4])
    mark = E.mark()

    class _AL:
        def __init__(self, fn):
            self.get = fn

        def reset(self):
            pass
    AFa = _AL(E.f32)
    ABa = _AL(E.bf16)
    ps = E.ps
    ident, ones, Umat, maskI, maskIT, strict01, Rm = [tab[:, i, :] for i in range(7)]
    g_attn = cst[:, 0:KC]
    g_q = cst[:, KC:KC + 4]
    g_kv = cst[:, KC + 4:KC + 8]
    ctap = lambda ci, j: cst[:, KC + 8 + ci * 4 + j: KC + 8 + ci * 4 + j + 1]
    invf = cst[:, KC + 56:KC + 57]
    dtb = hvb[:, 0, 0:4]
    gnw = hvb[:, 2, :]

    P.dma('sp', cst[:], consts, writes=['cst'])
    P.dma('sp', tab[:], tabs.rearrange("n p c -> p n c"), writes=['tab'])
    P.dma('sp', hvb[:], hv.partition_broadcast(128), writes=['hvb'])
    A('dve', lambda e: e.memset(epsc[:], EPS), [], ['epsc'])
    A('dve', lambda e: e.memset(negpi[:], -PI), [], ['negpi'])
    A('dve', lambda e: e.memset(onec[:], 1.0), [], ['onec'])
    A('act', lambda e: e.activation(out=negA[:], in_=hvb[:, 1, 0:4], func=AF.Exp), ['hvb'], ['negA0'])
    A('dve', lambda e: e.tensor_scalar(out=negA[:], in0=negA[:], scalar1=-1.0, scalar2=None, op0=ALU.mult),
      ['negA0'], ['negA'])
    P.last_w['consts'] = P.last_w['cst']
    P.readers['consts'] = []
    P.last_w['ident'] = P.last_w['tab']
    P.readers['ident'] = []

    if 1 in phases:
        h2 = [AFa.get([128, D]) for _ in range(2)]
        hbuf = [h2[i % 2] for i in range(NTL)]
        hs = AFa.get([128, max(D, 512)])
        junk = ABa.get([128, max(D, 512)])
        hnT = ABa.get([128, KC, T])
        cqnT = ABa.get([128, 4, T])
        cknT = ABa.get([128, 4, T])
        wfb = [ABa.get([128, KC, 128]) for _ in range(4)]
        wtbs = [ABa.get([128, KC, 512]) for _ in range(3)]
        wabb = ABa.get([128, KC, 8])
        w2b = [ABa.get([128, 4, 128]) for _ in range(4)]
        w2t = ABa.get([128, 4, 512])
        posi = E.i32([128, T])
        tmpi = E.i32([128, T])
        posf = AFa.get([128, T]); pf = AFa.get([128, T]); tmpf = AFa.get([128, T])
        cosF = AFa.get([128, T]); sinF = AFa.get([128, T])
        ubs = [AFa.get([128, T + 3]) for _ in range(2)]; tmps = [AFa.get([128, T]) for _ in range(2)]
        css = [AFa.get([128, T]) for _ in range(2)]; sqs = [AFa.get([128, T]) for _ in range(2)]
        rns = [AFa.get([128, T]) for _ in range(2)]
        raw = AFa.get([128, T]); r1 = AFa.get([128, T]); r2 = AFa.get([128, T])
        stgT = [AFa.get([128, T]) for _ in range(2)]
        stg = [AFa.get([128, 512]) for _ in range(2)]
        cqb = [AFa.get([128, 512]) for _ in range(2)]
        hs4 = [AFa.get([128, 512]) for _ in range(2)]
        st4 = [AFa.get([128, 4]) for _ in range(2)]
        abt = AFa.get([128, 8]); abw = AFa.get([128, 8]); gbt = AFa.get([128, 8])
        carry = AFa.get([128, 12, 3])
        C16 = dict(D=D, junk=junk, st=st, hs=hs, ident=ident, ps=ps, epsc=epsc)
        C4 = dict(D=512, junk=junk, st=st, hs=hs, ident=ident, ps=ps, epsc=epsc)
        hkeys = lambda i: [('h', i % 2)]
        cqkeys = lambda i: [('cqb', i % 2)]
        A('pool', lambda e: e.memset(carry[:], 0.0), [], ['carry'])
        cnt = dict(f=0, s=0, sT=0, c=0, w2=0)
        QSC = 128 ** -0.5
        MSC = 192 ** -0.5

        def rope64(src_key, blk, dst, dkey):
            A('pe', lambda e: e.matmul(out=ps[5][0:64, 0:T], lhsT=Rm[0:64, 0:64], rhs=raw[0:64, 0:T],
                                       start=True, stop=True), [src_key, 'tab'], [('ps', 5)])
            A('dve', lambda e: e.tensor_tensor(out=r1[0:64, :], in0=raw[0:64, :], in1=cosF[0:64, :], op=ALU.mult),
              [src_key, 'rcos'], ['r1'])
            A('dve', lambda e: e.tensor_tensor(out=r2[0:64, :], in0=ps[5][0:64, 0:T], in1=sinF[0:64, :], op=ALU.mult),
              [('ps', 5), 'rsin'], ['r2'])
            A('dve', lambda e: e.tensor_tensor(out=r1[0:64, :], in0=r1[0:64, :], in1=r2[0:64, :], op=ALU.add),
              ['r1', 'r2'], ['r1'])
            P.dma('sp', dst, r1[0:64, :], reads=['r1'], writes=[dkey], group='scr')

        for blk in range(NB):
            c0 = blk * T
            if blk == 0:
                for i in range(2):
                    P.dma('sp', hbuf[i][:], x[i * 128:(i + 1) * 128, :], writes=hkeys(i))
            for i in range(NTL):
                _rms(P, hbuf, i, hkeys, g_attn, hnT, 'hnT', C16)
                gi_ = blk * NTL + i + 2
                if gi_ < NB * NTL:
                    P.dma('sp', hbuf[i][:], x[gi_ * 128:(gi_ + 1) * 128, :], writes=hkeys(i))
            P.dma('sp', posi[:], pos[c0:c0 + T].partition_broadcast(128), writes=['posi'])
            A('dve', lambda e: e.tensor_copy(out=posf[:], in_=posi[:]), ['posi'], ['posf'])
            sincos_block(P, posf, 0, T, invf, pf, tmpf, tmpi, negpi[:, 0:1], cosF, sinF, 'r')
            def fm_chunk(ci, blk, c0, pp):
                ub, tmp, cs, sq, rn = ubs[pp], tmps[pp], css[pp], sqs[pp], rns[pp]
                b = cnt['f'] % 4
                bank = cnt['f'] % 2
                cnt['f'] += 1
                M = 64 if ci == 12 else 128
                P.dma('pool', wfb[b][:], wfm[ci], writes=[('wfb', b)])
                for k in range(KC):
                    A('pe', lambda e, k=k, b=b, bank=bank, M=M: e.matmul(
                        out=ps[bank][0:M, 0:T], lhsT=wfb[b][:, k, 0:M], rhs=hnT[:, k, 0:T],
                        start=(k == 0), stop=(k == KC - 1)), [('hnT', k), ('wfb', b)], [('ps', bank)])
                if ci == 12:
                    A('act', lambda e, bank=bank: e.copy(out=raw[0:64, :], in_=ps[bank][0:64, 0:T]),
                      [('ps', bank)], ['raw'])
                    yield
                    rope64('raw', blk, kpes[:, c0:c0 + T], ('kpes', blk))
                    return
                qkv, hh = ci // 4, ci % 4
                A('dve', lambda e, ci=ci: e.tensor_copy(out=ub[:, 0:3], in_=carry[:, ci, :]),
                  [('carry', ci), 'carry'], [('ubc', pp)])
                A('act', lambda e, bank=bank: e.copy(out=ub[:, 3:3 + T], in_=ps[bank][:, 0:T]), [('ps', bank)], [('ub', pp)])
                A('dve', lambda e, ci=ci: e.tensor_copy(out=carry[:, ci, :], in_=ub[:, T:T + 3]),
                  [('ub', pp), 'carry'], [('carry', ci)])
                A('dve', lambda e, ci=ci: e.tensor_scalar(out=tmp[:], in0=ub[:, 0:T], scalar1=ctap(ci, 0),
                                                          scalar2=None, op0=ALU.mult), [('ub', pp), ('ubc', pp), 'cst'], [('tmp', pp)])
                for j in (1, 2, 3):
                    A('dve', lambda e, ci=ci, j=j: e.scalar_tensor_tensor(
                        out=tmp[:], in0=ub[:, j:j + T], scalar=ctap(ci, j), in1=tmp[:], op0=ALU.mult, op1=ALU.add),
                      [('ub', pp), ('ubc', pp), ('tmp', pp), 'cst'], [('tmp', pp)])
                A('act', lambda e: e.activation(out=cs[:], in_=tmp[:], func=AF.Silu), [('tmp', pp)], [('cs', pp)])
                if qkv == 2:
                    P.dma('sp', gT[2, hh, :, c0:c0 + T], cs[:], reads=[('cs', pp)], writes=[('gT', 2, hh, blk)], group='scr')
                    return
                A('act', lambda e: e.activation(out=sq[:], in_=cs[:], func=AF.Square), [('cs', pp)], [('sq', pp)])
                yield
                A('pe', lambda e: e.matmul(out=ps[4][:, 0:T], lhsT=ones, rhs=sq[:], start=True, stop=True),
                  [('sq', pp), 'tab'], [('ps', 4)])
                A('act', lambda e: e.activation(out=rn[:], in_=ps[4][:, 0:T], func=AF.Sqrt, bias=epsc[:, 0:1]),
                  [('ps', 4), 'epsc'], [('rn', pp)])
                A('dve', lambda e: e.reciprocal(out=rn[:], in_=rn[:]), [('rn', pp)], [('rn', pp)])
                sb_ = cnt['sT'] % 2
                cnt['sT'] += 1
                A('dve', lambda e, sb_=sb_, qkv=qkv: e.scalar_tensor_tensor(
                    out=stgT[sb_][:], in0=cs[:], scalar=(QSC if qkv == 0 else 1.0), in1=rn[:],
                    op0=ALU.mult, op1=ALU.mult), [('cs', pp), ('rn', pp)], [('stgT', sb_)])
                P.dma('sp', gT[qkv, hh, :, c0:c0 + T], stgT[sb_][:], reads=[('stgT', sb_)],
                      writes=[('gT', qkv, hh, blk)], group='scr')
            def run_pipelined(gens):
                prev = None
                for g_ in gens:
                    try:
                        next(g_)
                    except StopIteration:
                        g_ = None
                    if prev is not None:
                        for _ in prev:
                            pass
                    prev = g_
                if prev is not None:
                    for _ in prev:
                        pass
            run_pipelined(fm_chunk(ci, blk, c0, ci % 2) for ci in range(13))
            def tm_tile(wi, i):
                wtb = wtbs[wi]
                bank = 2 + (i % 2)
                for k in range(KC):
                    A('pe', lambda e, k=k, bank=bank, i=i, wtb=wtb: e.matmul(
                        out=ps[bank][:, 0:512], lhsT=hnT[:, k, i * 128:(i + 1) * 128], rhs=wtb[:, k, :],
                        start=(k == 0), stop=(k == KC - 1)), [('hnT', k), ('wtb', wi)], [('ps', bank)])
                if wi == 0:
                    sb_ = cnt['s'] % 2
                    cnt['s'] += 1
                    A('act', lambda e, bank=bank, sb_=sb_: e.activation(out=stg[sb_][:], in_=ps[bank][:, 0:512],
                                                                        func=AF.Silu), [('ps', bank)], [('stg', sb_)])
                    P.dma('sp', zs[c0 + i * 128: c0 + (i + 1) * 128, :], stg[sb_][:], reads=[('stg', sb_)],
                          writes=[('zs', blk * NTL + i)], group='scr')
                    return
                pp = cnt['c'] % 2
                cnt['c'] += 1
                cq, s4, h4 = cqb[pp], st4[pp], hs4[pp]
                A('act', lambda e, bank=bank: e.copy(out=cq[:], in_=ps[bank][:, 0:512]), [('ps', bank)], [('cqb', pp)])
                A('act', lambda e: e.activation(out=junk[:, 0:512], in_=cq[:], func=AF.Square, accum_out=s4[:, 0:1]),
                  [('cqb', pp)], [('st4', pp, 0), 'junk'])
                A('act', lambda e: e.activation(out=s4[:, 1:2], in_=s4[:, 0:1], func=AF.Sqrt, scale=1.0 / 512,
                                                bias=epsc[:, 0:1]), [('st4', pp, 0), 'epsc'], [('st4', pp, 1)])
                A('dve', lambda e: e.reciprocal(out=s4[:, 2:3], in_=s4[:, 1:2]), [('st4', pp, 1)], [('st4', pp, 2)])
                A('dve', lambda e: e.tensor_scalar(out=h4[:], in0=cq[:], scalar1=s4[:, 2:3], scalar2=None, op0=ALU.mult),
                  [('cqb', pp), ('st4', pp, 2)], [('hs4', pp)])
                yield
                gcol = g_q if wi == 1 else g_kv
                dstT, dkey = (cqnT, 'cqnT') if wi == 1 else (cknT, 'cknT')
                tb = 6 + (i % 2)
                for k in range(4):
                    A('pe', lambda e, k=k, tb=tb: e.transpose(out=ps[tb][:, k * 128:(k + 1) * 128],
                                                              in_=h4[:, k * 128:(k + 1) * 128], identity=ident),
                      [('hs4', pp), 'tab'], [('ps', tb)])
                for k in range(4):
                    if tb == 6:
                        A('act', lambda e, k=k, tb=tb: e.activation(
                            out=dstT[:, k, i * 128:(i + 1) * 128], in_=ps[tb][:, k * 128:(k + 1) * 128], func=AF.Copy,
                            scale=gcol[:, k:k + 1]), [('ps', tb), 'cst'], [(dkey, k)])
                    else:
                        A('dve', lambda e, k=k, tb=tb: e.tensor_scalar(
                            out=dstT[:, k, i * 128:(i + 1) * 128], in0=ps[tb][:, k * 128:(k + 1) * 128],
                            scalar1=gcol[:, k:k + 1], scalar2=None, op0=ALU.mult), [('ps', tb), 'cst'], [(dkey, k)])
            for wi in range(3):
                P.dma('pool', wtbs[wi][:], wtm[wi], writes=[('wtb', wi)])
                run_pipelined(tm_tile(wi, i) for i in range(NTL))
            P.dma('pool', wabb[:], wab, writes=['wabb'])
            for i in range(NTL):
                for k in range(KC):
                    A('pe', lambda e, k=k, i=i: e.matmul(out=ps[4][:, 0:8], lhsT=hnT[:, k, i * 128:(i + 1) * 128],
                                                         rhs=wabb[:, k, :], start=(k == 0), stop=(k == KC - 1)),
                      [('hnT', k), 'wabb'], [('ps', 4)])
                A('act', lambda e: e.copy(out=abt[:], in_=ps[4][:, 0:8]), [('ps', 4)], ['abt'])
                A('dve', lambda e: e.tensor_tensor(out=abw[:, 0:4], in0=abt[:, 0:4], in1=dtb, op=ALU.add),
                  ['abt', 'hvb'], ['abw'])
                A('act', lambda e: e.activation(out=abw[:, 0:4], in_=abw[:, 0:4], func=AF.Exp), ['abw'], ['abw'])
                A('act', lambda e: e.activation(out=abw[:, 0:4], in_=abw[:, 0:4], func=AF.Ln, bias=onec[:, 0:1]),
                  ['abw', 'onec'], ['abw'])
                A('dve', lambda e: e.tensor_tensor(out=gbt[:, 0:4], in0=abw[:, 0:4], in1=negA[:], op=ALU.mult),
                  ['abw', 'negA'], ['gbt0'])
                A('act', lambda e: e.activation(out=gbt[:, 4:8], in_=abt[:, 4:8], func=AF.Sigmoid), ['abt'], ['gbt1'])
                P.dma('sp', gbs[c0 + i * 128: c0 + (i + 1) * 128, :], gbt[:], reads=['gbt0', 'gbt1'],
                      writes=[('gbs', blk * NTL + i)], group='scr')
            def l2_chunk(hh, kind):
                b = cnt['w2'] % 4
                bank = cnt['w2'] % 2
                cnt['w2'] += 1
                M = 64 if kind == 1 else 128
                src, skey = (cknT, 'cknT') if kind == 2 else (cqnT, 'cqnT')
                P.dma('pool', w2b[b][:], w2fm[hh * 3 + kind], writes=[('w2b', b)])
                for k in range(4):
                    A('pe', lambda e, k=k, b=b, bank=bank, M=M, src=src: e.matmul(
                        out=ps[bank][0:M, 0:T], lhsT=w2b[b][:, k, 0:M], rhs=src[:, k, 0:T],
                        start=(k == 0), stop=(k == 3)), [(skey, k), ('w2b', b)], [('ps', bank)])
                if kind == 1:
                    A('act', lambda e, bank=bank: e.activation(out=raw[0:64, :], in_=ps[bank][0:64, 0:T],
                                                               func=AF.Copy, scale=MSC), [('ps', bank)], ['raw'])
                    yield
                    rope64('raw', blk, qpTs[hh, :, c0:c0 + T], ('qpTs', hh, blk))
                else:
                    sb_ = cnt['sT'] % 2
                    cnt['sT'] += 1
                    A('act', lambda e, bank=bank, sb_=sb_, kind=kind: e.activation(
                        out=stgT[sb_][:], in_=ps[bank][:, 0:T], func=AF.Copy, scale=(MSC if kind == 0 else 1.0)),
                      [('ps', bank)], [('stgT', sb_)])
                    dst = qnTs if kind == 0 else knTs
                    P.dma('sp', dst[hh, :, c0:c0 + T], stgT[sb_][:], reads=[('stgT', sb_)],
                          writes=[('qnTs' if kind == 0 else 'knTs', hh, blk)], group='scr')
                return
                yield
            run_pipelined(l2_chunk(hh, kind) for hh in range(H) for kind in range(3))
            P.dma('pool', w2t[:], w2tm, writes=['w2t'])
            for i in range(NTL):
                bank = 2 + (i % 2)
                for k in range(4):
                    A('pe', lambda e, k=k, bank=bank, i=i: e.matmul(
                        out=ps[bank][:, 0:512], lhsT=cknT[:, k, i * 128:(i + 1) * 128], rhs=w2t[:, k, :],
                        start=(k == 0), stop=(k == 3)), [('cknT', k), 'w2t'], [('ps', bank)])
                sb_ = cnt['s'] % 2
                cnt['s'] += 1
                A('act', lambda e, bank=bank, sb_=sb_: e.copy(out=stg[sb_][:], in_=ps[bank][:, 0:512]),
                  [('ps', bank)], [('stg', sb_)])
                P.dma('sp', mvs[c0 + i * 128: c0 + (i + 1) * 128, :], stg[sb_][:], reads=[('stg', sb_)],
                      writes=[('mvs', blk * NTL + i)], group='scr')
        E.reset(mark)

    if 2 in phases:
        G = []
        for hh in range(H):
            g = dict(
                qT=AFa.get([128, 128]), kT=AFa.get([128, 128]), vT=AFa.get([128, 128]), zt=AFa.get([128, 128]),
                gb=AFa.get([128, 8]), kv=AFa.get([128, 256]), gbc=AFa.get([128, 128]), sc=AFa.get([128, 16]),
                Dm=AFa.get([128, 128]), DmT=AFa.get([128, 128]), gam=AFa.get([128, 128]), gamT=AFa.get([128, 128]),
                Am=[AFa.get([128, 128]) for _ in range(2)], AmT=[AFa.get([128, 128]) for _ in range(2)],
                Q=[AFa.get([128, 128]) for _ in range(2)], attnT=AFa.get([128, 128]), rhsM=AFa.get([128, 256]),
                u=AFa.get([128, 128]), wT=AFa.get([128, 128]), vnew=AFa.get([128, 128]), eGr=AFa.get([128, 128]),
                qd=AFa.get([128, 128]), kd=AFa.get([128, 128]), state=AFa.get([128, 128]), yt=AFa.get([128, 128]),
                jk=ABa.get([128, 128]),
            )
            G.append(g)
            A('pool', lambda e, g=g: e.memset(g['state'], 0.0), [], [('state', hh)])
        def gdn_body(n, hh):
            t0 = n * 128
            blk = t0 // T
            g = G[hh]
            K = lambda nm, hh=hh: (nm, hh)
            bX, bY = 2 * hh, 2 * hh + 1
            sc = g['sc']
            P.dma('sp', g['qT'], gT[0, hh, :, t0:t0 + 128], reads=[('gT', 0, hh, blk)], writes=[K('qT')])
            P.dma('sp', g['kT'], gT[1, hh, :, t0:t0 + 128], reads=[('gT', 1, hh, blk)], writes=[K('kT')])
            P.dma('sp', g['vT'], gT[2, hh, :, t0:t0 + 128], reads=[('gT', 2, hh, blk)], writes=[K('vT')])
            P.dma('sp', g['zt'], zs[t0:t0 + 128, hh * 128:(hh + 1) * 128], reads=[('zs', n)], writes=[K('zt')])
            P.dma('sp', g['gb'], gbs[t0:t0 + 128, :], reads=[('gbs', n)], writes=[K('gb')])
            gcol = g['gb'][:, hh:hh + 1]
            bcol = g['gb'][:, 4 + hh:5 + hh]
            A('pe', lambda e, g=g, bX=bX: e.transpose(out=ps[bX][:, 0:128], in_=g['kT'], identity=ident),
              [K('kT'), 'tab'], [('ps', bX)])
            A('pe', lambda e, g=g, bX=bX: e.transpose(out=ps[bX][:, 128:256], in_=g['vT'], identity=ident),
              [K('vT'), 'tab'], [('ps', bX)])
            A('act', lambda e, g=g, bX=bX: e.copy(out=g['kv'], in_=ps[bX][:, 0:256]), [('ps', bX)], [K('kv')])
            A('dve', lambda e, g=g, gcol=gcol: e.tensor_scalar(out=g['gbc'], in0=ones, scalar1=gcol, scalar2=None,
                                                              op0=ALU.mult), [K('gb'), 'tab'], [K('gbc')])
            A('pe', lambda e, g=g, bY=bY: e.matmul(out=ps[bY][:, 0:8], lhsT=Umat, rhs=g['gb'],
                                                   start=True, stop=True), [K('gb'), 'tab'], [('ps', bY)])
            A('pe', lambda e, g=g, bY=bY: e.matmul(out=ps[bY][:, 128:256], lhsT=g['gbc'], rhs=Umat,
                                                   start=True, stop=True), [K('gbc'), 'tab'], [('ps', bY)])
            A('act', lambda e, g=g, bY=bY, hh=hh: e.copy(out=g['sc'][:, 0:1], in_=ps[bY][:, hh:hh + 1]), [('ps', bY)],
              [K('Gc')])
            A('act', lambda e, g=g, bY=bY: e.copy(out=g['sc'][:, 1:2], in_=ps[bY][:, 255:256]), [('ps', bY)],
              [K('gl')])
            A('dve', lambda e, g=g, bY=bY: e.tensor_scalar(out=g['Dm'], in0=ps[bY][:, 128:256],
                                                           scalar1=g['sc'][:, 0:1], scalar2=-1.0,
                                                           op0=ALU.subtract, op1=ALU.mult),
              [('ps', bY), K('Gc')], [K('Dm')])
            A('dve', lambda e, g=g: e.tensor_tensor(out=g['Dm'], in0=g['Dm'], in1=maskI, op=ALU.add),
              [K('Dm'), 'tab'], [K('Dm')])
            A('act', lambda e, g=g: e.activation(out=g['gam'], in_=g['Dm'], func=AF.Exp), [K('Dm')], [K('gam')])
            A('dve', lambda e, g=g, bY=bY: e.tensor_scalar(out=g['DmT'], in0=ps[bY][:, 128:256],
                                                           scalar1=g['sc'][:, 0:1], scalar2=None,
                                                           op0=ALU.subtract), [('ps', bY), K('Gc')], [K('DmT')])
            A('dve', lambda e, g=g: e.tensor_tensor(out=g['DmT'], in0=g['DmT'], in1=maskIT, op=ALU.add),
              [K('DmT'), 'tab'], [K('DmT')])
            A('act', lambda e, g=g: e.activation(out=g['gamT'], in_=g['DmT'], func=AF.Exp), [K('DmT')], [K('gamT')])
            A('act', lambda e, g=g, bY=bY: e.activation(out=g['eGr'], in_=ps[bY][:, 128:256], func=AF.Exp),
              [('ps', bY)], [K('eGr')])
            A('act', lambda e, g=g: e.activation(out=g['sc'][:, 2:3], in_=g['sc'][:, 0:1], func=AF.Exp),
              [K('Gc')], [K('eG')])
            A('dve', lambda e, g=g, bcol=bcol: e.tensor_tensor(out=g['sc'][:, 3:4], in0=g['sc'][:, 2:3], in1=bcol,
                                                              op=ALU.mult), [K('eG'), K('gb')], [K('bE')])
            A('act', lambda e, g=g: e.activation(out=g['sc'][:, 4:5], in_=g['sc'][:, 0:1], func=AF.Exp, scale=-1.0,
                                                 bias=g['sc'][:, 1:2]), [K('Gc'), K('gl')], [K('kdsc')])
            A('act', lambda e, g=g: e.activation(out=g['sc'][:, 5:6], in_=g['sc'][:, 1:2], func=AF.Exp),
              [K('gl')], [K('egl')])
            yield
            A('pe', lambda e, g=g, bX=bX: e.matmul(out=ps[bX][:, 0:128], lhsT=g['kT'], rhs=g['kT'],
                                                   start=True, stop=True), [K('kT')], [('ps', bX)])
            A('pe', lambda e, g=g, bX=bX: e.matmul(out=ps[bX][:, 128:256], lhsT=g['kT'], rhs=g['qT'],
                                                   start=True, stop=True), [K('kT'), K('qT')], [('ps', bX)])
            A('dve', lambda e, g=g, bX=bX, bcol=bcol: e.scalar_tensor_tensor(
                out=g['Am'][0], in0=ps[bX][:, 0:128], scalar=bcol, in1=g['gam'], op0=ALU.mult, op1=ALU.mult),
              [('ps', bX), K('gb'), K('gam')], [K('Am0')])
            A('dve', lambda e, g=g: e.tensor_tensor(out=g['Am'][0], in0=g['Am'][0], in1=strict01, op=ALU.mult),
              [K('Am0'), 'tab'], [K('Am0')])
            A('dve', lambda e, g=g, bX=bX: e.tensor_tensor(out=g['attnT'], in0=ps[bX][:, 128:256], in1=g['gamT'],
                                                           op=ALU.mult), [('ps', bX), K('gamT')], [K('attnT')])
            yield
            A('pe', lambda e, g=g, bY=bY: e.transpose(out=ps[bY][:, 0:128], in_=g['Am'][0], identity=ident),
              [K('Am0'), 'tab'], [('ps', bY)])
            A('act', lambda e, g=g, bY=bY: e.copy(out=g['AmT'][0], in_=ps[bY][:, 0:128]), [('ps', bY)], [K('AmT0')])
            A('dve', lambda e, g=g, bY=bY: e.scalar_tensor_tensor(
                out=g['Q'][0], in0=ps[bY][:, 0:128], scalar=-1.0, in1=ident, op0=ALU.mult, op1=ALU.add),
              [('ps', bY), 'tab'], [K('Q0')])
            for lv in range(1, 7):
                yield
                pa, ca = (lv - 1) % 2, lv % 2
                Ap, ATp = g['Am'][pa], g['AmT'][pa]
                Ac, ATc = g['Am'][ca], g['AmT'][ca]
                Qp, Qc = g['Q'][pa], g['Q'][ca]
                kAp, kATp, kAc, kATc = K('Am%d' % pa), K('AmT%d' % pa), K('Am%d' % ca), K('AmT%d' % ca)
                kQp, kQc = K('Q%d' % pa), K('Q%d' % ca)
                A('pe', lambda e, bX=bX, Ap=Ap, ATp=ATp: e.matmul(out=ps[bX][:, 0:128], lhsT=ATp, rhs=Ap,
                                                                  start=True, stop=True), [kAp, kATp], [('ps', bX)])
                if lv < 6:
                    A('pe', lambda e, bX=bX, Ap=Ap, ATp=ATp: e.matmul(out=ps[bX][:, 128:256], lhsT=Ap, rhs=ATp,
                                                                      start=True, stop=True), [kAp, kATp],
                      [('ps', bX)])
                A('act', lambda e, bX=bX, Ac=Ac: e.copy(out=Ac, in_=ps[bX][:, 0:128]), [('ps', bX)], [kAc])
                if lv < 6:
                    A('dve', lambda e, bX=bX, ATc=ATc: e.tensor_copy(out=ATc, in_=ps[bX][:, 128:256]),
                      [('ps', bX)], [kATc])
                yield
                A('pe', lambda e, bY=bY, Ac=Ac, Qp=Qp: e.matmul(out=ps[bY][:, 0:128], lhsT=Ac, rhs=Qp,
                                                                start=True, stop=True), [kAc, kQp], [('ps', bY)])
                A('dve', lambda e, bY=bY, Qp=Qp, Qc=Qc: e.tensor_tensor(out=Qc, in0=Qp, in1=ps[bY][:, 0:128],
                                                                        op=ALU.add), [('ps', bY), kQp], [kQc])
            Qf, kQf = g['Q'][0], K('Q0')
            yield
            A('dve', lambda e, g=g, bcol=bcol: e.tensor_scalar(out=g['rhsM'][:, 0:128], in0=g['kv'][:, 128:256],
                                                              scalar1=bcol, scalar2=None, op0=ALU.mult),
              [K('kv'), K('gb')], [K('rhsv')])
            A('dve', lambda e, g=g: e.tensor_scalar(out=g['rhsM'][:, 128:256], in0=g['kv'][:, 0:128],
                                                    scalar1=g['sc'][:, 3:4], scalar2=None, op0=ALU.mult),
              [K('kv'), K('bE')], [K('rhsk')])
            A('pe', lambda e, g=g, bX=bX, Qf=Qf: e.matmul(out=ps[bX][:, 0:128], lhsT=Qf, rhs=g['rhsM'][:, 0:128],
                                                          start=True, stop=True), [kQf, K('rhsv')], [('ps', bX)])
            A('pe', lambda e, g=g, bX=bX, Qf=Qf: e.matmul(out=ps[bX][:, 128:256], lhsT=g['rhsM'][:, 128:256], rhs=Qf,
                                                          start=True, stop=True), [kQf, K('rhsk')], [('ps', bX)])
            A('act', lambda e, g=g, bX=bX: e.copy(out=g['u'], in_=ps[bX][:, 0:128]), [('ps', bX)], [K('u')])
            A('dve', lambda e, g=g, bX=bX: e.tensor_copy(out=g['wT'], in_=ps[bX][:, 128:256]), [('ps', bX)],
              [K('wT')])
            yield
            A('pe', lambda e, g=g, bY=bY: e.matmul(out=ps[bY][:, 0:128], lhsT=g['wT'], rhs=g['state'],
                                                   start=True, stop=True), [K('wT'), ('state', hh)], [('ps', bY)])
            A('dve', lambda e, g=g, bY=bY: e.tensor_tensor(out=g['vnew'], in0=g['u'], in1=ps[bY][:, 0:128],
                                                           op=ALU.subtract), [K('u'), ('ps', bY)], [K('vnew')])
            A('dve', lambda e, g=g: e.tensor_tensor(out=g['qd'], in0=g['qT'], in1=g['eGr'], op=ALU.mult),
              [K('qT'), K('eGr')], [K('qd')])
            A('dve', lambda e, g=g: e.tensor_scalar(out=g['kd'], in0=g['kv'][:, 0:128], scalar1=g['sc'][:, 4:5],
                                                    scalar2=None, op0=ALU.mult), [K('kv'), K('kdsc')], [K('kd')])
            yield
            A('pe', lambda e, g=g, bX=bX: e.matmul(out=ps[bX][:, 0:128], lhsT=g['qd'], rhs=g['state'],
                                                   start=True, stop=False), [K('qd'), ('state', hh)], [('ps', bX)])
            A('pe', lambda e, g=g, bX=bX: e.matmul(out=ps[bX][:, 0:128], lhsT=g['attnT'], rhs=g['vnew'],
                                                   start=False, stop=True), [K('attnT'), K('vnew')], [('ps', bX)])
            A('pe', lambda e, g=g, bY=bY: e.matmul(out=ps[bY][:, 128:256], lhsT=g['kd'], rhs=g['vnew'],
                                                   start=True, stop=True), [K('kd'), K('vnew')], [('ps', bY)])
            A('dve', lambda e, g=g, bY=bY: e.scalar_tensor_tensor(
                out=g['state'], in0=g['state'], scalar=g['sc'][:, 5:6], in1=ps[bY][:, 128:256],
                op0=ALU.mult, op1=ALU.add), [('state', hh), K('egl'), ('ps', bY)], [('state', hh)])
            yield
            A('act', lambda e, g=g, bX=bX: e.activation(out=g['jk'], in_=ps[bX][:, 0:128], func=AF.Square,
                                                        accum_out=g['sc'][:, 6:7]), [('ps', bX)],
              [K('ss'), K('jk')])
            A('act', lambda e, g=g: e.activation(out=g['sc'][:, 7:8], in_=g['sc'][:, 6:7], func=AF.Sqrt,
                                                 scale=1.0 / 128, bias=epsc[:, 0:1]), [K('ss'), 'epsc'], [K('sd')])
            A('dve', lambda e, g=g: e.reciprocal(out=g['sc'][:, 8:9], in_=g['sc'][:, 7:8]), [K('sd')], [K('rs')])
            A('dve', lambda e, g=g, bX=bX: e.scalar_tensor_tensor(
                out=g['yt'], in0=ps[bX][:, 0:128], scalar=g['sc'][:, 8:9], in1=gnw, op0=ALU.mult, op1=ALU.mult),
              [('ps', bX), K('rs'), 'hvb'], [K('yt')])
            A('dve', lambda e, g=g: e.tensor_tensor(out=g['yt'], in0=g['yt'], in1=g['zt'], op=ALU.mult),
              [K('yt'), K('zt')], [K('yt')])
            P.dma('pool', y[t0:t0 + 128, ycols[0] + hh * 128: ycols[0] + (hh + 1) * 128], g['yt'], reads=[K('yt')],
                  writes=[(pfx + 'y', 0, n, hh)], group='y')
        for n in range(NT):
            alive = [gdn_body(n, hh) for hh in range(H)]
            while alive:
                for g_ in list(alive):
                    try:
                        next(g_)
                    except StopIteration:
                        alive.remove(g_)
        E.reset(mark)

    if 3 in phases:
        knT = E.bf16([128, S]); kpe = E.bf16([128, S]); Vb = E.bf16([128, NT, 128])
        Srow = [E.f32([128, S]) for _ in range(2)]
        Pb = [E.bf16([128, S]) for _ in range(2)]
        qn = [E.bf16([128, 128]) for _ in range(2)]
        qp = [E.bf16([128, 128]) for _ in range(2)]
        PT = [E.bf16([128, 512]) for _ in range(4)]
        identb = E.bf16([128, 128])
        yo = [E.f32([128, 128]) for _ in range(2)]
        sm = [E.f32([128, 4]) for _ in range(2)]
        tbanks = (2, 3, 6, 7)
        psb = {t: ps[t].bitcast(BF16) for t in tbanks}
        A('dve', lambda e: e.tensor_copy(out=identb, in_=ident), ['tab'], ['identb'])
        P.dma('pool', kpe[0:64, :], kpes, reads=[('kpes', b) for b in range(NB)], writes=['kpe'])
        qi = 0
        gi = 0
        for hh in range(H):
            P.dma('pool', knT, knTs[hh], reads=[('knTs', hh, b) for b in range(NB)], writes=['knT'])
            P.dma('pool', Vb, mvs[:, hh * 128:(hh + 1) * 128].rearrange("(n p) e -> p n e", p=128),
                  reads=[('mvs', n) for n in range(NT)], writes=['Vb'])
            for i in range(NT):
                t0 = i * 128
                L = t0 + 128
                blk = t0 // T
                b = qi % 2
                qi += 1
                Sr = Srow[b]
                Pr = Pb[b]
                P.dma('pool', qn[b], qnTs[hh, :, t0:t0 + 128], reads=[('qnTs', hh, blk)], writes=[('qn', b)])
                P.dma('pool', qp[b][0:64, :], qpTs[hh, :, t0:t0 + 128], reads=[('qpTs', hh, blk)], writes=[('qp', b)])
                nkb = (L + 511) // 512
                for kb in range(nkb):
                    w = min(512, L - kb * 512)
                    bank = kb % 2
                    A('pe', lambda e, b=b, kb=kb, w=w, bank=bank: e.matmul(
                        out=ps[bank][:, 0:w], lhsT=qn[b], rhs=knT[:, kb * 512: kb * 512 + w], start=True, stop=False),
                      [('qn', b), 'knT'], [('ps', bank)])
                    A('pe', lambda e, b=b, kb=kb, w=w, bank=bank: e.matmul(
                        out=ps[bank][:, 0:w], lhsT=qp[b][0:64, :], rhs=kpe[0:64, kb * 512: kb * 512 + w],
                        start=False, stop=True), [('qp', b), 'kpe'], [('ps', bank)])
                    A('act', lambda e, Sr=Sr, kb=kb, w=w, bank=bank: e.copy(out=Sr[:, kb * 512: kb * 512 + w],
                                                                           in_=ps[bank][:, 0:w]),
                      [('ps', bank)], [('Sr', b)])
                A('dve', lambda e, Sr=Sr, L=L: e.tensor_tensor(out=Sr[:, L - 128:L], in0=Sr[:, L - 128:L], in1=maskI,
                                                               op=ALU.add), [('Sr', b), 'tab'], [('Sr', b)])
                A('dve', lambda e, Sr=Sr, L=L, b=b: e.reduce_max(out=sm[b][:, 0:1], in_=Sr[:, 0:L], axis=AX.X),
                  [('Sr', b)], [('mx', b)])
                A('dve', lambda e, b=b: e.tensor_scalar(out=sm[b][:, 1:2], in0=sm[b][:, 0:1], scalar1=-1.0, scalar2=None,
                                                        op0=ALU.mult), [('mx', b)], [('nmx', b)])
                A('act', lambda e, Sr=Sr, Pr=Pr, L=L, b=b: e.activation(out=Pr[:, 0:L], in_=Sr[:, 0:L], func=AF.Exp,
                                                                        bias=sm[b][:, 1:2], accum_out=sm[b][:, 2:3]),
                  [('Sr', b), ('nmx', b)], [('Pb', b), ('rsum', b)])
                ob = 4 + b
                ng = (i + 4) // 4
                grp = []
                for g in range(ng):
                    slot = gi % 4
                    gi += 1
                    grp.append((g, slot, tbanks[slot], list(range(4 * g, min(4 * g + 4, i + 1)))))

                def emit_T(g, slot, tb, js):
                    for jj, j in enumerate(js):
                        A('pe', lambda e, Pr=Pr, j=j, jj=jj, tb=tb: e.transpose(
                            out=psb[tb][:, jj * 128:(jj + 1) * 128], in_=Pr[:, j * 128:(j + 1) * 128], identity=identb),
                          [('Pb', b), 'identb'], [('ps', tb)])
                    w = len(js) * 128
                    if tb in (2, 6):
                        A('act', lambda e, slot=slot, tb=tb, w=w: e.copy(out=PT[slot][:, 0:w], in_=psb[tb][:, 0:w]),
                          [('ps', tb)], [('PT', slot)])
                    else:
                        A('dve', lambda e, slot=slot, tb=tb, w=w: e.tensor_copy(out=PT[slot][:, 0:w], in_=psb[tb][:, 0:w]),
                          [('ps', tb)], [('PT', slot)])

                def emit_M(g, slot, tb, js):
                    for jj, j in enumerate(js):
                        A('pe', lambda e, slot=slot, j=j, jj=jj, ob=ob, i=i: e.matmul(
                            out=ps[ob][:, 0:128], lhsT=PT[slot][:, jj * 128:(jj + 1) * 128], rhs=Vb[:, j, :],
                            start=(j == 0), stop=(j == i)), [('PT', slot), 'Vb'], [('ps', ob)])
                emit_T(*grp[0])
                for g in range(1, ng):
                    emit_T(*grp[g])
                    emit_M(*grp[g - 1])
                emit_M(*grp[ng - 1])
                A('dve', lambda e, b=b: e.reciprocal(out=sm[b][:, 3:4], in_=sm[b][:, 2:3]), [('rsum', b)], [('rinv', b)])
                A('dve', lambda e, b=b, ob=ob: e.tensor_scalar(out=yo[b], in0=ps[ob][:, 0:128], scalar1=sm[b][:, 3:4],
                                                               scalar2=None, op0=ALU.mult),
                  [('ps', ob), ('rinv', b)], [('yo', b)])
                P.dma('sp', y[t0:t0 + 128, ycols[1] + hh * 128: ycols[1] + (hh + 1) * 128], yo[b], reads=[('yo', b)],
                      writes=[(pfx + 'y', 1, i, hh)], group='y')
    ykeys = []
    if 2 in phases:
        ykeys += [(pfx + 'y', 0, n, hh) for n in range(NT) for hh in range(H)]
    if 3 in phases:
        ykeys += [(pfx + 'y', 1, n, hh) for n in range(NT) for hh in range(H)]
    if not own:
        return ykeys
    P.fence('sp', ykeys)
    P.emit()
    return nc


def l0_tabs():
    i = np.arange(128)
    ident = np.eye(128)
    ones = np.ones((128, 128))
    U = (i[:, None] <= i[None, :]).astype(np.float64)
    maskI = np.where(i[:, None] >= i[None, :], 0.0, NEG)
    maskIT = np.where(i[None, :] >= i[:, None], 0.0, NEG)
    strict = (i[:, None] > i[None, :]).astype(np.float64)
    Rm = np.zeros((128, 128))
    for m in range(32):
        Rm[m + 32, m] = -1.0
        Rm[m, m + 32] = 1.0
    return np.ascontiguousarray(np.stack([ident, ones, U, maskI, maskIT, strict, Rm]).astype(np.float32))


def l0_inputs(xb, posb, attn_norm, w_in, gdn_conv, A_log, dt_bias, gdn_norm, q_norm, w_uq, kv_norm, w_ukv, hg):
    D = xb.shape[1]
    KC = D // 128
    H = 4
    hs_ = [4 * hg + i for i in range(H)]
    ck = lambda W, c0, n, kc: np.pad(W[:, c0:c0 + n], ((0, 0), (0, 128 - n))).reshape(kc, 128, 128).transpose(1, 0, 2)
    chunks = [ck(w_in, qkv * 1024 + h * 128, 128, KC) for qkv in range(3) for h in hs_]
    chunks.append(ck(w_in, 5136, 64, KC))
    wfm = np.ascontiguousarray(np.stack(chunks))
    tm = lambda W, c0, kc: W[:, c0:c0 + 512].reshape(kc, 128, 512).transpose(1, 0, 2)
    wtm = np.ascontiguousarray(np.stack([tm(w_in, 3072 + 4 * hg * 128, KC), tm(w_in, 4112, KC), tm(w_in, 4624, KC)]))
    abcols = [4096 + h for h in hs_] + [4104 + h for h in hs_]
    wab = np.ascontiguousarray(w_in[:, abcols].reshape(KC, 128, 8).transpose(1, 0, 2))
    c2 = []
    for h in hs_:
        c2.append(ck(w_uq, h * 192, 128, 4))
        c2.append(ck(w_uq, h * 192 + 128, 64, 4))
        c2.append(ck(w_ukv, h * 256, 128, 4))
    w2fm = np.ascontiguousarray(np.stack(c2))
    vcols = np.concatenate([np.arange(h * 256 + 128, h * 256 + 256) for h in hs_])
    w2tm = np.ascontiguousarray(w_ukv[:, vcols].reshape(4, 128, 512).transpose(1, 0, 2))
    taps = np.zeros((128, 12, 4), np.float32)
    for qkv in range(3):
        for i, h in enumerate(hs_):
            ch0 = qkv * 1024 + h * 128
            taps[:, qkv * 4 + i, :] = gdn_conv[:, ch0:ch0 + 128].T
    invf32 = (10000.0 ** (-np.arange(0, 64, 2, dtype=np.float32) / 64)).astype(np.float32)
    invf = np.zeros((128, 1), np.float32)
    invf[:, 0] = np.tile(invf32, 4)
    consts = np.concatenate([arr_vec(attn_norm), arr_vec(q_norm), arr_vec(kv_norm), taps.reshape(128, 48), invf],
                            axis=1).astype(np.float32)
    hv = np.zeros((3, 128), np.float32)
    hv[0, 0:4] = dt_bias[hs_]
    hv[1, 0:4] = A_log[hs_]
    hv[2, :] = gdn_norm
    return dict(x=np.ascontiguousarray(xb), pos=np.ascontiguousarray(posb.astype(np.int32)),
                consts=np.ascontiguousarray(consts), tabs=l0_tabs(), wfm=wfm, wtm=wtm, wab=wab, w2fm=w2fm, w2tm=w2tm,
                hv=hv)


def emit_select(E, flag, h1s, y1s, hsel, ysel, S, D, KM):
    P = E.P
    fl = E.f32([128, 2])
    P.dma('sp', fl, flag, writes=['selflag'])
    half = S // 2
    lo = [E.f32([128, 2048]) for _ in range(2)]
    hi = [E.f32([128, 2048]) for _ in range(2)]
    P.slots['selst'] = 4
    cnt = 0
    for (src, dst, W) in ((h1s, hsel, D), (y1s, ysel, KM)):
        for q in range((W + 2047) // 2048):
            w = min(2048, W - q * 2048)
            cs = slice(q * 2048, q * 2048 + w)
            b = cnt % 2
            cnt += 1
            P.dma('sp', lo[b][:, 0:w], src[half - 128:half, cs], writes=[('sello', b)])
            P.dma('pool', dst[0:128, cs], lo[b][:, 0:w], reads=[('sello', b)], writes=[('selst', cnt)], group='selst')
            for i in range(half // 128):
                b = cnt % 2
                cnt += 1
                P.dma('sp', lo[b][:, 0:w], src[i * 128:(i + 1) * 128, cs], writes=[('sello', b)])
                P.dma('sp', hi[b][:, 0:w], src[half + i * 128: half + (i + 1) * 128, cs], writes=[('selhi', b)])
                P.add('dve', lambda e, b=b, w=w: e.tensor_scalar(out=lo[b][:, 0:w], in0=lo[b][:, 0:w], scalar1=fl[:, 1:2],
                                                                scalar2=None, op0=ALU.mult),
                      reads=[('sello', b), 'selflag'], writes=[('sello', b)])
                P.add('dve', lambda e, b=b, w=w: e.scalar_tensor_tensor(
                    out=lo[b][:, 0:w], in0=hi[b][:, 0:w], scalar=fl[:, 0:1], in1=lo[b][:, 0:w], op0=ALU.mult,
                    op1=ALU.add), reads=[('sello', b), ('selhi', b), 'selflag'], writes=[('sello', b)])
                P.dma('pool', dst[128 + i * 128: 256 + i * 128, cs], lo[b][:, 0:w], reads=[('sello', b)],
                      writes=[('selst', cnt)], group='selst')


def build_fused(D, S, FF, PD, split_last=True):
    nc = bass.Bass("TRN2", target_bir_lowering=False)
    E = Env(nc)
    dr = lambda n, s, dt=F32, kind="Internal": nc.dram_tensor(n, list(s), dt, kind=kind).ap()
    x = dr("x", [S, D], kind="ExternalInput")
    NTD = S // 2 if split_last else S
    out = dr("out", [NTD, D], kind="ExternalOutput")
    y0s = dr("y0s", [S, 2048])
    h1s = dr("h1s", [S, D])
    y1s = dr("y1s", [S, 4096])
    for hg in range(2):
        build_l0(D, S, E=E, io=dict(x=x, y=y0s), pfx="a%d_" % hg, ycols=(hg * 512, 1024 + hg * 512))
        E.reset()
    build_ffn(D, FF, 2048, PD, S, False, E=E, io=dict(hin=x, ysrc=y0s, out=h1s), halo=False, pfx="b_")
    E.reset()
    for hg in range(2):
        build_ret(D, S, E=E, io=dict(x=h1s, y=y1s), pfx="c%d_" % hg, ycol0=hg * 2048)
        E.reset()
    if split_last:
        flag = dr("flag", [128, 2], kind="ExternalInput")
        hsel = dr("hsel", [NTD + 128, D])
        ysel = dr("ysel", [NTD + 128, 4096])
        emit_select(E, flag, h1s, y1s, hsel, ysel, S, D, 4096)
        E.reset()
        fl2 = E.f32([128, 2])
        E.P.dma('sp', fl2, flag, writes=['cscale'])
        okeys = build_ffn(D, FF, 4096, PD, NTD, True, E=E, io=dict(hin=hsel, ysrc=ysel, out=out), halo=True, pfx="d_",
                          cscale=fl2)
    else:
        okeys = build_ffn(D, FF, 4096, PD, S, True, E=E, io=dict(hin=h1s, ysrc=y1s, out=out), halo=False, pfx="d_")
    E.P.fence('sp', okeys)
    E.P.emit()
    return nc


def _pref(d, pfx, drop=()):
    return {pfx + k: v for k, v in d.items() if k not in drop}


def fused_weights(W):
    m = {}
    dummy_x = np.zeros((1, W['l0_w_in'].shape[0]), np.float32)
    dummy_pos = np.zeros((1,), np.int32)
    for hg in range(2):
        d = l0_inputs(dummy_x, dummy_pos, W['l0_attn_norm'], W['l0_w_in'], W['l0_gdn_conv'], W['l0_gdn_A_log'],
                      W['l0_gdn_dt_bias'], W['l0_gdn_norm'], W['l0_mla_q_norm'], W['l0_mla_w_uq'],
                      W['l0_mla_kv_norm'], W['l0_mla_w_ukv'], hg)
        m.update(_pref(d, "a%d_" % hg, drop=('x', 'pos')))
        d = ret_inputs(dummy_x, dummy_pos, W['l1_attn_norm'], W['l1_w_in'], W['l1_ret_norm'], hg)
        m.update(_pref(d, "c%d_" % hg, drop=('x', 'pos')))
    layers = (
        ("b_", W['l0_w_out'], W['l0_ffn_norm'], W['l0_ffn_w_up'], W['l0_ffn_conv_w'], W['l0_ffn_conv_b'],
         W['l0_ffn_w_down'], W['l0_ple_proj'], W['l0_ple_gate_norm'], W['l0_ple_gate']),
        ("d_", W['l1_w_out'], W['l1_ffn_norm'], W['l1_ffn_w_up'], W['l1_ffn_conv_w'], W['l1_ffn_conv_b'],
         W['l1_ffn_w_down'], W['l1_ple_proj'], W['l1_ple_gate_norm'], W['l1_ple_gate']),
    )
    for pfx, w_out, ffn_norm, w_up, conv_w, conv_b, w_down, ple_proj, gate_norm, ple_gate in layers:
        consts = np.concatenate([arr_vec(ffn_norm), arr_vec(gate_norm)] + [arr_vec(conv_w[j]) for j in range(3)]
                                + [arr_vec(conv_b)], axis=1).astype(np.float32)
        d = dict(consts=np.ascontiguousarray(consts), idn=np.eye(128, dtype=np.float32),
                 fng=np.ascontiguousarray(W['final_norm'].astype(np.float32)))
        d.update(ffn_weights(w_out, w_up, w_down, ple_proj, ple_gate))
        m.update(_pref(d, pfx))
    return m


def fused_acts(xb, pb, posb, half=None):
    m = dict(x=np.ascontiguousarray(xb))
    pos = np.ascontiguousarray(posb.astype(np.int32))
    for pfx in ("a0_", "a1_", "c0_", "c1_"):
        m[pfx + "pos"] = pos
    S = xb.shape[0]
    for i, pfx in enumerate(("b_", "d_")):
        p = pb[i]
        if half is not None and i == 1:
            p = p[half * (S // 2):(half + 1) * (S // 2)]
        m[pfx + "pT"] = np.ascontiguousarray(p.T.reshape(p.shape[1] // 128, 128, -1).transpose(1, 0, 2))
    if half is not None:
        fl = np.zeros((128, 2), np.float32)
        fl[:, 0] = float(half)
        fl[:, 1] = 1.0 - float(half)
        m["flag"] = fl
    return m

_NC_CACHE = {}


def kernel(x, p, positions,
           l0_attn_norm, l0_w_in, l0_gdn_conv, l0_gdn_A_log, l0_gdn_dt_bias, l0_gdn_norm,
           l0_mla_q_norm, l0_mla_w_uq, l0_mla_kv_norm, l0_mla_w_ukv, l0_w_out,
           l0_ffn_norm, l0_ffn_w_up, l0_ffn_conv_w, l0_ffn_conv_b, l0_ffn_w_down,
           l0_ple_proj, l0_ple_gate_norm, l0_ple_gate,
           l1_attn_norm, l1_w_in, l1_ret_norm, l1_w_out,
           l1_ffn_norm, l1_ffn_w_up, l1_ffn_conv_w, l1_ffn_conv_b, l1_ffn_w_down,
           l1_ple_proj, l1_ple_gate_norm, l1_ple_gate,
           final_norm):
    inputs = dict(
        x=x, p=p, positions=positions,
        l0_attn_norm=l0_attn_norm, l0_w_in=l0_w_in, l0_gdn_conv=l0_gdn_conv, l0_gdn_A_log=l0_gdn_A_log,
        l0_gdn_dt_bias=l0_gdn_dt_bias, l0_gdn_norm=l0_gdn_norm, l0_mla_q_norm=l0_mla_q_norm,
        l0_mla_w_uq=l0_mla_w_uq, l0_mla_kv_norm=l0_mla_kv_norm, l0_mla_w_ukv=l0_mla_w_ukv, l0_w_out=l0_w_out,
        l0_ffn_norm=l0_ffn_norm, l0_ffn_w_up=l0_ffn_w_up, l0_ffn_conv_w=l0_ffn_conv_w, l0_ffn_conv_b=l0_ffn_conv_b,
        l0_ffn_w_down=l0_ffn_w_down, l0_ple_proj=l0_ple_proj, l0_ple_gate_norm=l0_ple_gate_norm,
        l0_ple_gate=l0_ple_gate, l1_attn_norm=l1_attn_norm, l1_w_in=l1_w_in, l1_ret_norm=l1_ret_norm,
        l1_w_out=l1_w_out, l1_ffn_norm=l1_ffn_norm, l1_ffn_w_up=l1_ffn_w_up, l1_ffn_conv_w=l1_ffn_conv_w,
        l1_ffn_conv_b=l1_ffn_conv_b, l1_ffn_w_down=l1_ffn_w_down, l1_ple_proj=l1_ple_proj,
        l1_ple_gate_norm=l1_ple_gate_norm, l1_ple_gate=l1_ple_gate, final_norm=final_norm)
    W = {}
    for k, v in inputs.items():
        a = np.asarray(v)
        W[k] = a.astype(np.int32) if k == 'positions' else a.astype(np.float32)
    x, p, positions = W['x'], W['p'], W['positions']
    B, S, D = x.shape
    FF = W['l0_ffn_w_down'].shape[0]
    PD = p.shape[3]
    key = (D, S, FF, PD)
    if key not in _NC_CACHE:
        _NC_CACHE[key] = build_fused(D, S, FF, PD, split_last=True)
    nc = _NC_CACHE[key]
    wts = fused_weights(W)
    in_maps = []
    for c in range(2 * B):
        b, half = c // 2, c % 2
        m = dict(wts)
        m.update(fused_acts(x[b], p[:, b], positions[b], half=half))
        in_maps.append(m)
    res = run_bass_kernel_spmd(nc, in_maps, core_ids=list(range(2 * B)))
    out = np.empty((B, S, D), np.float32)
    for c in range(2 * B):
        b, half = c // 2, c % 2
        out[b, half * (S // 2):(half + 1) * (S // 2)] = np.asarray(res.results[c]["out"], dtype=np.float32)
    return out
```
